# Optimizing a Trainium2 kernel written in Bass

```python
import jax, jax.numpy as jnp
from jax import lax
import numpy as np

D_MODEL = 4096
BATCH = 4
SEQ = 2048
DEPTH = 1
DEC_BATCH = 128
DEC_SEQ = 8
PAST_LEN = 16384
PAGE_SIZE = 128

EPS = 1e-6
A_WIDTH = D_MODEL // 2
A_CONV = 3
SSD_INNER = D_MODEL
SSD_HEAD_DIM = 64
SSD_HEADS = SSD_INNER // SSD_HEAD_DIM
SSD_GROUPS = 8
SSD_STATE = 128
SSD_CONV = 4
SSD_CHUNK = 128
SSD_XBC = SSD_INNER + 2 * SSD_GROUPS * SSD_STATE
N_MEM = 256
XATT_HEADS = 4
XATT_HEAD_DIM = D_MODEL // 8
XATT_WIDTH = XATT_HEADS * XATT_HEAD_DIM
N_BRANCH = 3
PEER_HEADS = 8
PEER_KEYS = 128
PEER_TOPK = 16
PEER_QDIM = 256
PEER_HALF = PEER_QDIM // 2
N_EXPERTS = PEER_KEYS * PEER_KEYS
PEER_TOKEN_BLOCK = 64
IN_SPLITS = (A_WIDTH, A_WIDTH, A_WIDTH, SSD_INNER, SSD_XBC, SSD_HEADS, XATT_WIDTH, N_BRANCH * D_MODEL)
IN_WIDTH = sum(IN_SPLITS)

kernel_name = 'hybrid_conv_ssd_xattn_peer_step'


def rmsnorm(x, g):
    x32 = x.astype(jnp.float32)
    inv = lax.rsqrt(jnp.mean(x32 * x32, axis=-1, keepdims=True) + EPS)
    return (x32 * inv).astype(x.dtype) * g


def causal_dwconv(u, prev, w):
    width = w.shape[0]
    length = u.shape[1]
    full = jnp.concatenate([prev.astype(u.dtype), u], axis=1)
    out = sum(full[:, k:k + length] * w[k] for k in range(width))
    return out, full[:, length:]


def ssd_chunked(x, dt, a, bmat, cmat, h0, chunk):
    b_, length = x.shape[:2]
    c = length // chunk
    r = SSD_HEADS // SSD_GROUPS
    xs = (x * dt[..., None]).reshape(b_, c, chunk, SSD_GROUPS, r, SSD_HEAD_DIM)
    adt = (dt * a).reshape(b_, c, chunk, SSD_GROUPS, r)
    bm = bmat.reshape(b_, c, chunk, SSD_GROUPS, SSD_STATE)
    cm = cmat.reshape(b_, c, chunk, SSD_GROUPS, SSD_STATE)
    acum = jnp.cumsum(adt, axis=2)
    causal = jnp.tril(jnp.ones((chunk, chunk), dtype=bool))[:, :, None, None]
    seg = acum[:, :, :, None] - acum[:, :, None, :]
    decay = jnp.exp(jnp.where(causal, seg, -jnp.inf))
    cb = jnp.einsum('bcign,bcjgn->bcijg', cm, bm)
    y_diag = jnp.einsum('bcijgr,bcjgrp->bcigrp', cb[..., None] * decay, xs)
    decay_states = jnp.exp(acum[:, :, -1:] - acum)
    chunk_states = jnp.einsum('bcjgn,bcjgr,bcjgrp->bcgrpn', bm, decay_states, xs)
    chunk_decay = jnp.exp(acum[:, :, -1])

    def step(h, inp):
        s, d = inp
        return h * d[..., None, None] + s, h

    h_last, h_prev = lax.scan(
        step, h0.reshape(b_, SSD_GROUPS, r, SSD_HEAD_DIM, SSD_STATE),
        (jnp.moveaxis(chunk_states, 1, 0), jnp.moveaxis(chunk_decay, 1, 0)))
    h_prev = jnp.moveaxis(h_prev, 0, 1)
    y_off = jnp.einsum('bcign,bcgrpn,bcigr->bcigrp', cm, h_prev, jnp.exp(acum))
    y = (y_diag + y_off).reshape(b_, length, SSD_HEADS, SSD_HEAD_DIM)
    return y, h_last.reshape(b_, SSD_HEADS, SSD_HEAD_DIM, SSD_STATE)


def memory_kv(mem, norm_mem, w_mem_k, w_mem_v):
    b_, m = mem.shape[:2]
    mn = rmsnorm(mem, norm_mem)
    k = (mn @ w_mem_k).reshape(b_, m, XATT_HEADS, XATT_HEAD_DIM)
    v = (mn @ w_mem_v).reshape(b_, m, XATT_HEADS, XATT_HEAD_DIM)
    return k, v


def hybrid_mixer(xn, mem_k, mem_v, conv_a_prev, ssd_conv_prev, ssd_h_prev, chunk,
                 w_in, a_conv_w, a_out, ssd_conv_w, ssd_conv_b, ssd_dt_bias, ssd_a_log,
                 ssd_d, ssd_norm, ssd_out, xatt_out, w_o):
    f32 = jnp.float32
    b_, length, _ = xn.shape
    proj = xn @ w_in
    cuts = [int(i) for i in np.cumsum(IN_SPLITS)[:-1]]
    a_in, a_bg, a_cg, z, xbc, dt_raw, q, gates = jnp.split(proj, cuts, axis=-1)

    conv_a_out, conv_a_new = causal_dwconv(a_cg * a_in, conv_a_prev, a_conv_w)
    h_a = (a_bg * conv_a_out) @ a_out

    xbc_conv, ssd_conv_new = causal_dwconv(xbc, ssd_conv_prev, ssd_conv_w)
    xbc_conv = jax.nn.silu(xbc_conv + ssd_conv_b)
    xs, bm, cm = jnp.split(xbc_conv, [SSD_INNER, SSD_INNER + SSD_GROUPS * SSD_STATE], axis=-1)
    xs = xs.reshape(b_, length, SSD_HEADS, SSD_HEAD_DIM).astype(f32)
    dt = jax.nn.softplus(dt_raw.astype(f32) + ssd_dt_bias.astype(f32))
    a = -jnp.exp(ssd_a_log.astype(f32))
    y, ssd_h_new = ssd_chunked(
        xs, dt, a,
        bm.reshape(b_, length, SSD_GROUPS, SSD_STATE).astype(f32),
        cm.reshape(b_, length, SSD_GROUPS, SSD_STATE).astype(f32),
        ssd_h_prev.astype(f32), chunk)
    y = y + ssd_d.astype(f32)[:, None] * xs
    y = y.reshape(b_, length, SSD_INNER).astype(xn.dtype) * jax.nn.silu(z)
    y = rmsnorm(y.reshape(b_, length, SSD_GROUPS, SSD_INNER // SSD_GROUPS),
                ssd_norm.reshape(SSD_GROUPS, SSD_INNER // SSD_GROUPS)).reshape(b_, length, SSD_INNER)
    h_b = y @ ssd_out

    qh = q.reshape(b_, length, XATT_HEADS, XATT_HEAD_DIM)
    s = jnp.einsum('blhd,bmhd->bhlm', qh, mem_k).astype(f32) * (XATT_HEAD_DIM ** -0.5)
    p = jax.nn.softmax(s, axis=-1).astype(xn.dtype)
    o = jnp.einsum('bhlm,bmhd->blhd', p, mem_v).reshape(b_, length, XATT_WIDTH)
    h_c = o @ xatt_out

    g_a, g_b, g_c = jnp.split(jax.nn.sigmoid(gates), N_BRANCH, axis=-1)
    out = (g_a * h_a + g_b * h_b + g_c * h_c) @ w_o
    return out, conv_a_new, ssd_conv_new, ssd_h_new


def peer_ffn(xn, peer_wq, peer_subkeys, peer_u, peer_v):
    shp = xn.shape
    x2 = xn.reshape(-1, D_MODEL)
    t = x2.shape[0]
    q = (x2 @ peer_wq).reshape(t, PEER_HEADS, 2, PEER_HALF)
    s = jnp.einsum('thpd,hpkd->thpk', q, peer_subkeys).astype(jnp.float32)
    s1, i1 = lax.top_k(s[:, :, 0], PEER_TOPK)
    s2, i2 = lax.top_k(s[:, :, 1], PEER_TOPK)
    cand = (s1[..., :, None] + s2[..., None, :]).reshape(t, PEER_HEADS, PEER_TOPK * PEER_TOPK)
    cidx = (i1[..., :, None] * PEER_KEYS + i2[..., None, :]).reshape(t, PEER_HEADS, PEER_TOPK * PEER_TOPK)
    top, pos = lax.top_k(cand, PEER_TOPK)
    eidx = jnp.take_along_axis(cidx, pos, axis=-1).reshape(t, PEER_HEADS * PEER_TOPK)
    gate = jax.nn.softmax(top, axis=-1).reshape(t, PEER_HEADS * PEER_TOPK).astype(xn.dtype)
    nb = -(-t // PEER_TOKEN_BLOCK)
    pad = nb * PEER_TOKEN_BLOCK - t
    xb = jnp.pad(x2, ((0, pad), (0, 0))).reshape(nb, PEER_TOKEN_BLOCK, D_MODEL)
    ib = jnp.pad(eidx, ((0, pad), (0, 0))).reshape(nb, PEER_TOKEN_BLOCK, PEER_HEADS * PEER_TOPK)
    gb = jnp.pad(gate, ((0, pad), (0, 0))).reshape(nb, PEER_TOKEN_BLOCK, PEER_HEADS * PEER_TOPK)

    def block(args):
        xt, it, gt = args
        u = jnp.take(peer_u, it, axis=0)
        hid = jax.nn.gelu(jnp.einsum('td,tkd->tk', xt, u), approximate=False)
        v = jnp.take(peer_v, it, axis=0)
        return jnp.einsum('tk,tkd->td', hid * gt, v)

    out = lax.map(block, (xb, ib, gb)).reshape(nb * PEER_TOKEN_BLOCK, D_MODEL)[:t]
    return out.reshape(shp)


def setup_inputs(seed: int = 0) -> dict:
    key = jax.random.key(seed)
    ks = jax.random.split(key, 32)
    f32 = jnp.float32
    nrm = lambda k, shape, scale: jax.random.normal(k, shape, f32) * scale
    dt0 = jnp.exp(jax.random.uniform(ks[13], (DEPTH, SSD_HEADS), f32, np.log(1e-3), np.log(1e-1)))
    return {
        'x_prompt': nrm(ks[0], (BATCH, SEQ, D_MODEL), 1.0),
        'x_sample': nrm(ks[1], (DEC_BATCH, DEC_SEQ, D_MODEL), 1.0),
        'mem_prompt': nrm(ks[2], (BATCH, N_MEM, D_MODEL), 1.0),
        'cache_mem_k': nrm(ks[3], (DEPTH, DEC_BATCH, N_MEM, XATT_HEADS, XATT_HEAD_DIM), 1.0),
        'cache_mem_v': nrm(ks[4], (DEPTH, DEC_BATCH, N_MEM, XATT_HEADS, XATT_HEAD_DIM), 1.0),
        'state_conv_a': nrm(ks[5], (DEPTH, DEC_BATCH, A_CONV - 1, A_WIDTH), 1.0),
        'state_ssd_conv': nrm(ks[6], (DEPTH, DEC_BATCH, SSD_CONV - 1, SSD_XBC), 1.0),
        'state_ssd': nrm(ks[7], (DEPTH, DEC_BATCH, SSD_HEADS, SSD_HEAD_DIM, SSD_STATE), 0.5),
        'norm_mix': 1.0 + nrm(ks[8], (DEPTH, D_MODEL), 0.02),
        'norm_mem': 1.0 + nrm(ks[9], (DEPTH, D_MODEL), 0.02),
        'norm_ffn': 1.0 + nrm(ks[10], (DEPTH, D_MODEL), 0.02),
        'norm_final': 1.0 + nrm(ks[11], (D_MODEL,), 0.02),
        'w_in': nrm(ks[12], (DEPTH, D_MODEL, IN_WIDTH), D_MODEL ** -0.5),
        'a_conv_w': nrm(ks[14], (DEPTH, A_CONV, A_WIDTH), A_CONV ** -0.5),
        'a_out': nrm(ks[15], (DEPTH, A_WIDTH, D_MODEL), A_WIDTH ** -0.5),
        'ssd_conv_w': nrm(ks[16], (DEPTH, SSD_CONV, SSD_XBC), SSD_CONV ** -0.5),
        'ssd_conv_b': nrm(ks[17], (DEPTH, SSD_XBC), 0.02),
        'ssd_dt_bias': dt0 + jnp.log(-jnp.expm1(-dt0)),
        'ssd_a_log': jnp.log(jax.random.uniform(ks[18], (DEPTH, SSD_HEADS), f32, 1.0, 16.0)),
        'ssd_d': 1.0 + nrm(ks[19], (DEPTH, SSD_HEADS), 0.02),
        'ssd_norm': 1.0 + nrm(ks[20], (DEPTH, SSD_INNER), 0.02),
        'ssd_out': nrm(ks[21], (DEPTH, SSD_INNER, D_MODEL), SSD_INNER ** -0.5),
        'w_mem_k': nrm(ks[22], (DEPTH, D_MODEL, XATT_WIDTH), D_MODEL ** -0.5),
        'w_mem_v': nrm(ks[23], (DEPTH, D_MODEL, XATT_WIDTH), D_MODEL ** -0.5),
        'xatt_out': nrm(ks[24], (DEPTH, XATT_WIDTH, D_MODEL), XATT_WIDTH ** -0.5),
        'w_o': nrm(ks[25], (DEPTH, D_MODEL, D_MODEL), D_MODEL ** -0.5),
        'peer_wq': nrm(ks[26], (DEPTH, D_MODEL, PEER_HEADS * PEER_QDIM), D_MODEL ** -0.5),
        'peer_subkeys': nrm(ks[27], (DEPTH, PEER_HEADS, 2, PEER_KEYS, PEER_HALF), PEER_HALF ** -0.5),
        'peer_u': nrm(ks[28], (DEPTH, N_EXPERTS, D_MODEL), D_MODEL ** -0.5),
        'peer_v': nrm(ks[29], (DEPTH, N_EXPERTS, D_MODEL), (PEER_HEADS * PEER_TOPK) ** -0.5),
    }


def reference(x_prompt, x_sample, mem_prompt, cache_mem_k, cache_mem_v, state_conv_a,
              state_ssd_conv, state_ssd, norm_mix, norm_mem, norm_ffn, norm_final, w_in,
              a_conv_w, a_out, ssd_conv_w, ssd_conv_b, ssd_dt_bias, ssd_a_log, ssd_d, ssd_norm,
              ssd_out, w_mem_k, w_mem_v, xatt_out, w_o, peer_wq, peer_subkeys, peer_u, peer_v):
    f32 = jnp.float32
    xp, xs = x_prompt, x_sample
    bp, lp = x_prompt.shape[:2]
    ls = x_sample.shape[1]
    chunk_p = SSD_CHUNK if lp % SSD_CHUNK == 0 else lp
    mk_p_l, mv_p_l, ca_p_l, sc_p_l, st_p_l, ca_s_l, sc_s_l, st_s_l = [], [], [], [], [], [], [], []
    for l in range(DEPTH):
        mix_w = (w_in[l], a_conv_w[l], a_out[l], ssd_conv_w[l], ssd_conv_b[l], ssd_dt_bias[l],
                 ssd_a_log[l], ssd_d[l], ssd_norm[l], ssd_out[l], xatt_out[l], w_o[l])
        ffn_w = (peer_wq[l], peer_subkeys[l], peer_u[l], peer_v[l])
        mem_k_p, mem_v_p = memory_kv(mem_prompt, norm_mem[l], w_mem_k[l], w_mem_v[l])
        h, ca_p, sc_p, st_p = hybrid_mixer(
            rmsnorm(xp, norm_mix[l]), mem_k_p, mem_v_p,
            jnp.zeros((bp, A_CONV - 1, A_WIDTH), xp.dtype),
            jnp.zeros((bp, SSD_CONV - 1, SSD_XBC), xp.dtype),
            jnp.zeros((bp, SSD_HEADS, SSD_HEAD_DIM, SSD_STATE), f32),
            chunk_p, *mix_w)
        xp = xp + h
        xp = xp + peer_ffn(rmsnorm(xp, norm_ffn[l]), *ffn_w)
        h, ca_s, sc_s, st_s = hybrid_mixer(
            rmsnorm(xs, norm_mix[l]), cache_mem_k[l], cache_mem_v[l],
            state_conv_a[l], state_ssd_conv[l], state_ssd[l], ls, *mix_w)
        xs = xs + h
        xs = xs + peer_ffn(rmsnorm(xs, norm_ffn[l]), *ffn_w)
        mk_p_l.append(mem_k_p)
        mv_p_l.append(mem_v_p)
        ca_p_l.append(ca_p)
        sc_p_l.append(sc_p)
        st_p_l.append(st_p)
        ca_s_l.append(ca_s)
        sc_s_l.append(sc_s)
        st_s_l.append(st_s)
    y_prompt = rmsnorm(xp, norm_final)
    y_sample = rmsnorm(xs, norm_final)
    return (y_prompt, y_sample,
            jnp.stack(mk_p_l), jnp.stack(mv_p_l), jnp.stack(ca_p_l), jnp.stack(sc_p_l), jnp.stack(st_p_l),
            jnp.stack(ca_s_l), jnp.stack(sc_s_l), jnp.stack(st_s_l))
```

```python
import numpy as np
import concourse.bass as bass
import concourse.mybir as mybir
from concourse.bass_utils import run_bass_kernel_spmd

F32 = mybir.dt.float32
BF16 = mybir.dt.bfloat16
AF = mybir.ActivationFunctionType
ALU = mybir.AluOpType

D = 4096
NPRE = 1024
NOWN = 1152
NALL = NPRE + NOWN
INW = 30784
C_AIN, C_ABG, C_ACG, C_Z, C_XBC, C_DT, C_Q, C_G = 0, 2048, 4096, 6144, 10240, 16384, 16448, 18496
EPS = 1e-6


class T:
    _n = 0

    def __init__(self, h, key):
        self.h = h
        self.key = key

    def __getitem__(self, idx):
        return self.h[idx]

    def ap(self):
        return self.h.ap()


class HL:
    NDMA = 32

    def __init__(self, nc):
        self.nc = nc
        self.eng = {"pe": nc.tensor, "dve": nc.vector, "act": nc.scalar, "pool": nc.gpsimd, "sp": nc.sync}
        self.sem = {e: nc.alloc_semaphore("s_" + e) for e in ("pe", "dve", "act", "pool")}
        self.cnt = {e: 0 for e in self.sem}
        self.dsem = [nc.alloc_semaphore("d%d" % i) for i in range(self.NDMA)]
        self.dcnt = [0] * self.NDMA
        self.dnext = 0
        self.seen = {e: {} for e in self.eng}
        self.lastw = {}
        self.reads = {}
        self.nid = 0

    def sb(self, shape, dt=F32, name=None):
        self.nid += 1
        name = (name or "sb") + "_%d" % self.nid
        return T(self.nc.alloc_sbuf_tensor(name, list(shape), dt), name)

    def ps(self, shape, dt=F32, name=None):
        self.nid += 1
        name = name or ("ps%d" % self.nid)
        return T(self.nc.alloc_psum_tensor(name, list(shape), dt), name)

    def dram(self, name, shape, dt=F32, kind="Internal"):
        return T(self.nc.dram_tensor(name, list(shape), dt, kind=kind), name)

    def _wait(self, e, ticket):
        kind, ident, val = ticket
        k = (kind, ident)
        if self.seen[e].get(k, 0) >= val:
            return
        self.seen[e][k] = val
        if kind == "c":
            if ident == e and e == "pe":
                return
            self.eng[e].wait_ge(self.sem[ident], val)
        else:
            self.eng[e].wait_ge(self.dsem[ident], 16 * val)

    @staticmethod
    def _k(b):
        return b.key if isinstance(b, T) else b

    def _deps(self, e, reads, writes):
        for b in list(reads) + list(writes):
            t = self.lastw.get(self._k(b))
            if t is not None:
                self._wait(e, t)
        for b in writes:
            for t in self.reads.get(self._k(b), ()):
                self._wait(e, t)

    def _record(self, ticket, reads, writes):
        for b in reads:
            k = self._k(b)
            lst = self.reads.setdefault(k, [])
            lst.append(ticket)
            if len(lst) > 16:
                d = {}
                for t in lst:
                    kk = (t[0], t[1])
                    if kk not in d or d[kk][2] < t[2]:
                        d[kk] = t
                self.reads[k] = list(d.values())
        for b in writes:
            k = self._k(b)
            self.lastw[k] = ticket
            self.reads[k] = []

    def op(self, e, fn, reads=(), writes=()):
        px = [b for b in reads if isinstance(b, T) and b.key.startswith("ps") and e != "pe"]
        if px:
            writes = list(writes) + px
        self._deps(e, reads, writes)
        inst = fn(self.eng[e])
        inst.then_inc(self.sem[e], 1)
        self.cnt[e] += 1
        ticket = ("c", e, self.cnt[e])
        self._record(ticket, reads, writes)
        return ticket

    def dma(self, out, in_, reads=(), writes=(), q="sp", **kw):
        s = self.dnext
        self.dnext = (self.dnext + 1) % self.NDMA
        if self.dcnt[s] > 0:
            self._wait(q, ("d", s, self.dcnt[s]))
        self._deps(q, reads, writes)
        inst = self.eng[q].dma_start(out=out, in_=in_, **kw)
        inst.then_inc(self.dsem[s], 16)
        self.dcnt[s] += 1
        ticket = ("d", s, self.dcnt[s])
        self._record(ticket, reads, writes)
        return ticket

    def barrier(self):
        for e in self.eng:
            for s in range(self.NDMA):
                if self.dcnt[s]:
                    self._wait(e, ("d", s, self.dcnt[s]))
            for e2 in self.sem:
                if self.cnt[e2] and e2 != e:
                    self._wait(e, ("c", e2, self.cnt[e2]))

    def mark(self):
        return (self.nc.sbuf_base, self.nc.sbuf_top)

    def release(self, m):
        self.barrier()
        self.nc.sbuf_base, self.nc.sbuf_top = m

    def finish(self):
        for s in range(self.NDMA):
            if self.dcnt[s]:
                self._wait("sp", ("d", s, self.dcnt[s]))
        for e in self.sem:
            if self.cnt[e]:
                self._wait("sp", ("c", e, self.cnt[e]))


def build(stop_after=99, dbg=False):
    nc = bass.Bass("TRN2", target_bir_lowering=False)
    hl = HL(nc)
    op, dma = hl.op, hl.dma

    def din(name, shape, dt=F32):
        return hl.dram(name, shape, dt, kind="ExternalInput")

    def dout(name, shape, dt=F32):
        return hl.dram(name, shape, dt, kind="ExternalOutput")

    SHP = dict(xa=[NOWN, D], xpre=[NPRE, D], flag=[128, 1], mem=[256, D], ck=[16, 256, 2048], cv=[16, 256, 2048],
               sca=[16, 2, 2048], ssc=[16, 3, 6144], sst=[16, 64, 64, 128], norm_mix=[D], norm_mem=[D], norm_ffn=[D],
               norm_final=[D], w_in=[D, INW], a_conv_w=[3, 2048], a_out=[2048, D], ssd_conv_w=[4, 6144], ssd_conv_b=[6144],
               ssd_dt_bias=[64], ssd_a_log=[64], ssd_d=[64], ssd_norm=[D], ssd_out=[D, D], w_mem_k=[D, 2048],
               w_mem_v=[D, 2048], xatt_out=[2048, D], w_o=[D, D], peer_wq=[D, 2048], peer_subkeys=[16, 128, 128],
               peer_u=[16384, D], peer_v=[16384, D], c_ident=[128, 128], c_masks=[128, 6, 128], c_rowm=[128, 32])
    used = {}

    class _I:
        def __getattr__(self, name):
            if name not in used:
                used[name] = din(name, SHP[name])
            return used[name]
    I = _I()
    nc._used_inputs = used
    nc._hl = hl

    y = dout("y", [NOWN, D])
    mk = dout("mk", [256, 2048]); mv = dout("mv", [256, 2048])
    ca_p = dout("ca_p", [2, 2048]); sc_p = dout("sc_p", [3, 6144]); st_p = dout("st_p", [64, 64, 128])
    ca_s = dout("ca_s", [16, 2, 2048]); sc_s = dout("sc_s", [16, 3, 6144]); st_s = dout("st_s", [16, 64, 64, 128])

    projT = hl.dram("projT", [INW, NALL], BF16)
    dtT = hl.dram("dtT", [64, NALL], F32)

    ident = hl.sb([128, 128], F32, "ident")
    identb = hl.sb([128, 128], BF16, "identb")
    ss = hl.sb([128, 4], F32, "ss")
    mA = hl.mark()
    kT = hl.sb([128, 16, 256], BF16, "kT")
    vtok = hl.sb([128, 2, 2048], BF16, "vtok")
    psg = [hl.ps([128, 1536], F32, "psg%d" % i) for i in range(2)]
    pst = [hl.ps([128, 512], F32, "pst%d" % i) for i in range(2)]
    cnt = {"g": 0, "t": 0, "x": 0, "s": 0}
    B = {}

    def bank(i, j):
        t = T(psg[i].h, "psg%d_b%d" % (i, j))
        return t
    PB = [[bank(i, j) for j in range(3)] for i in range(2)]

    def alloc_gemm(ntok_act, kc_act=32, norm=True, gemm_bufs=True):
        if norm:
            B["gbc"] = hl.sb([128, D], F32, "gbc")
            B["xt"] = [hl.sb([128, D], F32, "xt%d" % i) for i in range(2)]
            B["junk"] = hl.sb([128, D], BF16, "junk")
        if ntok_act:
            B["actT"] = hl.sb([128, kc_act, ntok_act], BF16, "actT")
        if not gemm_bufs:
            return
        B["wst"] = [hl.sb([128, 32, 128], F32, "wst%d" % i) for i in range(2)]
        B["wbf"] = [hl.sb([128, 32, 128], BF16, "wbf%d" % i) for i in range(2)]
        B["stg"] = [hl.sb([128, NOWN], BF16, "stg%d" % i) for i in range(2)]
        B["stgf"] = [hl.sb([128, NOWN], F32, "stgf%d" % i) for i in range(2)]

    dma(ident[:], I.c_ident.ap(), writes=[ident])
    op("dve", lambda e: e.tensor_copy(out=identb[:], in_=ident[:]), reads=[ident], writes=[identb])

    def load_gain(g):
        dma(B["gbc"][:], g.ap().partition_broadcast(128), writes=[B["gbc"]])

    def rms_tile(src_ap, xb, rd=()):
        dma(xb[:], src_ap, reads=list(rd), writes=[xb])
        op("dve", lambda e: e.memset(ss[:, 0:1], 0.0), writes=[ss])
        op("act", lambda e: e.activation(out=B["junk"][:], in_=xb[:], func=AF.Square, accum_out=ss[:, 0:1]),
           reads=[xb, ss], writes=[B["junk"], ss])
        op("dve", lambda e: e.tensor_scalar(out=ss[:, 1:2], in0=ss[:, 0:1], scalar1=1.0 / D, scalar2=EPS,
                                            op0=ALU.mult, op1=ALU.add), reads=[ss], writes=[ss])
        op("act", lambda e: e.activation(out=ss[:, 3:4], in_=ss[:, 1:2], func=AF.Sqrt), reads=[ss], writes=[ss])
        op("dve", lambda e: e.reciprocal(out=ss[:, 2:3], in_=ss[:, 3:4]), reads=[ss], writes=[ss])
        op("dve", lambda e: e.scalar_tensor_tensor(out=xb[:], in0=xb[:], scalar=ss[:, 2:3], in1=B["gbc"][:],
                                                   op0=ALU.mult, op1=ALU.mult), reads=[xb, ss, B["gbc"]], writes=[xb])

    def to_fm(xb, dst, col0):
        for g4 in range(8):
            p = pst[cnt["t"] % 2]; cnt["t"] += 1
            for j in range(4):
                c = g4 * 4 + j
                op("pe", lambda e: e.transpose(p[:, j * 128:(j + 1) * 128], xb[:, c * 128:(c + 1) * 128], ident[:]),
                   reads=[xb, ident], writes=[p])
            eng = "act" if g4 % 2 == 0 else "dve"
            src = p[:, :].rearrange("p (j n) -> p j n", n=128)
            dsta = dst[:, g4 * 4:(g4 + 1) * 4, col0:col0 + 128]
            if eng == "act":
                op("act", lambda e: e.copy(out=dsta, in_=src), reads=[p], writes=[dst])
            else:
                op("dve", lambda e: e.tensor_copy(out=dsta, in_=src), reads=[p], writes=[dst])

    def norm_fm(src, nt, gain, dst, rd=()):
        load_gain(gain)
        for i in range(nt):
            xb = B["xt"][cnt["x"] % 2]; cnt["x"] += 1
            rms_tile(src[i * 128:(i + 1) * 128, :], xb, rd)
            to_fm(xb, dst, i * 128)

    def gemm(act, KC, t0, ntok, wfn, nblk, epi):
        nch = (ntok + 511) // 512
        csz = ntok // nch
        assert csz * nch == ntok

        def load(b):
            i = b % 2
            dma(B["wst"][i][:, 0:KC, :], wfn(b), writes=[B["wst"][i]])
            h2 = KC // 2
            op("dve", lambda e: e.tensor_copy(out=B["wbf"][i][:, 0:h2, :], in_=B["wst"][i][:, 0:h2, :]), reads=[B["wst"][i]], writes=[B["wbf"][i]])
            op("act", lambda e: e.copy(out=B["wbf"][i][:, h2:KC, :], in_=B["wst"][i][:, h2:KC, :]), reads=[B["wst"][i]], writes=[B["wbf"][i]])
        load(0)
        for b in range(nblk):
            if b + 1 < nblk:
                load(b + 1)
            ps = psg[cnt["g"] % 2]; cnt["g"] += 1
            w = B["wbf"][b % 2]
            for c in range(nch):
                for kc in range(KC):
                    op("pe", lambda e: e.matmul(ps[:, c * 512:c * 512 + csz], lhsT=w[:, kc, :],
                                                rhs=act[:, kc, t0 + c * csz:t0 + (c + 1) * csz],
                                                start=(kc == 0), stop=(kc == KC - 1)),
                       reads=[w, act], writes=[ps])
            psv = ps[:, 0:nch * 512].rearrange("p (c n) -> p c n", n=512)[:, :, 0:csz]
            epi(b, ps, psv, nch, csz)

    def wview(W, c0):
        v = W.ap().rearrange("(kc p) n -> p kc n", p=128)
        return lambda b: v[:, :, c0 + b * 128:c0 + (b + 1) * 128]

    def v3(t, nch, csz, n0=0):
        return t[:, n0:n0 + nch * csz].rearrange("p (c n) -> p c n", n=csz)

    m0 = hl.mark()
    alloc_gemm(256)
    if stop_after == 0.05:
        load_gain(I.norm_mem)
        dma(mk.ap()[0:128, :], B["gbc"][:, 0:2048], reads=[B["gbc"]], writes=[mk])
        hl.finish(); return nc
    if stop_after == 0.07:
        load_gain(I.norm_mem)
        rms_tile(I.mem.ap()[0:128, :], B["xt"][0])
        dma(mk.ap()[0:128, :], B["xt"][0][:, 0:2048], reads=[B["xt"][0]], writes=[mk])
        hl.finish(); return nc
    norm_fm(I.mem.ap(), 2, I.norm_mem, B["actT"])
    if stop_after == 0.1:
        o = B["stgf"][0]
        op("dve", lambda e: e.tensor_copy(out=o[:, 0:256], in_=B["actT"][:, 0, 0:256]), reads=[B["actT"]], writes=[o])
        dma(mk.ap()[0:128, 0:256], o[:, 0:256], reads=[o], writes=[mk])
        hl.finish(); return nc

    def kv_epi(dst_dram, keepT, keepTok):
        def epi(b, ps, psv, nch, csz):
            s = B["stgf"][cnt["s"] % 2]; cnt["s"] += 1
            op("act", lambda e: e.copy(out=s[:, 0:256], in_=ps[:, 0:256]), reads=[ps], writes=[s])
            if keepT is not None:
                op("dve", lambda e: e.tensor_copy(out=keepT[:, b, :], in_=ps[:, 0:256]), reads=[ps], writes=[keepT])
            p = pst[cnt["t"] % 2]; cnt["t"] += 1
            for i in range(2):
                op("pe", lambda e: e.transpose(p[:, i * 128:(i + 1) * 128], s[:, i * 128:(i + 1) * 128], ident[:]),
                   reads=[s, ident], writes=[p])
            o = B["stgf"][cnt["s"] % 2]; cnt["s"] += 1
            op("dve", lambda e: e.tensor_copy(out=o[:, 0:256], in_=p[:, 0:256]), reads=[p], writes=[o])
            if keepTok is not None:
                op("act", lambda e: e.copy(out=keepTok[:, :, b * 128:(b + 1) * 128],
                                           in_=p[:, 0:256].rearrange("p (i n) -> p i n", n=128)), reads=[p], writes=[keepTok])
            if stop_after == 0.3:
                dma(dst_dram.ap()[0:128, 0:256], o[:, 0:256], reads=[o], writes=[dst_dram])
            else:
                for i in range(2):
                    dma(dst_dram.ap()[i * 128:(i + 1) * 128, b * 128:(b + 1) * 128], o[:, i * 128:(i + 1) * 128], reads=[o], writes=[dst_dram])
        return epi
    if stop_after in (0.2, 0.25):
        def epi0(b, ps, psv, nch, csz):
            s = B["stgf"][cnt["s"] % 2]; cnt["s"] += 1
            op("act", lambda e: e.copy(out=s[:, 0:256], in_=ps[:, 0:256]), reads=[ps], writes=[s])
            dma(mk.ap()[0:128, (b % 8) * 256:(b % 8 + 1) * 256], s[:, 0:256], reads=[s], writes=[mk])
        gemm(B["actT"], 32, 0, 256, wview(I.w_mem_k, 0), 2 if stop_after == 0.2 else 16, epi0)
        hl.finish(); return nc
    gemm(B["actT"], 32, 0, 256, wview(I.w_mem_k, 0), 16, kv_epi(mk, kT, None))
    gemm(B["actT"], 32, 0, 256, wview(I.w_mem_v, 0), 16, kv_epi(mv, None, vtok))
    hl.release(m0)
    if stop_after <= 1:
        hl.finish(); return nc
    alloc_gemm(NOWN)

    def proj_epi(row0, tokbase):
        def epi(b, ps, psv, nch, csz):
            s = B["stg"][cnt["s"] % 2]; cnt["s"] += 1
            op("act", lambda e: e.copy(out=v3(s, nch, csz), in_=psv), reads=[ps], writes=[s])
            r = row0 + b * 128
            dma(projT[r:r + 128, tokbase:tokbase + nch * csz], s[:, 0:nch * csz], reads=[s], writes=[projT])
        return epi

    def dt_epi(tokbase):
        def epi(b, ps, psv, nch, csz):
            s = B["stgf"][cnt["s"] % 2]; cnt["s"] += 1
            op("act", lambda e: e.copy(out=v3(s, nch, csz)[0:64], in_=psv[0:64]), reads=[ps], writes=[s])
            dma(dtT[:, tokbase:tokbase + nch * csz], s[0:64, 0:nch * csz], reads=[s], writes=[dtT])
        return epi

    def wview_dt(b):
        v = I.w_in.ap().rearrange("(kc p) n -> p kc n", p=128)
        return v[:, :, C_DT:C_DT + 128]

    norm_fm(I.xpre.ap(), 8, I.norm_mix, B["actT"])
    gemm(B["actT"], 32, 0, NPRE, wview(I.w_in, C_XBC), 48, proj_epi(C_XBC, 0))
    gemm(B["actT"], 32, 0, NPRE, wview_dt, 1, dt_epi(0))
    gemm(B["actT"], 32, NPRE - 128, 128, wview(I.w_in, C_AIN), 16, proj_epi(C_AIN, NPRE - 128))
    gemm(B["actT"], 32, NPRE - 128, 128, wview(I.w_in, C_ACG), 16, proj_epi(C_ACG, NPRE - 128))
    norm_fm(I.xa.ap(), 9, I.norm_mix, B["actT"])
    gemm(B["actT"], 32, 0, NOWN, wview(I.w_in, 0), 128, proj_epi(0, NPRE))
    gemm(B["actT"], 32, 0, NOWN, wview_dt, 1, dt_epi(NPRE))
    gemm(B["actT"], 32, 0, NOWN, wview(I.w_in, C_Q), 112, proj_epi(C_Q, NPRE))

    hl.release(m0)
    acw = hl.sb([128, 16, 3], F32, "acw")
    for c in range(16):
        dma(acw[:, c, :], I.a_conv_w.ap()[:, c * 128:(c + 1) * 128].rearrange("k p -> p k"), writes=[acw], allow_slow_non_contiguous=True)
    scaN = hl.sb([32, 2048], F32, "scaN")
    dma(scaN[:], I.sca.ap().rearrange("s k f -> (s k) f"), writes=[scaN])
    caPS = hl.sb([128, 2, 2048], F32, "caPS")
    ain = hl.sb([128, 1280], BF16, "ain"); acg = hl.sb([128, 1280], BF16, "acg"); abg = hl.sb([128, NOWN], BF16, "abg")
    u = hl.sb([128, 1280], F32, "u"); uext = hl.sb([128, 16, 10], F32, "uext"); cvo = hl.sb([128, NOWN], F32, "cvo")
    vT = hl.sb([128, 16, NOWN], BF16, "vT")
    for c in range(16):
        dma(ain[:], projT[C_AIN + c * 128:C_AIN + (c + 1) * 128, 896:2176], reads=[projT], writes=[ain])
        dma(acg[:], projT[C_ACG + c * 128:C_ACG + (c + 1) * 128, 896:2176], reads=[projT], writes=[acg])
        dma(abg[:], projT[C_ABG + c * 128:C_ABG + (c + 1) * 128, 1024:2176], reads=[projT], writes=[abg])
        op("dve", lambda e: e.tensor_tensor(out=u[:], in0=ain[:], in1=acg[:], op=ALU.mult), reads=[ain, acg], writes=[u])
        op("dve", lambda e: e.tensor_scalar(out=cvo[:, 0:1024], in0=u[:, 126:1150], scalar1=acw[:, c, 0:1], scalar2=None, op0=ALU.mult),
           reads=[u, acw], writes=[cvo])
        for k in (1, 2):
            op("dve", lambda e: e.scalar_tensor_tensor(out=cvo[:, 0:1024], in0=u[:, 126 + k:1150 + k], scalar=acw[:, c, k:k + 1],
                                                       in1=cvo[:, 0:1024], op0=ALU.mult, op1=ALU.add), reads=[u, acw, cvo], writes=[cvo])
        kk = cnt["t"] % 2; cnt["t"] += 1
        op("pe", lambda e: e.transpose(pst[kk][:, 0:32], scaN[0:32, c * 128:(c + 1) * 128], ident[0:32, 0:32]), reads=[scaN, ident], writes=[pst[kk]])
        op("act", lambda e: e.copy(out=uext[:, :, 0:2], in_=pst[kk][:, 0:32].rearrange("p (s k) -> p s k", k=2)), reads=[pst[kk]], writes=[uext])
        for j, c0 in enumerate((1024, 1152)):
            op("pe", lambda e: e.transpose(pst[kk][:, 128 + j * 128:256 + j * 128], u[:, c0:c0 + 128], ident[:]), reads=[u, ident], writes=[pst[kk]])
        op("dve", lambda e: e.tensor_copy(out=caPS[:, :, c * 128:(c + 1) * 128], in_=pst[kk][:, 128:384].rearrange("p (a b) -> p a b", b=128)),
           reads=[pst[kk]], writes=[caPS])
        op("act", lambda e: e.copy(out=uext[:, :, 2:10], in_=u[:, 1152:1280].rearrange("p (s t) -> p s t", t=8)), reads=[u], writes=[uext])
        cs = cvo[:, 1024:1152].rearrange("p (s t) -> p s t", t=8)
        op("dve", lambda e: e.tensor_scalar(out=cs, in0=uext[:, :, 0:8], scalar1=acw[:, c, 0:1], scalar2=None, op0=ALU.mult),
           reads=[uext, acw], writes=[cvo])
        for k in (1, 2):
            op("dve", lambda e: e.scalar_tensor_tensor(out=cs, in0=uext[:, :, k:k + 8], scalar=acw[:, c, k:k + 1], in1=cs,
                                                       op0=ALU.mult, op1=ALU.add), reads=[uext, acw, cvo], writes=[cvo])
        op("dve", lambda e: e.tensor_tensor(out=vT[:, c, :], in0=cvo[:], in1=abg[:], op=ALU.mult), reads=[cvo, abg], writes=[vT])
    dma(ca_p.ap(), caPS[126:128, 0, :], reads=[caPS], writes=[ca_p])
    for sq in range(16):
        dma(ca_s.ap()[sq], caPS[sq * 8 + 6:sq * 8 + 8, 1, :], reads=[caPS], writes=[ca_s])
    if stop_after <= 3:
        hl.finish(); return nc
    mixT = hl.dram("mixT", [D, NOWN], F32, kind="ExternalOutput" if dbg else "Internal")
    gt = hl.sb([128, NOWN], BF16, "gt"); gs = hl.sb([128, NOWN], F32, "gs"); mprev = hl.sb([128, NOWN], F32, "mprev")

    def gate_epi(gi, first):
        def epi(b, ps, psv, nch, csz):
            r0 = C_G + gi * D + b * 128
            dma(gt[:], projT[r0:r0 + 128, NPRE:NALL], reads=[projT], writes=[gt])
            op("act", lambda e: e.activation(out=gs[:], in_=gt[:], func=AF.Sigmoid), reads=[gt], writes=[gs])
            if not first:
                dma(mprev[:], mixT[b * 128:(b + 1) * 128, :], reads=[mixT], writes=[mprev])
            s = B["stgf"][cnt["s"] % 2]; cnt["s"] += 1
            op("dve", lambda e: e.tensor_tensor(out=v3(s, nch, csz), in0=psv, in1=v3(gs, nch, csz), op=ALU.mult), reads=[ps, gs], writes=[s])
            if not first:
                op("dve", lambda e: e.tensor_tensor(out=s[:], in0=s[:], in1=mprev[:], op=ALU.add), reads=[s, mprev], writes=[s])
            dma(mixT[b * 128:(b + 1) * 128, :], s[:], reads=[s], writes=[mixT])
        return epi
    alloc_gemm(0, norm=False)
    gemm(vT, 16, 0, NOWN, wview(I.a_out, 0), 32, gate_epi(0, True))
    hl.release(m0)
    if stop_after <= 3.5:
        hl.finish(); return nc

    xcT = hl.dram("xcT", [6144, NALL], BF16)
    yT = hl.dram("yT", [D, NOWN], BF16, kind="ExternalOutput" if dbg else "Internal")
    cw = hl.sb([128, 48, 4], F32, "cw"); cb = hl.sb([128, 48], F32, "cb")
    for c in range(48):
        dma(cw[:, c, :], I.ssd_conv_w.ap()[:, c * 128:(c + 1) * 128].rearrange("k p -> p k"), writes=[cw], allow_slow_non_contiguous=True)
    dma(cb[:], I.ssd_conv_b.ap().rearrange("(c p) -> p c", p=128), writes=[cb], allow_slow_non_contiguous=True)
    xin = [hl.sb([128, NALL], BF16, "xin%d" % i) for i in range(2)]
    xe = hl.sb([128, 3 + 2048], F32, "xe"); xes = hl.sb([128, 16, 11], F32, "xes")
    cacc = hl.sb([128, NALL], F32, "cacc"); xco = [hl.sb([128, NALL], BF16, "xco%d" % i) for i in range(2)]
    sscN = hl.sb([48, 6144], F32, "sscN")
    dma(sscN[:], I.ssc.ap().rearrange("s k f -> (s k) f"), writes=[sscN])
    scPS = hl.sb([128, 2, 6144], F32, "scPS")
    pbb = [pst[i][:, :].bitcast(BF16) for i in range(2)]
    op("dve", lambda e: e.memset(xe[:, 0:3], 0.0), writes=[xe])
    for c in range(48):
        xi = xin[c % 2]; xo = xco[c % 2]
        dma(xi[:], projT[C_XBC + c * 128:C_XBC + (c + 1) * 128, :], reads=[projT], writes=[xi])
        op("act", lambda e: e.copy(out=xe[:, 3:3 + 2048], in_=xi[:, 0:2048]), reads=[xi], writes=[xe])
        kk = cnt["t"] % 2; cnt["t"] += 1
        op("pe", lambda e: e.transpose(pst[kk][:, 0:48], sscN[0:48, c * 128:(c + 1) * 128], ident[0:48, 0:48]), reads=[sscN, ident], writes=[pst[kk]])
        op("act", lambda e: e.copy(out=xes[:, :, 0:3], in_=pst[kk][:, 0:48].rearrange("p (s k) -> p s k", k=3)), reads=[pst[kk]], writes=[xes])
        for j, c0 in enumerate((1920, 2048)):
            op("pe", lambda e: e.transpose(pbb[kk][:, 256 + j * 128:384 + j * 128], xi[:, c0:c0 + 128], identb[:]), reads=[xi, identb], writes=[pst[kk]])
        op("dve", lambda e: e.tensor_copy(out=scPS[:, :, c * 128:(c + 1) * 128], in_=pbb[kk][:, 256:512].rearrange("p (a b) -> p a b", b=128)),
           reads=[pst[kk]], writes=[scPS])
        op("act", lambda e: e.copy(out=xes[:, :, 3:11], in_=xi[:, 2048:NALL].rearrange("p (s t) -> p s t", t=8)), reads=[xi], writes=[xes])
        cs = cacc[:, 2048:NALL].rearrange("p (s t) -> p s t", t=8)
        op("dve", lambda e: e.tensor_scalar(out=cacc[:, 0:2048], in0=xe[:, 0:2048], scalar1=cw[:, c, 0:1], scalar2=None, op0=ALU.mult),
           reads=[xe, cw], writes=[cacc])
        op("dve", lambda e: e.tensor_scalar(out=cs, in0=xes[:, :, 0:8], scalar1=cw[:, c, 0:1], scalar2=None, op0=ALU.mult),
           reads=[xes, cw], writes=[cacc])
        for k in (1, 2, 3):
            op("dve", lambda e: e.scalar_tensor_tensor(out=cacc[:, 0:2048], in0=xe[:, k:k + 2048], scalar=cw[:, c, k:k + 1],
                                                       in1=cacc[:, 0:2048], op0=ALU.mult, op1=ALU.add), reads=[xe, cw, cacc], writes=[cacc])
            op("dve", lambda e: e.scalar_tensor_tensor(out=cs, in0=xes[:, :, k:k + 8], scalar=cw[:, c, k:k + 1], in1=cs,
                                                       op0=ALU.mult, op1=ALU.add), reads=[xes, cw, cacc], writes=[cacc])
        op("act", lambda e: e.activation(out=xo[:], in_=cacc[:], func=AF.Silu, bias=cb[:, c:c + 1]), reads=[cacc, cb], writes=[xo])
        dma(xcT[c * 128:(c + 1) * 128, :], xo[:], reads=[xo], writes=[xcT])
    dma(sc_p.ap(), scPS[125:128, 0, :], reads=[scPS], writes=[sc_p])
    for sq in range(16):
        dma(sc_s.ap()[sq], scPS[sq * 8 + 5:sq * 8 + 8, 1, :], reads=[scPS], writes=[sc_s])
    hl.release(m0)
    if stop_after <= 4.1:
        hl.finish(); return nc

    MK = hl.sb([128, 6, 128], F32, "MK")
    dma(MK[:], I.c_masks.ap(), writes=[MK])
    rowm = hl.sb([128, 32], F32, "rowm")
    dma(rowm[:], I.c_rowm.ap(), writes=[rowm])
    hv = hl.sb([64, 4], F32, "hv")
    dma(hv[:, 0:1], I.ssd_dt_bias.ap().rearrange("(p o) -> p o", o=1), writes=[hv])
    dma(hv[:, 1:2], I.ssd_a_log.ap().rearrange("(p o) -> p o", o=1), writes=[hv])
    op("act", lambda e: e.activation(out=hv[:, 2:3], in_=hv[:, 1:2], func=AF.Exp), reads=[hv], writes=[hv])
    op("dve", lambda e: e.tensor_scalar(out=hv[:, 2:3], in0=hv[:, 2:3], scalar1=-1.0, scalar2=None, op0=ALU.mult), reads=[hv], writes=[hv])
    dtA = hl.sb([64, NALL], F32, "dtA"); adtA = hl.sb([64, NALL], F32, "adtA")
    dma(dtA[:], dtT.ap(), reads=[dtT], writes=[dtA])
    op("act", lambda e: e.activation(out=dtA[:], in_=dtA[:], func=AF.Exp, bias=hv[:, 0:1]), reads=[dtA, hv], writes=[dtA])
    op("act", lambda e: e.activation(out=dtA[:], in_=dtA[:], func=AF.Ln, bias=1.0), reads=[dtA], writes=[dtA])
    op("dve", lambda e: e.tensor_scalar(out=adtA[:], in0=dtA[:], scalar1=hv[:, 2:3], scalar2=None, op0=ALU.mult), reads=[dtA, hv], writes=[adtA])
    dbc = hl.sb([128, 64], F32, "dbc")
    dma(dbc[:], I.ssd_d.ap().partition_broadcast(128), writes=[dbc])
    nbc = hl.sb([128, D], F32, "nbc")
    dma(nbc[:], I.ssd_norm.ap().partition_broadcast(128), writes=[nbc])
    flg = hl.sb([128, 1], F32, "flg")
    dma(flg[:], I.flag.ap(), writes=[flg])

    xcc = hl.sb([128, 48, 128], BF16, "xcc")
    zc = hl.sb([128, 32, 128], BF16, "zc")
    x_tok = hl.sb([128, D], BF16, "x_tok"); xs_tok = hl.sb([128, D], BF16, "xs_tok"); xs_dec = hl.sb([128, D], BF16, "xs_dec")
    B_tok = hl.sb([128, 1024], BF16, "B_tok"); Bm = hl.sb([128, 1024], BF16, "Bm")
    z_tok = hl.sb([128, D], BF16, "z_tok")
    ytk = hl.sb([128, D], F32, "ytk"); ybf = hl.sb([128, D], BF16, "ybf")
    dtk = hl.sb([128, 64 * 6], F32, "dtk")
    DT_, ADT_, ACU_, TOT_, EAC_, EDS_ = [dtk[:, i * 64:(i + 1) * 64] for i in range(6)]
    cdb = hl.sb([128, 64], F32, "cdb")
    segrs = [hl.sb([128, 4, 128], F32, "segr%d" % i) for i in range(1)] * 2; LTs = [hl.sb([128, 4, 128], F32, "LT%d" % i) for i in range(1)] * 2; MT = [hl.sb([128, 4, 128], BF16, "MT%d" % i) for i in range(2)]
    CBms = [hl.sb([128, 128], F32, "CBm%d" % i) for i in range(2)]
    hT = hl.sb([128, D], F32, "hT"); hTb = hl.sb([128, D], BF16, "hTb")
    hnat = hl.sb([128, 32, 128], F32, "hnat")
    yoff = hl.sb([128, D], F32, "yoff")
    selm = hl.sb([128, 128], F32, "selm")
    g8 = hl.sb([128, 16], F32, "g8")
    yTs = hl.sb([128, 32, 128], BF16, "yTs")
    pb = [pst[i][:, :].bitcast(BF16) for i in range(2)]

    def tr_bf(src_fn, n, dst, dcol0, w=128):
        for g0 in range(0, n, 8):
            k = cnt["t"] % 2; cnt["t"] += 1
            m = min(8, n - g0)
            for j in range(m):
                ap_, rd = src_fn(g0 + j)
                op("pe", lambda e: e.transpose(pb[k][:, j * 128:(j + 1) * 128], ap_, identb[:]), reads=[rd, identb], writes=[pst[k]])
            eng = "act" if (g0 // 8) % 2 == 0 else "dve"
            o = dst[:, dcol0 + g0 * 128:dcol0 + (g0 + m) * 128]
            if eng == "act":
                op("act", lambda e: e.copy(out=o, in_=pb[k][:, 0:m * 128]), reads=[pst[k]], writes=[dst])
            else:
                op("dve", lambda e: e.tensor_copy(out=o, in_=pb[k][:, 0:m * 128]), reads=[pst[k]], writes=[dst])

    def small_mm(out_ps, lhsT, rhs, rds):
        op("pe", lambda e: e.matmul(out_ps, lhsT=lhsT, rhs=rhs, start=True, stop=True), reads=rds, writes=[pst[0]])

    def chunk(tok0, smp, need_y, own_tile):
        mo = 3 if smp else 0
        TRI, SL, ONES = MK[:, mo + 0, :], MK[:, mo + 1, :], MK[:, mo + 2, :]
        dma(xcc[:], xcT.ap().rearrange("(c p) t -> p c t", p=128)[:, :, tok0:tok0 + 128], reads=[xcT], writes=[xcc])
        for i, srcA in enumerate((dtA, adtA)):
            op("pe", lambda e: e.transpose(pst[0][:, i * 64:(i + 1) * 64], srcA[:, tok0:tok0 + 128], ident[0:64, 0:64]),
               reads=[srcA, ident], writes=[pst[0]])
        op("dve", lambda e: e.tensor_copy(out=dtk[:, 0:128], in_=pst[0][:, 0:128]), reads=[pst[0]], writes=[dtk])
        small_mm(pst[0][:, 0:64], TRI, ADT_, [MK, dtk])
        small_mm(pst[0][:, 64:128], ONES, ADT_, [MK, dtk])
        op("dve", lambda e: e.tensor_copy(out=dtk[:, 128:256], in_=pst[0][:, 0:128]), reads=[pst[0]], writes=[dtk])
        op("act", lambda e: e.activation(out=EAC_, in_=ACU_, func=AF.Exp), reads=[dtk], writes=[dtk])
        op("dve", lambda e: e.tensor_tensor(out=EDS_, in0=TOT_, in1=ACU_, op=ALU.subtract), reads=[dtk], writes=[dtk])
        op("act", lambda e: e.activation(out=EDS_, in_=EDS_, func=AF.Exp), reads=[dtk], writes=[dtk])
        op("act", lambda e: e.activation(out=cdb[:], in_=TOT_, func=AF.Exp), reads=[dtk], writes=[cdb])
        tr_bf(lambda j: (xcc[:, j, :], xcc), 32, x_tok, 0)
        tr_bf(lambda j: (xcc[:, 32 + j, :], xcc), 8, B_tok, 0)
        x3 = x_tok[:, :].rearrange("p (h q) -> p h q", q=64)
        op("dve", lambda e: e.tensor_tensor(out=xs_tok[:, :].rearrange("p (h q) -> p h q", q=64), in0=x3,
                                            in1=DT_.unsqueeze(2).to_broadcast([128, 64, 64]), op=ALU.mult), reads=[x_tok, dtk], writes=[xs_tok])
        op("pool", lambda e: e.tensor_tensor(out=xs_dec[:, :].rearrange("p (h q) -> p h q", q=64),
                                             in0=xs_tok[:, :].rearrange("p (h q) -> p h q", q=64),
                                             in1=EDS_.unsqueeze(2).to_broadcast([128, 64, 64]), op=ALU.mult), reads=[xs_tok, dtk], writes=[xs_dec])
        if need_y:
            if smp:
                op("dve", lambda e: e.memset(yoff[:], 0.0), writes=[yoff])
            for g in range(8):
                BT, CT = xcc[:, 32 + g, :], xcc[:, 40 + g, :]
                CBm = CBms[g % 2]
                op("pe", lambda e: e.matmul(pst[1][:, 0:128], lhsT=BT, rhs=CT, start=True, stop=True), reads=[xcc], writes=[pst[1]])
                op("dve", lambda e: e.tensor_tensor(out=CBm[:], in0=pst[1][:, 0:128], in1=TRI, op=ALU.mult), reads=[pst[1], MK], writes=[CBm])
                for hh in range(2):
                    h0 = g * 8 + hh * 4
                    segr, LT = segrs[hh], LTs[hh]
                    op("dve", lambda e: e.tensor_tensor(out=segr[:], in0=TRI.unsqueeze(1).to_broadcast([128, 4, 128]),
                                                        in1=dtk[:, 64 + h0:64 + h0 + 4].unsqueeze(2).to_broadcast([128, 4, 128]), op=ALU.mult),
                       reads=[MK, dtk], writes=[segr])
                    pseg = psg[0][:, hh * 512:(hh + 1) * 512]
                    op("pe", lambda e: e.matmul(pseg, lhsT=SL, rhs=segr[:, :, :].rearrange("p a b -> p (a b)"), start=True, stop=True),
                       reads=[MK, segr], writes=[PB[0][hh]])
                    op("act", lambda e: e.activation(out=LT[:, :, :].rearrange("p a b -> p (a b)"), in_=pseg, func=AF.Exp), reads=[PB[0][hh]], writes=[LT])
                    mt = MT[hh]
                    op("dve", lambda e: e.tensor_tensor(out=mt[:], in0=LT[:], in1=CBm[:, :].unsqueeze(1).to_broadcast([128, 4, 128]), op=ALU.mult),
                       reads=[LT, CBm], writes=[mt])
                    for r in range(4):
                        hd = h0 + r
                        op("pe", lambda e: e.matmul(psg[1][:, (hh * 4 + r) * 64:(hh * 4 + r + 1) * 64], lhsT=mt[:, r, :],
                                                    rhs=xs_tok[:, hd * 64:(hd + 1) * 64], start=True, stop=True), reads=[mt, xs_tok], writes=[PB[1][0]])
                if not smp:
                    op("pe", lambda e: e.matmul(psg[1][:, 512:1024], lhsT=CT, rhs=hTb[:, g * 512:(g + 1) * 512], start=True, stop=True),
                       reads=[xcc, hTb], writes=[PB[1][1]])
                    op("act", lambda e: e.copy(out=yoff[:, g * 512:(g + 1) * 512], in_=psg[1][:, 512:1024]), reads=[PB[1][1]], writes=[yoff])
                op("act", lambda e: e.copy(out=ytk[:, g * 512:(g + 1) * 512], in_=psg[1][:, 0:512]), reads=[PB[1][0]], writes=[ytk])
        if not smp:
            for g in range(8):
                sl = slice(g * 512, (g + 1) * 512)
                op("pe", lambda e: e.matmul(psg[1][:, 1024:1536], lhsT=B_tok[:, g * 128:(g + 1) * 128], rhs=xs_dec[:, sl], start=True, stop=True),
                   reads=[B_tok, xs_dec], writes=[PB[1][2]])
                h3 = hT[:, sl].rearrange("p (r q) -> p r q", q=64)
                op("dve", lambda e: e.tensor_tensor(out=h3, in0=h3, in1=cdb[:, g * 8:(g + 1) * 8].unsqueeze(2).to_broadcast([128, 8, 64]), op=ALU.mult),
                   reads=[hT, cdb], writes=[hT])
                op("dve", lambda e: e.tensor_tensor(out=hT[:, sl], in0=hT[:, sl], in1=psg[1][:, 1024:1536], op=ALU.add), reads=[hT, PB[1][2]], writes=[hT])
            op("act", lambda e: e.copy(out=hTb[:], in_=hT[:]), reads=[hT], writes=[hTb])
        else:
            for s in range(16):
                load_state_T(I.sst.ap()[s])
                for g in range(8):
                    op("pe", lambda e: e.matmul(psg[1][:, 512:1024], lhsT=xcc[:, 40 + g, :], rhs=hTb[:, g * 512:(g + 1) * 512], start=True, stop=True),
                       reads=[xcc, hTb], writes=[PB[1][1]])
                    ysl = yoff[:, g * 512:(g + 1) * 512]
                    op("dve", lambda e: e.scalar_tensor_tensor(out=ysl, in0=psg[1][:, 512:1024], scalar=rowm[:, s:s + 1], in1=ysl,
                                                               op0=ALU.mult, op1=ALU.add), reads=[PB[1][1], rowm, yoff], writes=[yoff])
                op("dve", lambda e: e.tensor_scalar(out=selm[:], in0=MK[:, 2, :], scalar1=rowm[:, 16 + s:17 + s], scalar2=None, op0=ALU.mult),
                   reads=[MK, rowm], writes=[selm])
                small_mm(pst[0][:, 0:64], selm[:], TOT_, [selm, dtk])
                op("act", lambda e: e.activation(out=cdb[:], in_=pst[0][:, 0:64], func=AF.Exp), reads=[pst[0]], writes=[cdb])
                op("pool", lambda e: e.tensor_scalar(out=Bm[:], in0=B_tok[:], scalar1=rowm[:, s:s + 1], scalar2=None, op0=ALU.mult),
                   reads=[B_tok, rowm], writes=[Bm])
                for g in range(8):
                    sl = slice(g * 512, (g + 1) * 512)
                    op("pe", lambda e: e.matmul(psg[1][:, 1024:1536], lhsT=Bm[:, g * 128:(g + 1) * 128], rhs=xs_dec[:, sl], start=True, stop=True),
                       reads=[Bm, xs_dec], writes=[PB[1][2]])
                    h3 = hT[:, sl].rearrange("p (r q) -> p r q", q=64)
                    op("dve", lambda e: e.tensor_tensor(out=h3, in0=h3, in1=cdb[:, g * 8:(g + 1) * 8].unsqueeze(2).to_broadcast([128, 8, 64]), op=ALU.mult),
                       reads=[hT, cdb], writes=[hT])
                    op("dve", lambda e: e.tensor_tensor(out=hT[:, sl], in0=hT[:, sl], in1=psg[1][:, 1024:1536], op=ALU.add), reads=[hT, PB[1][2]], writes=[hT])
                store_state(st_s.ap()[s], None)
        if need_y:
            finish_y(tok0, own_tile)

    def load_state_T(src):
        dma(hnat[:], src.rearrange("(pr h2) q n -> (h2 q) pr n", h2=2), writes=[hnat])
        for g0 in range(0, 32, 4):
            k = cnt["t"] % 2; cnt["t"] += 1
            for j in range(4):
                op("pe", lambda e: e.transpose(pst[k][:, j * 128:(j + 1) * 128], hnat[:, g0 + j, :], ident[:]), reads=[hnat, ident], writes=[pst[k]])
            op("act", lambda e: e.copy(out=hT[:, g0 * 128:(g0 + 4) * 128], in_=pst[k][:, :]), reads=[pst[k]], writes=[hT])
            op("dve", lambda e: e.tensor_copy(out=hTb[:, g0 * 128:(g0 + 4) * 128], in_=pst[k][:, :]), reads=[pst[k]], writes=[hTb])

    def store_state(dst, scale):
        for g0 in range(0, 32, 4):
            k = cnt["t"] % 2; cnt["t"] += 1
            for j in range(4):
                op("pe", lambda e: e.transpose(pst[k][:, j * 128:(j + 1) * 128], hT[:, (g0 + j) * 128:(g0 + j + 1) * 128], ident[:]),
                   reads=[hT, ident], writes=[pst[k]])
            o = hnat[:, g0:g0 + 4, :].rearrange("p a b -> p (a b)")
            if scale is None:
                op("act", lambda e: e.copy(out=o, in_=pst[k][:, :]), reads=[pst[k]], writes=[hnat])
            else:
                op("dve", lambda e: e.tensor_scalar(out=o, in0=pst[k][:, :], scalar1=scale, scalar2=None, op0=ALU.mult), reads=[pst[k], flg], writes=[hnat])
        dma(dst.rearrange("(pr h2) q n -> (h2 q) pr n", h2=2), hnat[:], reads=[hnat], writes=["st_out"])

    def finish_y(tok0, own_tile):
        y3 = ytk[:, :].rearrange("p (h q) -> p h q", q=64)
        o3 = yoff[:, :].rearrange("p (h q) -> p h q", q=64)
        op("dve", lambda e: e.tensor_tensor(out=o3, in0=o3, in1=EAC_.unsqueeze(2).to_broadcast([128, 64, 64]), op=ALU.mult), reads=[yoff, dtk], writes=[yoff])
        op("dve", lambda e: e.tensor_tensor(out=ytk[:], in0=ytk[:], in1=yoff[:], op=ALU.add), reads=[ytk, yoff], writes=[ytk])
        op("pool", lambda e: e.tensor_tensor(out=o3, in0=x_tok[:, :].rearrange("p (h q) -> p h q", q=64),
                                             in1=dbc[:, :].unsqueeze(2).to_broadcast([128, 64, 64]), op=ALU.mult), reads=[x_tok, dbc], writes=[yoff])
        op("dve", lambda e: e.tensor_tensor(out=ytk[:], in0=ytk[:], in1=yoff[:], op=ALU.add), reads=[ytk, yoff], writes=[ytk])
        dma(zc[:], projT.ap()[C_Z:C_Z + D, :].rearrange("(c p) t -> p c t", p=128)[:, :, tok0:tok0 + 128], reads=[projT], writes=[zc])
        tr_bf(lambda j: (zc[:, j, :], zc), 32, z_tok, 0)
        op("act", lambda e: e.activation(out=z_tok[:], in_=z_tok[:], func=AF.Silu), reads=[z_tok], writes=[z_tok])
        op("dve", lambda e: e.tensor_tensor(out=ytk[:], in0=ytk[:], in1=z_tok[:], op=ALU.mult), reads=[ytk, z_tok], writes=[ytk])
        op("dve", lambda e: e.memset(g8[:], 0.0), writes=[g8])
        for g in range(8):
            op("act", lambda e: e.activation(out=ybf[:, g * 512:(g + 1) * 512], in_=ytk[:, g * 512:(g + 1) * 512], func=AF.Square,
                                             accum_out=g8[:, g:g + 1]), reads=[ytk, g8], writes=[ybf, g8])
        op("dve", lambda e: e.tensor_scalar(out=g8[:, 0:8], in0=g8[:, 0:8], scalar1=1.0 / 512, scalar2=EPS, op0=ALU.mult, op1=ALU.add), reads=[g8], writes=[g8])
        op("act", lambda e: e.activation(out=g8[:, 0:8], in_=g8[:, 0:8], func=AF.Sqrt), reads=[g8], writes=[g8])
        op("dve", lambda e: e.reciprocal(out=g8[:, 8:16], in_=g8[:, 0:8]), reads=[g8], writes=[g8])
        op("dve", lambda e: e.tensor_tensor(out=ytk[:, :].rearrange("p (g q) -> p g q", q=512), in0=ytk[:, :].rearrange("p (g q) -> p g q", q=512),
                                            in1=g8[:, 8:16].unsqueeze(2).to_broadcast([128, 8, 512]), op=ALU.mult), reads=[ytk, g8], writes=[ytk])
        op("dve", lambda e: e.tensor_tensor(out=ybf[:], in0=ytk[:], in1=nbc[:], op=ALU.mult), reads=[ytk, nbc], writes=[ybf])
        tr_bf(lambda j: (ybf[:, j * 128:(j + 1) * 128], ybf), 32, yTs[:, :, :].rearrange("p a b -> p (a b)") if False else yTs_flat, 0)
        dma(yT.ap().rearrange("(c p) t -> p c t", p=128)[:, :, own_tile * 128:(own_tile + 1) * 128], yTs[:], reads=[yTs], writes=[yT])

    class _Flat:
        key = yTs.key

        def __getitem__(self, idx):
            return yTs[:, :, :].rearrange("p a b -> p (a b)")[idx]
    yTs_flat = T.__new__(T); yTs_flat.h = _Flat(); yTs_flat.key = yTs.key

    op("dve", lambda e: e.memset(hT[:], 0.0), writes=[hT])
    op("dve", lambda e: e.memset(hTb[:], 0.0), writes=[hTb])
    for ci in range(8):
        chunk(ci * 128, False, False, None)
    op("dve", lambda e: e.tensor_scalar(out=hT[:], in0=hT[:], scalar1=flg[:, 0:1], scalar2=None, op0=ALU.mult), reads=[hT, flg], writes=[hT])
    op("act", lambda e: e.copy(out=hTb[:], in_=hT[:]), reads=[hT], writes=[hTb])
    for ci in range(8):
        chunk(NPRE + ci * 128, False, True, ci)
    store_state(st_p.ap(), None)
    chunk(NPRE + 1024, True, True, 8)
    hl.release(m0)
    if stop_after <= 4.5:
        hl.finish(); return nc
    gt = hl.sb([128, NOWN], BF16, "gt"); gs = hl.sb([128, NOWN], F32, "gs"); mprev = hl.sb([128, NOWN], F32, "mprev")
    alloc_gemm(NOWN, norm=False)
    dma(B["actT"][:], yT.ap().rearrange("(c p) t -> p c t", p=128), reads=[yT], writes=[B["actT"]])
    gemm(B["actT"], 32, 0, NOWN, wview(I.ssd_out, 0), 32, gate_epi(1, False))
    hl.release(m0)
    if stop_after <= 4.9:
        hl.finish(); return nc
    SC = 512 ** -0.5
    oT_all = hl.sb([128, 16, NOWN], BF16, "oT_all")
    qc = hl.sb([128, 16, 128], BF16, "qc")
    sms = [hl.sb([128, 8], F32, "sm%d" % i) for i in range(4)]
    Pbs = [hl.sb([128, 4, 256], BF16, "Pb%d" % i) for i in range(4)]; PTs = [hl.sb([128, 2, 128], BF16, "PT%d" % i) for i in range(4)]
    o_tok = hl.sb([128, 2048], BF16, "o_tok")
    Knat = hl.sb([128, 2, 2048], F32, "Knat"); Kbf = hl.sb([128, 2, 2048], BF16, "Kbf"); kTs = hl.sb([128, 16, 256], BF16, "kTs")
    Vnat = hl.sb([128, 2, 2048], F32, "Vnat"); Vbf = hl.sb([128, 2, 2048], BF16, "Vbf")
    pb = [pst[i][:, :].bitcast(BF16) for i in range(2)]
    qv = projT.ap()[C_Q:C_Q + 2048, :].rearrange("(c p) t -> p c t", p=128)

    def attend(np_, qsl, kT_, v_, h):
        sm, Pb, PT = sms[h], Pbs[h], PTs[h]
        ps = psg[0]; psk = PB[0][h % 3]; pc0 = (h % 3) * 512
        for dc in range(4):
            op("pe", lambda e: e.matmul(ps[0:np_, pc0:pc0 + 256], lhsT=qc[:, h * 4 + dc, qsl], rhs=kT_[:, h * 4 + dc, :], start=(dc == 0), stop=(dc == 3)),
               reads=[qc, kT_], writes=[psk])
        op("dve", lambda e: e.reduce_max(out=sm[0:np_, 0:1], in_=ps[0:np_, pc0:pc0 + 256], axis=mybir.AxisListType.X), reads=[psk], writes=[sm])
        op("dve", lambda e: e.tensor_scalar(out=sm[0:np_, 1:2], in0=sm[0:np_, 0:1], scalar1=-SC, scalar2=None, op0=ALU.mult), reads=[sm], writes=[sm])
        op("dve", lambda e: e.memset(sm[0:np_, 2:3], 0.0), reads=[sm], writes=[sm])
        op("act", lambda e: e.activation(out=Pb[0:np_, h, :], in_=ps[0:np_, pc0:pc0 + 256], func=AF.Exp, scale=SC, bias=sm[0:np_, 1:2], accum_out=sm[0:np_, 2:3]),
           reads=[psk, sm], writes=[Pb, sm])
        op("dve", lambda e: e.reciprocal(out=sm[0:np_, 3:4], in_=sm[0:np_, 2:3]), reads=[sm], writes=[sm])
        k = cnt["t"] % 2; cnt["t"] += 1
        for mt in range(2):
            op("pe", lambda e: e.transpose(pb[k][:, mt * 128:mt * 128 + np_], Pb[0:np_, h, mt * 128:(mt + 1) * 128], identb[0:np_, 0:np_]),
               reads=[Pb, identb], writes=[pst[k]])
        op("act", lambda e: e.copy(out=PT[:, 0:2, 0:np_], in_=pb[k][:, 0:256].rearrange("p (a b) -> p a b", b=128)[:, :, 0:np_]), reads=[pst[k]], writes=[PT])
        po = psg[1]; pok = PB[1][h % 3]; oc0 = (h % 3) * 512
        for mt in range(2):
            op("pe", lambda e: e.matmul(po[0:np_, oc0:oc0 + 512], lhsT=PT[:, mt, 0:np_], rhs=v_[:, mt, h * 512:(h + 1) * 512], start=(mt == 0), stop=(mt == 1)),
               reads=[PT, v_], writes=[pok])
        op("dve", lambda e: e.tensor_scalar(out=o_tok[0:np_, h * 512:(h + 1) * 512], in0=po[0:np_, oc0:oc0 + 512], scalar1=sm[0:np_, 3:4], scalar2=None, op0=ALU.mult),
           reads=[pok, sm], writes=[o_tok])

    def o_to_fm(np_, col0):
        for g0 in (0, 8):
            k = cnt["t"] % 2; cnt["t"] += 1
            for j in range(8):
                c = g0 + j
                op("pe", lambda e: e.transpose(pb[k][:, j * 128:j * 128 + np_], o_tok[0:np_, c * 128:(c + 1) * 128], identb[0:np_, 0:np_]),
                   reads=[o_tok, identb], writes=[pst[k]])
            op("act", lambda e: e.copy(out=oT_all[:, g0:g0 + 8, col0:col0 + np_], in_=pb[k][:, :].rearrange("p (a b) -> p a b", b=128)[:, :, 0:np_]),
               reads=[pst[k]], writes=[oT_all])

    for ti in range(8):
        dma(qc[:], qv[:, :, NPRE + ti * 128:NPRE + (ti + 1) * 128], reads=[projT], writes=[qc])
        for h in range(4):
            attend(128, slice(0, 128), kT, vtok, h)
        o_to_fm(128, ti * 128)
    dma(qc[:], qv[:, :, NPRE + 1024:NALL], reads=[projT], writes=[qc])
    for s in range(16):
        dma(Knat[:], I.ck.ap()[s].rearrange("(a p) f -> p a f", p=128), writes=[Knat])
        dma(Vnat[:], I.cv.ap()[s].rearrange("(a p) f -> p a f", p=128), writes=[Vnat])
        op("pool", lambda e: e.tensor_copy(out=Kbf[:], in_=Knat[:]), reads=[Knat], writes=[Kbf])
        op("act", lambda e: e.copy(out=Vbf[:], in_=Vnat[:]), reads=[Vnat], writes=[Vbf])
        for mt in range(2):
            for g0 in (0, 8):
                k = cnt["t"] % 2; cnt["t"] += 1
                for j in range(8):
                    op("pe", lambda e: e.transpose(pb[k][:, j * 128:(j + 1) * 128], Kbf[:, mt, (g0 + j) * 128:(g0 + j + 1) * 128], identb[:]),
                       reads=[Kbf, identb], writes=[pst[k]])
                op("dve", lambda e: e.tensor_copy(out=kTs[:, g0:g0 + 8, mt * 128:(mt + 1) * 128], in_=pb[k][:, :].rearrange("p (a b) -> p a b", b=128)),
                   reads=[pst[k]], writes=[kTs])
        for h in range(4):
            attend(8, slice(s * 8, s * 8 + 8), kTs, Vbf, h)
        o_to_fm(8, 1024 + s * 8)
    gt = hl.sb([128, NOWN], BF16, "gt"); gs = hl.sb([128, NOWN], F32, "gs"); mprev = hl.sb([128, NOWN], F32, "mprev")
    alloc_gemm(0, norm=False)
    gemm(oT_all, 16, 0, NOWN, wview(I.xatt_out, 0), 32, gate_epi(2, False))
    hl.release(m0)
    if stop_after <= 5:
        hl.finish(); return nc

    x2 = hl.dram("x2", [NOWN, D], F32, kind="ExternalOutput" if dbg else "Internal")
    alloc_gemm(NOWN, norm=False)
    for c in range(32):
        s = B["stgf"][c % 2]
        dma(s[:], mixT[c * 128:(c + 1) * 128, :], reads=[mixT], writes=[s])
        op("act" if c % 2 else "dve", (lambda e: e.copy(out=B["actT"][:, c, :], in_=s[:])) if c % 2 else (lambda e: e.tensor_copy(out=B["actT"][:, c, :], in_=s[:])),
           reads=[s], writes=[B["actT"]])
    xres = hl.sb([128, 9, 128], F32, "xres"); xo9 = hl.sb([128, 9, 128], F32, "xo9")

    def res_epi(src_tok, dst_tok):
        def epi(b, ps, psv, nch, csz):
            s = B["stgf"][cnt["s"] % 2]; cnt["s"] += 1
            op("act", lambda e: e.copy(out=v3(s, nch, csz), in_=psv), reads=[ps], writes=[s])
            dma(xres[:], src_tok.ap().rearrange("(i p) f -> p i f", p=128)[:, :, b * 128:(b + 1) * 128], reads=[src_tok], writes=[xres])
            for g0 in (0, 4, 8):
                k = cnt["t"] % 2; cnt["t"] += 1
                m = min(4, 9 - g0)
                for j in range(m):
                    op("pe", lambda e: e.transpose(pst[k][:, j * 128:(j + 1) * 128], s[:, (g0 + j) * 128:(g0 + j + 1) * 128], ident[:]),
                       reads=[s, ident], writes=[pst[k]])
                op("dve", lambda e: e.tensor_tensor(out=xo9[:, g0:g0 + m, :], in0=pst[k][:, 0:m * 128].rearrange("p (a b) -> p a b", b=128),
                                                    in1=xres[:, g0:g0 + m, :], op=ALU.add), reads=[pst[k], xres], writes=[xo9])
            dma(dst_tok.ap().rearrange("(i p) f -> p i f", p=128)[:, :, b * 128:(b + 1) * 128], xo9[:], reads=[xo9], writes=[dst_tok])
        return epi
    gemm(B["actT"], 32, 0, NOWN, wview(I.w_o, 0), 32, res_epi(I.xa, x2))
    hl.release(m0)
    if stop_after <= 6:
        hl.finish(); return nc
    hl.release(mA)
    m0 = mA
    sS = hl.dram("sS", [NOWN, 2048], F32)
    HgT = hl.dram("HgT", [16384, NOWN], BF16)
    x3 = hl.dram("x3", [NOWN, D], F32, kind="ExternalOutput" if dbg else "Internal")
    actT = hl.sb([128, 32, NOWN], BF16, "xn2T")
    B["actT"] = actT
    m1 = hl.mark()
    alloc_gemm(0, norm=True, gemm_bufs=False)
    norm_fm(x2.ap(), 9, I.norm_ffn, actT, rd=[x2])
    hl.release(m1)
    alloc_gemm(0, norm=False)
    qT_all = hl.sb([128, 16, NOWN], BF16, "qT_all")

    def q_epi(b, ps, psv, nch, csz):
        op("act", lambda e: e.copy(out=v3(qT_all[:, b, :], nch, csz), in_=psv), reads=[ps], writes=[qT_all])
    gemm(actT, 32, 0, NOWN, wview(I.peer_wq, 0), 16, q_epi)
    skn = hl.sb([128, 16, 128], F32, "skn"); skb = hl.sb([128, 16, 128], BF16, "skb"); skT = hl.sb([128, 16, 128], BF16, "skT")
    pb = [pst[i][:, :].bitcast(BF16) for i in range(2)]
    dma(skn[:], I.peer_subkeys.ap().rearrange("b i d -> i b d"), writes=[skn])
    op("dve", lambda e: e.tensor_copy(out=skb[:], in_=skn[:]), reads=[skn], writes=[skb])
    for g0 in (0, 8):
        k = cnt["t"] % 2; cnt["t"] += 1
        for j in range(8):
            op("pe", lambda e: e.transpose(pb[k][:, j * 128:(j + 1) * 128], skb[:, g0 + j, :], identb[:]), reads=[skb, identb], writes=[pst[k]])
        op("act", lambda e: e.copy(out=skT[:, g0:g0 + 8, :], in_=pb[k][:, :].rearrange("p (a b) -> p a b", b=128)), reads=[pst[k]], writes=[skT])
    for ti in range(9):
        for g0 in range(0, 16, 4):
            k = cnt["t"] % 2; cnt["t"] += 1
            for j in range(4):
                b = g0 + j
                op("pe", lambda e: e.matmul(pst[k][:, j * 128:(j + 1) * 128], lhsT=qT_all[:, b, ti * 128:(ti + 1) * 128], rhs=skT[:, b, :], start=True, stop=True),
                   reads=[qT_all, skT], writes=[pst[k]])
            s = B["stgf"][cnt["s"] % 2]; cnt["s"] += 1
            op("act", lambda e: e.copy(out=s[:, 0:512], in_=pst[k][:, :]), reads=[pst[k]], writes=[s])
            dma(sS[ti * 128:(ti + 1) * 128, g0 * 128:(g0 + 4) * 128], s[:, 0:512], reads=[s], writes=[sS])
    hl.release(m1)
    if stop_after <= 7.1:
        hl.finish(); return nc

    s2_all = hl.sb([128, 9, 8, 128], F32, "s2_all"); r1_all = hl.sb([128, 9, 8, 128], F32, "r1_all")
    Dg = hl.sb([128, 9, 8, 128], BF16, "Dg")
    m2 = hl.mark()
    St = hl.sb([128, 16, 128], F32, "St"); scr = hl.sb([128, 256], F32, "scr")
    m16 = hl.sb([128, 16, 16], F32, "m16"); cand = hl.sb([128, 8, 256], F32, "cand"); c16 = hl.sb([128, 8, 16], F32, "c16")
    e16 = hl.sb([128, 8, 16], F32, "e16"); tk = hl.sb([128, 8, 4], F32, "tk")
    for ti in range(9):
        dma(St[:], sS[ti * 128:(ti + 1) * 128, :].rearrange("p (b i) -> p b i", i=128), reads=[sS], writes=[St])
        for b in range(16):
            op("dve", lambda e: e.max(out=m16[:, b, 0:8], in_=St[:, b, :]), reads=[St], writes=[m16])
            op("dve", lambda e: e.match_replace(out=scr[:, 0:128], in_to_replace=m16[:, b, 0:8], in_values=St[:, b, :], imm_value=-1e30),
               reads=[St, m16], writes=[scr])
            op("dve", lambda e: e.max(out=m16[:, b, 8:16], in_=scr[:, 0:128]), reads=[scr], writes=[m16])
        for h in range(8):
            op("dve", lambda e: e.tensor_tensor(out=cand[:, h, :].rearrange("p (a b) -> p a b", b=16),
                                                in0=m16[:, 2 * h, :].unsqueeze(2).to_broadcast([128, 16, 16]),
                                                in1=m16[:, 2 * h + 1, :].unsqueeze(1).to_broadcast([128, 16, 16]), op=ALU.add), reads=[m16], writes=[cand])
            op("dve", lambda e: e.max(out=c16[:, h, 0:8], in_=cand[:, h, :]), reads=[cand], writes=[c16])
            op("dve", lambda e: e.match_replace(out=scr[:], in_to_replace=c16[:, h, 0:8], in_values=cand[:, h, :], imm_value=-1e30),
               reads=[cand, c16], writes=[scr])
            op("dve", lambda e: e.max(out=c16[:, h, 8:16], in_=scr[:]), reads=[scr], writes=[c16])
        op("dve", lambda e: e.tensor_scalar(out=tk[:, :, 0:1], in0=c16[:, :, 15:16], scalar1=-1e-5, scalar2=None, op0=ALU.add), reads=[c16], writes=[tk])
        op("dve", lambda e: e.tensor_tensor(out=e16[:], in0=c16[:], in1=c16[:, :, 0:1].to_broadcast([128, 8, 16]), op=ALU.subtract), reads=[c16], writes=[e16])
        op("act", lambda e: e.activation(out=e16[:], in_=e16[:], func=AF.Exp), reads=[e16], writes=[e16])
        op("dve", lambda e: e.reduce_sum(out=tk[:, :, 1], in_=e16[:], axis=mybir.AxisListType.X), reads=[e16], writes=[tk])
        op("dve", lambda e: e.tensor_tensor(out=tk[:, :, 3:4], in0=tk[:, :, 0:1], in1=c16[:, :, 0:1], op=ALU.subtract), reads=[tk, c16], writes=[tk])
        op("act", lambda e: e.activation(out=tk[:, :, 3:4], in_=tk[:, :, 3:4], func=AF.Exp), reads=[tk], writes=[tk])
        op("dve", lambda e: e.reciprocal(out=tk[:, :, 1:2], in_=tk[:, :, 1:2]), reads=[tk], writes=[tk])
        op("dve", lambda e: e.tensor_tensor(out=tk[:, :, 2:3], in0=tk[:, :, 3:4], in1=tk[:, :, 1:2], op=ALU.mult), reads=[tk], writes=[tk])
        S4 = St[:, :, :].rearrange("p (h two) i -> p h two i", two=2)
        op("dve", lambda e: e.tensor_tensor(out=r1_all[:, ti], in0=S4[:, :, 0, :], in1=tk[:, :, 0:1].to_broadcast([128, 8, 128]), op=ALU.subtract),
           reads=[St, tk], writes=[r1_all])
        op("act", lambda e: e.copy(out=s2_all[:, ti], in_=S4[:, :, 1, :]), reads=[St], writes=[s2_all])
        for h in range(8):
            op("dve", lambda e: e.tensor_scalar(out=Dg[:, ti, h, :], in0=ident[:], scalar1=tk[:, h, 2:3], scalar2=None, op0=ALU.mult),
               reads=[ident, tk], writes=[Dg])
    hl.release(m2)
    if stop_after <= 7.2:
        hl.finish(); return nc

    Unat = [hl.sb([128, 1024], F32, "Unat%d" % i) for i in range(2)]; Ubf = hl.sb([128, D], BF16, "Ubf"); UT = hl.sb([128, 32, 128], BF16, "UT")
    hid = hl.sb([128, NOWN], BF16, "hid"); hgs = [hl.sb([128, NOWN], BF16, "hgs%d" % i) for i in range(1)]
    Ybs = [hl.sb([128, 8, 128], F32, "Yb%d" % i) for i in range(2)]; Ebs = [hl.sb([128, 8, 128], BF16, "Eb%d" % i) for i in range(2)]; Mb = [hl.sb([128, 8, 128], BF16, "Mb%d" % i) for i in range(2)]
    NEB = 128 if stop_after > 7.3 else 2
    def uload(i, qd):
        dma(Unat[qd % 2][:], I.peer_u.ap()[i * 128:(i + 1) * 128, qd * 1024:(qd + 1) * 1024], writes=[Unat[qd % 2]])
    uload(0, 0); uload(0, 1)
    for i in range(NEB):
        for qd in range(4):
            op("act", lambda e: e.copy(out=Ubf[:, qd * 1024:(qd + 1) * 1024], in_=Unat[qd % 2][:]), reads=[Unat[qd % 2]], writes=[Ubf])
            nq = i * 4 + qd + 2
            if nq < NEB * 4:
                uload(nq // 4, nq % 4)
        for g0 in range(0, 32, 8):
            k = cnt["t"] % 2; cnt["t"] += 1
            for j in range(8):
                op("pe", lambda e: e.transpose(pb[k][:, j * 128:(j + 1) * 128], Ubf[:, (g0 + j) * 128:(g0 + j + 1) * 128], identb[:]),
                   reads=[Ubf, identb], writes=[pst[k]])
            op("dve", lambda e: e.tensor_copy(out=UT[:, g0:g0 + 8, :], in_=pb[k][:, :].rearrange("p (a b) -> p a b", b=128)), reads=[pst[k]], writes=[UT])
        ps = psg[0]
        for c in range(3):
            for kc in range(32):
                op("pe", lambda e: e.matmul(ps[:, c * 512:c * 512 + 384], lhsT=UT[:, kc, :], rhs=actT[:, kc, c * 384:(c + 1) * 384],
                                            start=(kc == 0), stop=(kc == 31)), reads=[UT, actT], writes=[ps])
        op("act", lambda e: e.activation(out=v3(hid, 3, 384), in_=ps[:, :].rearrange("p (c n) -> p c n", n=512)[:, :, 0:384], func=AF.Gelu),
           reads=[ps], writes=[hid])
        pg = psg[1]
        for ti in range(9):
            Yb, Eb = Ybs[ti % 2], Ebs[ti % 2]
            op("pool", lambda e: e.tensor_tensor(out=Yb[:], in0=s2_all[:, ti], in1=r1_all[:, ti, :, i:i + 1].to_broadcast([128, 8, 128]), op=ALU.add),
               reads=[s2_all, r1_all], writes=[Yb])
            op("act", lambda e: e.activation(out=Eb[:], in_=Yb[:], func=AF.Exp), reads=[Yb], writes=[Eb])
            mb = Mb[ti % 2]
            op("dve", lambda e: e.scalar_tensor_tensor(out=mb[:], in0=Yb[:], scalar=0.0, in1=Eb[:], op0=ALU.is_ge, op1=ALU.mult), reads=[Yb, Eb], writes=[mb])
            col = (ti // 3) * 512 + (ti % 3) * 128
            for h in range(8):
                op("pe", lambda e: e.matmul(pg[:, col:col + 128], lhsT=mb[:, h, :], rhs=Dg[:, ti, h, :], start=(h == 0), stop=(h == 7)),
                   reads=[mb, Dg], writes=[pg])
        hg = hgs[0]
        op("dve", lambda e: e.tensor_tensor(out=v3(hg, 3, 384), in0=pg[:, :].rearrange("p (c n) -> p c n", n=512)[:, :, 0:384], in1=v3(hid, 3, 384), op=ALU.mult),
           reads=[pg, hid], writes=[hg])
        dma(HgT[i * 128:(i + 1) * 128, :], hg[:], reads=[hg], writes=[HgT])
    hl.release(mA)
    if stop_after <= 7.4:
        hl.finish(); return nc

    B["stgf"] = [hl.sb([128, NOWN], F32, "stgf%d" % i) for i in range(2)]
    xres = hl.sb([128, 9, 128], F32, "xres"); xo9 = hl.sb([128, 9, 128], F32, "xo9")
    vst = [hl.sb([128, 256], F32, "vst%d" % i) for i in range(3)]; vbf = [hl.sb([128, 256], BF16, "vbf%d" % i) for i in range(3)]
    hgl = [hl.sb([128, NOWN], BF16, "hgl%d" % i) for i in range(3)]
    NEC = 128 if stop_after > 7.3 else 2
    NRES = min(78, NEC)
    hgR = [hl.sb([128, NOWN], BF16, "hgR%d" % i) for i in range(NRES)]
    for ec in range(NRES):
        dma(hgR[ec][:], HgT[ec * 128:(ec + 1) * 128, :], reads=[HgT], writes=[hgR[ec]])
    for dp in range(16):
        for ec in range(NEC):
            vs, vb = vst[ec % 3], vbf[ec % 3]
            dma(vs[:], I.peer_v.ap()[ec * 128:(ec + 1) * 128, dp * 256:(dp + 1) * 256], writes=[vs])
            if ec < NRES:
                hgx = hgR[ec]
            else:
                hgx = hgl[ec % 3]
                dma(hgx[:], HgT[ec * 128:(ec + 1) * 128, :], reads=[HgT], writes=[hgx])
            op("pool", lambda e: e.tensor_copy(out=vb[:], in_=vs[:]), reads=[vs], writes=[vb])
            for half in range(2):
                ps = psg[half]
                for c in range(3):
                    op("pe", lambda e: e.matmul(ps[:, c * 512:c * 512 + 384], lhsT=vb[:, half * 128:(half + 1) * 128], rhs=hgx[:, c * 384:(c + 1) * 384],
                                                start=(ec == 0), stop=(ec == NEC - 1)), reads=[vb, hgx], writes=[ps])
        for half in range(2):
            ps = psg[half]
            res_epi(x2, x3)(dp * 2 + half, ps, ps[:, :].rearrange("p (c n) -> p c n", n=512)[:, :, 0:384], 3, 384)
    hl.release(mA)
    if stop_after <= 7.5:
        hl.finish(); return nc

    alloc_gemm(0, norm=True, gemm_bufs=False)
    load_gain(I.norm_final)
    for i in range(9):
        xb = B["xt"][i % 2]
        rms_tile(x3[i * 128:(i + 1) * 128, :], xb, rd=[x3])
        dma(y[i * 128:(i + 1) * 128, :], xb[:], reads=[xb], writes=[y])

    hl.finish()
    return nc


def const_masks():
    k = np.arange(128)[:, None]; i = np.arange(128)[None, :]
    same = (k // 8) == (i // 8)
    mk = np.stack([k <= i, k > i, np.ones((128, 128), bool), (k <= i) & same, (k > i) & same, same], axis=1).astype(np.float32)
    rowm = np.zeros((128, 32), np.float32)
    for s in range(16):
        rowm[s * 8:(s + 1) * 8, s] = 1.0
        rowm[s * 8, 16 + s] = 1.0
    return {"c_masks": np.ascontiguousarray(mk), "c_rowm": rowm}


def make_in_maps(inp):
    f = lambda a: np.ascontiguousarray(np.asarray(a, dtype=np.float32))
    maps = []
    shared = {k: f(inp[k][0]) for k in ["norm_mix", "norm_mem", "norm_ffn", "w_in", "a_conv_w", "a_out", "ssd_conv_w", "ssd_conv_b",
                                        "ssd_dt_bias", "ssd_a_log", "ssd_d", "ssd_norm", "ssd_out", "w_mem_k", "w_mem_v",
                                        "xatt_out", "w_o", "peer_wq", "peer_u", "peer_v"]}
    shared["norm_final"] = f(inp["norm_final"])
    shared["peer_subkeys"] = f(inp["peer_subkeys"][0]).reshape(16, 128, 128)
    shared["c_ident"] = np.eye(128, dtype=np.float32)
    shared.update(const_masks())
    xp = np.asarray(inp["x_prompt"]); xs = np.asarray(inp["x_sample"])
    for c in range(8):
        s, half = c // 2, c % 2
        m = dict(shared)
        m["xa"] = f(np.concatenate([xp[s, half * 1024:(half + 1) * 1024], xs[16 * c:16 * c + 16].reshape(128, D)], axis=0))
        m["xpre"] = f(xp[s, 0:1024]) if half == 1 else np.zeros((NPRE, D), np.float32)
        m["flag"] = np.full((128, 1), float(half), np.float32)
        m["mem"] = f(inp["mem_prompt"][s])
        m["ck"] = f(inp["cache_mem_k"][0, 16 * c:16 * c + 16]).reshape(16, 256, 2048)
        m["cv"] = f(inp["cache_mem_v"][0, 16 * c:16 * c + 16]).reshape(16, 256, 2048)
        m["sca"] = f(inp["state_conv_a"][0, 16 * c:16 * c + 16])
        m["ssc"] = f(inp["state_ssd_conv"][0, 16 * c:16 * c + 16])
        m["sst"] = f(inp["state_ssd"][0, 16 * c:16 * c + 16])
        maps.append(m)
    return maps


def assemble(res):
    r = res
    y_p = np.stack([np.concatenate([r[2 * s]["y"][:1024], r[2 * s + 1]["y"][:1024]], axis=0) for s in range(4)])
    y_s = np.concatenate([r[c]["y"][1024:].reshape(16, 8, D) for c in range(8)], axis=0)
    mk_ = np.stack([r[2 * s]["mk"].reshape(256, 4, 512) for s in range(4)])[None]
    mv_ = np.stack([r[2 * s]["mv"].reshape(256, 4, 512) for s in range(4)])[None]
    ca_p = np.stack([r[2 * s + 1]["ca_p"] for s in range(4)])[None]
    sc_p = np.stack([r[2 * s + 1]["sc_p"] for s in range(4)])[None]
    st_p = np.stack([r[2 * s + 1]["st_p"] for s in range(4)])[None]
    ca_s = np.concatenate([r[c]["ca_s"] for c in range(8)], axis=0)[None]
    sc_s = np.concatenate([r[c]["sc_s"] for c in range(8)], axis=0)[None]
    st_s = np.concatenate([r[c]["st_s"] for c in range(8)], axis=0)[None]
    return tuple(np.ascontiguousarray(a, dtype=np.float32) for a in (y_p, y_s, mk_, mv_, ca_p, sc_p, st_p, ca_s, sc_s, st_s))


def kernel(**inputs):
    nc = build()
    maps = make_in_maps(inputs)
    maps = [{k: v for k, v in m.items() if k in nc._used_inputs} for m in maps]
    res = run_bass_kernel_spmd(nc, maps, core_ids=list(range(8)))
    return assemble(res.results)
```

```python
import numpy as np
import concourse.bass as bass
import concourse.mybir as mybir
from concourse.bass_utils import run_bass_kernel_spmd

F32 = mybir.dt.float32
BF16 = mybir.dt.bfloat16
AF = mybir.ActivationFunctionType
ALU = mybir.AluOpType

D = 4096
NPRE = 1024
NOWN = 1152
NALL = NPRE + NOWN
INW = 30784
C_AIN, C_ABG, C_ACG, C_Z, C_XBC, C_DT, C_Q, C_G = 0, 2048, 4096, 6144, 10240, 16384, 16448, 18496
EPS = 1e-6


class T:
    _n = 0

    def __init__(self, h, key):
        self.h = h
        self.key = key

    def __getitem__(self, idx):
        return self.h[idx]

    def ap(self):
        return self.h.ap()


class HL:
    NDMA = 32

    def __init__(self, nc):
        self.nc = nc
        self.eng = {"pe": nc.tensor, "dve": nc.vector, "act": nc.scalar, "pool": nc.gpsimd, "sp": nc.sync}
        self.sem = {e: nc.alloc_semaphore("s_" + e) for e in ("pe", "dve", "act", "pool")}
        self.cnt = {e: 0 for e in self.sem}
        self.dsem = [nc.alloc_semaphore("d%d" % i) for i in range(self.NDMA)]
        self.dcnt = [0] * self.NDMA
        self.dnext = 0
        self.seen = {e: {} for e in self.eng}
        self.lastw = {}
        self.reads = {}
        self.nid = 0

    def sb(self, shape, dt=F32, name=None):
        self.nid += 1
        name = (name or "sb") + "_%d" % self.nid
        return T(self.nc.alloc_sbuf_tensor(name, list(shape), dt), name)

    def ps(self, shape, dt=F32, name=None):
        self.nid += 1
        name = name or ("ps%d" % self.nid)
        return T(self.nc.alloc_psum_tensor(name, list(shape), dt), name)

    def dram(self, name, shape, dt=F32, kind="Internal"):
        return T(self.nc.dram_tensor(name, list(shape), dt, kind=kind), name)

    def _wait(self, e, ticket):
        kind, ident, val = ticket
        k = (kind, ident)
        if self.seen[e].get(k, 0) >= val:
            return
        self.seen[e][k] = val
        if kind == "c":
            if ident == e and e == "pe":
                return
            self.eng[e].wait_ge(self.sem[ident], val)
        else:
            self.eng[e].wait_ge(self.dsem[ident], 16 * val)

    @staticmethod
    def _k(b):
        return b.key if isinstance(b, T) else b

    def _deps(self, e, reads, writes):
        for b in list(reads) + list(writes):
            t = self.lastw.get(self._k(b))
            if t is not None:
                self._wait(e, t)
        for b in writes:
            for t in self.reads.get(self._k(b), ()):
                self._wait(e, t)

    def _record(self, ticket, reads, writes):
        for b in reads:
            k = self._k(b)
            lst = self.reads.setdefault(k, [])
            lst.append(ticket)
            if len(lst) > 16:
                d = {}
                for t in lst:
                    kk = (t[0], t[1])
                    if kk not in d or d[kk][2] < t[2]:
                        d[kk] = t
                self.reads[k] = list(d.values())
        for b in writes:
            k = self._k(b)
            self.lastw[k] = ticket
            self.reads[k] = []

    def op(self, e, fn, reads=(), writes=()):
        px = [b for b in reads if isinstance(b, T) and b.key.startswith("ps") and e != "pe"]
        if px:
            writes = list(writes) + px
        self._deps(e, reads, writes)
        inst = fn(self.eng[e])
        inst.then_inc(self.sem[e], 1)
        self.cnt[e] += 1
        ticket = ("c", e, self.cnt[e])
        self._record(ticket, reads, writes)
        return ticket

    def dma(self, out, in_, reads=(), writes=(), q="sp", **kw):
        s = self.dnext
        self.dnext = (self.dnext + 1) % self.NDMA
        if self.dcnt[s] > 0:
            self._wait(q, ("d", s, self.dcnt[s]))
        self._deps(q, reads, writes)
        inst = self.eng[q].dma_start(out=out, in_=in_, **kw)
        inst.then_inc(self.dsem[s], 16)
        self.dcnt[s] += 1
        ticket = ("d", s, self.dcnt[s])
        self._record(ticket, reads, writes)
        return ticket

    def barrier(self):
        for e in self.eng:
            for s in range(self.NDMA):
                if self.dcnt[s]:
                    self._wait(e, ("d", s, self.dcnt[s]))
            for e2 in self.sem:
                if self.cnt[e2] and e2 != e:
                    self._wait(e, ("c", e2, self.cnt[e2]))

    def mark(self):
        return (self.nc.sbuf_base, self.nc.sbuf_top)

    def release(self, m):
        self.barrier()
        self.nc.sbuf_base, self.nc.sbuf_top = m

    def finish(self):
        for s in range(self.NDMA):
            if self.dcnt[s]:
                self._wait("sp", ("d", s, self.dcnt[s]))
        for e in self.sem:
            if self.cnt[e]:
                self._wait("sp", ("c", e, self.cnt[e]))


def build(stop_after=99, dbg=False):
    nc = bass.Bass("TRN2", target_bir_lowering=False)
    hl = HL(nc)
    op, dma = hl.op, hl.dma

    def din(name, shape, dt=F32):
        return hl.dram(name, shape, dt, kind="ExternalInput")

    def dout(name, shape, dt=F32):
        return hl.dram(name, shape, dt, kind="ExternalOutput")

    SHP = dict(xa=[NOWN, D], xpre=[NPRE, D], flag=[128, 1], mem=[256, D], ck=[16, 256, 2048], cv=[16, 256, 2048],
               sca=[16, 2, 2048], ssc=[16, 3, 6144], sst=[16, 64, 64, 128], norm_mix=[D], norm_mem=[D], norm_ffn=[D],
               norm_final=[D], w_in=[D, INW], a_conv_w=[3, 2048], a_out=[2048, D], ssd_conv_w=[4, 6144], ssd_conv_b=[6144],
               ssd_dt_bias=[64], ssd_a_log=[64], ssd_d=[64], ssd_norm=[D], ssd_out=[D, D], w_mem_k=[D, 2048],
               w_mem_v=[D, 2048], xatt_out=[2048, D], w_o=[D, D], peer_wq=[D, 2048], peer_subkeys=[16, 128, 128],
               peer_u=[16384, D], peer_v=[16384, D], c_ident=[128, 128], c_masks=[128, 6, 128], c_rowm=[128, 32])
    used = {}

    class _I:
        def __getattr__(self, name):
            if name not in used:
                used[name] = din(name, SHP[name])
            return used[name]
    I = _I()
    nc._used_inputs = used
    nc._hl = hl

    y = dout("y", [NOWN, D])
    mk = dout("mk", [256, 2048]); mv = dout("mv", [256, 2048])
    ca_p = dout("ca_p", [2, 2048]); sc_p = dout("sc_p", [3, 6144]); st_p = dout("st_p", [64, 64, 128])
    ca_s = dout("ca_s", [16, 2, 2048]); sc_s = dout("sc_s", [16, 3, 6144]); st_s = dout("st_s", [16, 64, 64, 128])

    projT = hl.dram("projT", [INW, NALL], BF16)
    dtT = hl.dram("dtT", [64, NALL], F32)

    ident = hl.sb([128, 128], F32, "ident")
    identb = hl.sb([128, 128], BF16, "identb")
    ss = hl.sb([128, 4], F32, "ss")
    mA = hl.mark()
    kT = hl.sb([128, 16, 256], BF16, "kT")
    vtok = hl.sb([128, 2, 2048], BF16, "vtok")
    psg = [hl.ps([128, 1536], F32, "psg%d" % i) for i in range(2)]
    pst = [hl.ps([128, 512], F32, "pst%d" % i) for i in range(2)]
    cnt = {"g": 0, "t": 0, "x": 0, "s": 0}
    B = {}

    def bank(i, j):
        t = T(psg[i].h, "psg%d_b%d" % (i, j))
        return t
    PB = [[bank(i, j) for j in range(3)] for i in range(2)]

    def alloc_gemm(ntok_act, kc_act=32, norm=True, gemm_bufs=True):
        if norm:
            B["gbc"] = hl.sb([128, D], F32, "gbc")
            B["xt"] = [hl.sb([128, D], F32, "xt%d" % i) for i in range(2)]
            B["junk"] = hl.sb([128, D], BF16, "junk")
        if ntok_act:
            B["actT"] = hl.sb([128, kc_act, ntok_act], BF16, "actT")
        if not gemm_bufs:
            return
        B["wst"] = [hl.sb([128, 32, 128], F32, "wst%d" % i) for i in range(2)]
        B["wbf"] = [hl.sb([128, 32, 128], BF16, "wbf%d" % i) for i in range(2)]
        B["stg"] = [hl.sb([128, NOWN], BF16, "stg%d" % i) for i in range(2)]
        B["stgf"] = [hl.sb([128, NOWN], F32, "stgf%d" % i) for i in range(2)]

    dma(ident[:], I.c_ident.ap(), writes=[ident])
    op("dve", lambda e: e.tensor_copy(out=identb[:], in_=ident[:]), reads=[ident], writes=[identb])

    def load_gain(g):
        dma(B["gbc"][:], g.ap().partition_broadcast(128), writes=[B["gbc"]])

    def rms_tile(src_ap, xb, rd=()):
        dma(xb[:], src_ap, reads=list(rd), writes=[xb])
        op("dve", lambda e: e.memset(ss[:, 0:1], 0.0), writes=[ss])
        op("act", lambda e: e.activation(out=B["junk"][:], in_=xb[:], func=AF.Square, accum_out=ss[:, 0:1]),
           reads=[xb, ss], writes=[B["junk"], ss])
        op("dve", lambda e: e.tensor_scalar(out=ss[:, 1:2], in0=ss[:, 0:1], scalar1=1.0 / D, scalar2=EPS,
                                            op0=ALU.mult, op1=ALU.add), reads=[ss], writes=[ss])
        op("act", lambda e: e.activation(out=ss[:, 3:4], in_=ss[:, 1:2], func=AF.Sqrt), reads=[ss], writes=[ss])
        op("dve", lambda e: e.reciprocal(out=ss[:, 2:3], in_=ss[:, 3:4]), reads=[ss], writes=[ss])
        op("dve", lambda e: e.scalar_tensor_tensor(out=xb[:], in0=xb[:], scalar=ss[:, 2:3], in1=B["gbc"][:],
                                                   op0=ALU.mult, op1=ALU.mult), reads=[xb, ss, B["gbc"]], writes=[xb])

    def to_fm(xb, dst, col0):
        for g4 in range(8):
            p = pst[cnt["t"] % 2]; cnt["t"] += 1
            for j in range(4):
                c = g4 * 4 + j
                op("pe", lambda e: e.transpose(p[:, j * 128:(j + 1) * 128], xb[:, c * 128:(c + 1) * 128], ident[:]),
                   reads=[xb, ident], writes=[p])
            eng = "act" if g4 % 2 == 0 else "dve"
            src = p[:, :].rearrange("p (j n) -> p j n", n=128)
            dsta = dst[:, g4 * 4:(g4 + 1) * 4, col0:col0 + 128]
            if eng == "act":
                op("act", lambda e: e.copy(out=dsta, in_=src), reads=[p], writes=[dst])
            else:
                op("dve", lambda e: e.tensor_copy(out=dsta, in_=src), reads=[p], writes=[dst])

    def norm_fm(src, nt, gain, dst, rd=()):
        load_gain(gain)
        for i in range(nt):
            xb = B["xt"][cnt["x"] % 2]; cnt["x"] += 1
            rms_tile(src[i * 128:(i + 1) * 128, :], xb, rd)
            to_fm(xb, dst, i * 128)

    def gemm(act, KC, t0, ntok, wfn, nblk, epi):
        nch = (ntok + 511) // 512
        csz = ntok // nch
        assert csz * nch == ntok

        def load(b):
            i = b % 2
            dma(B["wst"][i][:, 0:KC, :], wfn(b), writes=[B["wst"][i]])
            h2 = KC // 2
            op("dve", lambda e: e.tensor_copy(out=B["wbf"][i][:, 0:h2, :], in_=B["wst"][i][:, 0:h2, :]), reads=[B["wst"][i]], writes=[B["wbf"][i]])
            op("act", lambda e: e.copy(out=B["wbf"][i][:, h2:KC, :], in_=B["wst"][i][:, h2:KC, :]), reads=[B["wst"][i]], writes=[B["wbf"][i]])
        load(0)
        for b in range(nblk):
            if b + 1 < nblk:
                load(b + 1)
            ps = psg[cnt["g"] % 2]; cnt["g"] += 1
            w = B["wbf"][b % 2]
            for c in range(nch):
                for kc in range(KC):
                    op("pe", lambda e: e.matmul(ps[:, c * 512:c * 512 + csz], lhsT=w[:, kc, :],
                                                rhs=act[:, kc, t0 + c * csz:t0 + (c + 1) * csz],
                                                start=(kc == 0), stop=(kc == KC - 1)),
                       reads=[w, act], writes=[ps])
            psv = ps[:, 0:nch * 512].rearrange("p (c n) -> p c n", n=512)[:, :, 0:csz]
            epi(b, ps, psv, nch, csz)

    def wview(W, c0):
        v = W.ap().rearrange("(kc p) n -> p kc n", p=128)
        return lambda b: v[:, :, c0 + b * 128:c0 + (b + 1) * 128]

    def v3(t, nch, csz, n0=0):
        return t[:, n0:n0 + nch * csz].rearrange("p (c n) -> p c n", n=csz)

    m0 = hl.mark()
    alloc_gemm(256)
    if stop_after == 0.05:
        load_gain(I.norm_mem)
        dma(mk.ap()[0:128, :], B["gbc"][:, 0:2048], reads=[B["gbc"]], writes=[mk])
        hl.finish(); return nc
    if stop_after == 0.07:
        load_gain(I.norm_mem)
        rms_tile(I.mem.ap()[0:128, :], B["xt"][0])
        dma(mk.ap()[0:128, :], B["xt"][0][:, 0:2048], reads=[B["xt"][0]], writes=[mk])
        hl.finish(); return nc
    norm_fm(I.mem.ap(), 2, I.norm_mem, B["actT"])
    if stop_after == 0.1:
        o = B["stgf"][0]
        op("dve", lambda e: e.tensor_copy(out=o[:, 0:256], in_=B["actT"][:, 0, 0:256]), reads=[B["actT"]], writes=[o])
        dma(mk.ap()[0:128, 0:256], o[:, 0:256], reads=[o], writes=[mk])
        hl.finish(); return nc

    def kv_epi(dst_dram, keepT, keepTok):
        def epi(b, ps, psv, nch, csz):
            s = B["stgf"][cnt["s"] % 2]; cnt["s"] += 1
            op("act", lambda e: e.copy(out=s[:, 0:256], in_=ps[:, 0:256]), reads=[ps], writes=[s])
            if keepT is not None:
                op("dve", lambda e: e.tensor_copy(out=keepT[:, b, :], in_=ps[:, 0:256]), reads=[ps], writes=[keepT])
            p = pst[cnt["t"] % 2]; cnt["t"] += 1
            for i in range(2):
                op("pe", lambda e: e.transpose(p[:, i * 128:(i + 1) * 128], s[:, i * 128:(i + 1) * 128], ident[:]),
                   reads=[s, ident], writes=[p])
            o = B["stgf"][cnt["s"] % 2]; cnt["s"] += 1
            op("dve", lambda e: e.tensor_copy(out=o[:, 0:256], in_=p[:, 0:256]), reads=[p], writes=[o])
            if keepTok is not None:
                op("act", lambda e: e.copy(out=keepTok[:, :, b * 128:(b + 1) * 128],
                                           in_=p[:, 0:256].rearrange("p (i n) -> p i n", n=128)), reads=[p], writes=[keepTok])
            if stop_after == 0.3:
                dma(dst_dram.ap()[0:128, 0:256], o[:, 0:256], reads=[o], writes=[dst_dram])
            else:
                for i in range(2):
                    dma(dst_dram.ap()[i * 128:(i + 1) * 128, b * 128:(b + 1) * 128], o[:, i * 128:(i + 1) * 128], reads=[o], writes=[dst_dram])
        return epi
    if stop_after in (0.2, 0.25):
        def epi0(b, ps, psv, nch, csz):
            s = B["stgf"][cnt["s"] % 2]; cnt["s"] += 1
            op("act", lambda e: e.copy(out=s[:, 0:256], in_=ps[:, 0:256]), reads=[ps], writes=[s])
            dma(mk.ap()[0:128, (b % 8) * 256:(b % 8 + 1) * 256], s[:, 0:256], reads=[s], writes=[mk])
        gemm(B["actT"], 32, 0, 256, wview(I.w_mem_k, 0), 2 if stop_after == 0.2 else 16, epi0)
        hl.finish(); return nc
    gemm(B["actT"], 32, 0, 256, wview(I.w_mem_k, 0), 16, kv_epi(mk, kT, None))
    gemm(B["actT"], 32, 0, 256, wview(I.w_mem_v, 0), 16, kv_epi(mv, None, vtok))
    hl.release(m0)
    if stop_after <= 1:
        hl.finish(); return nc
    alloc_gemm(NOWN)

    def proj_epi(row0, tokbase):
        def epi(b, ps, psv, nch, csz):
            s = B["stg"][cnt["s"] % 2]; cnt["s"] += 1
            op("act", lambda e: e.copy(out=v3(s, nch, csz), in_=psv), reads=[ps], writes=[s])
            r = row0 + b * 128
            dma(projT[r:r + 128, tokbase:tokbase + nch * csz], s[:, 0:nch * csz], reads=[s], writes=[projT])
        return epi

    def dt_epi(tokbase):
        def epi(b, ps, psv, nch, csz):
            s = B["stgf"][cnt["s"] % 2]; cnt["s"] += 1
            op("act", lambda e: e.copy(out=v3(s, nch, csz)[0:64], in_=psv[0:64]), reads=[ps], writes=[s])
            dma(dtT[:, tokbase:tokbase + nch * csz], s[0:64, 0:nch * csz], reads=[s], writes=[dtT])
        return epi

    def wview_dt(b):
        v = I.w_in.ap().rearrange("(kc p) n -> p kc n", p=128)
        return v[:, :, C_DT:C_DT + 128]

    norm_fm(I.xpre.ap(), 8, I.norm_mix, B["actT"])
    gemm(B["actT"], 32, 0, NPRE, wview(I.w_in, C_XBC), 48, proj_epi(C_XBC, 0))
    gemm(B["actT"], 32, 0, NPRE, wview_dt, 1, dt_epi(0))
    gemm(B["actT"], 32, NPRE - 128, 128, wview(I.w_in, C_AIN), 16, proj_epi(C_AIN, NPRE - 128))
    gemm(B["actT"], 32, NPRE - 128, 128, wview(I.w_in, C_ACG), 16, proj_epi(C_ACG, NPRE - 128))
    norm_fm(I.xa.ap(), 9, I.norm_mix, B["actT"])
    gemm(B["actT"], 32, 0, NOWN, wview(I.w_in, 0), 128, proj_epi(0, NPRE))
    gemm(B["actT"], 32, 0, NOWN, wview_dt, 1, dt_epi(NPRE))
    gemm(B["actT"], 32, 0, NOWN, wview(I.w_in, C_Q), 112, proj_epi(C_Q, NPRE))

    hl.release(m0)
    acw = hl.sb([128, 16, 3], F32, "acw")
    for c in range(16):
        dma(acw[:, c, :], I.a_conv_w.ap()[:, c * 128:(c + 1) * 128].rearrange("k p -> p k"), writes=[acw], allow_slow_non_contiguous=True)
    scaN = hl.sb([32, 2048], F32, "scaN")
    dma(scaN[:], I.sca.ap().rearrange("s k f -> (s k) f"), writes=[scaN])
    caPS = hl.sb([128, 2, 2048], F32, "caPS")
    ain = hl.sb([128, 1280], BF16, "ain"); acg = hl.sb([128, 1280], BF16, "acg"); abg = hl.sb([128, NOWN], BF16, "abg")
    u = hl.sb([128, 1280], F32, "u"); uext = hl.sb([128, 16, 10], F32, "uext"); cvo = hl.sb([128, NOWN], F32, "cvo")
    vT = hl.sb([128, 16, NOWN], BF16, "vT")
    for c in range(16):
        dma(ain[:], projT[C_AIN + c * 128:C_AIN + (c + 1) * 128, 896:2176], reads=[projT], writes=[ain])
        dma(acg[:], projT[C_ACG + c * 128:C_ACG + (c + 1) * 128, 896:2176], reads=[projT], writes=[acg])
        dma(abg[:], projT[C_ABG + c * 128:C_ABG + (c + 1) * 128, 1024:2176], reads=[projT], writes=[abg])
        op("dve", lambda e: e.tensor_tensor(out=u[:], in0=ain[:], in1=acg[:], op=ALU.mult), reads=[ain, acg], writes=[u])
        op("dve", lambda e: e.tensor_scalar(out=cvo[:, 0:1024], in0=u[:, 126:1150], scalar1=acw[:, c, 0:1], scalar2=None, op0=ALU.mult),
           reads=[u, acw], writes=[cvo])
        for k in (1, 2):
            op("dve", lambda e: e.scalar_tensor_tensor(out=cvo[:, 0:1024], in0=u[:, 126 + k:1150 + k], scalar=acw[:, c, k:k + 1],
                                                       in1=cvo[:, 0:1024], op0=ALU.mult, op1=ALU.add), reads=[u, acw, cvo], writes=[cvo])
        kk = cnt["t"] % 2; cnt["t"] += 1
        op("pe", lambda e: e.transpose(pst[kk][:, 0:32], scaN[0:32, c * 128:(c + 1) * 128], ident[0:32, 0:32]), reads=[scaN, ident], writes=[pst[kk]])
        op("act", lambda e: e.copy(out=uext[:, :, 0:2], in_=pst[kk][:, 0:32].rearrange("p (s k) -> p s k", k=2)), reads=[pst[kk]], writes=[uext])
        for j, c0 in enumerate((1024, 1152)):
            op("pe", lambda e: e.transpose(pst[kk][:, 128 + j * 128:256 + j * 128], u[:, c0:c0 + 128], ident[:]), reads=[u, ident], writes=[pst[kk]])
        op("dve", lambda e: e.tensor_copy(out=caPS[:, :, c * 128:(c + 1) * 128], in_=pst[kk][:, 128:384].rearrange("p (a b) -> p a b", b=128)),
           reads=[pst[kk]], writes=[caPS])
        op("act", lambda e: e.copy(out=uext[:, :, 2:10], in_=u[:, 1152:1280].rearrange("p (s t) -> p s t", t=8)), reads=[u], writes=[uext])
        cs = cvo[:, 1024:1152].rearrange("p (s t) -> p s t", t=8)
        op("dve", lambda e: e.tensor_scalar(out=cs, in0=uext[:, :, 0:8], scalar1=acw[:, c, 0:1], scalar2=None, op0=ALU.mult),
           reads=[uext, acw], writes=[cvo])
        for k in (1, 2):
            op("dve", lambda e: e.scalar_tensor_tensor(out=cs, in0=uext[:, :, k:k + 8], scalar=acw[:, c, k:k + 1], in1=cs,
                                                       op0=ALU.mult, op1=ALU.add), reads=[uext, acw, cvo], writes=[cvo])
        op("dve", lambda e: e.tensor_tensor(out=vT[:, c, :], in0=cvo[:], in1=abg[:], op=ALU.mult), reads=[cvo, abg], writes=[vT])
    dma(ca_p.ap(), caPS[126:128, 0, :], reads=[caPS], writes=[ca_p])
    for sq in range(16):
        dma(ca_s.ap()[sq], caPS[sq * 8 + 6:sq * 8 + 8, 1, :], reads=[caPS], writes=[ca_s])
    if stop_after <= 3:
        hl.finish(); return nc
    mixT = hl.dram("mixT", [D, NOWN], F32, kind="ExternalOutput" if dbg else "Internal")
    gt = hl.sb([128, NOWN], BF16, "gt"); gs = hl.sb([128, NOWN], F32, "gs"); mprev = hl.sb([128, NOWN], F32, "mprev")

    def gate_epi(gi, first):
        def epi(b, ps, psv, nch, csz):
            r0 = C_G + gi * D + b * 128
            dma(gt[:], projT[r0:r0 + 128, NPRE:NALL], reads=[projT], writes=[gt])
            op("act", lambda e: e.activation(out=gs[:], in_=gt[:], func=AF.Sigmoid), reads=[gt], writes=[gs])
            if not first:
                dma(mprev[:], mixT[b * 128:(b + 1) * 128, :], reads=[mixT], writes=[mprev])
            s = B["stgf"][cnt["s"] % 2]; cnt["s"] += 1
            op("dve", lambda e: e.tensor_tensor(out=v3(s, nch, csz), in0=psv, in1=v3(gs, nch, csz), op=ALU.mult), reads=[ps, gs], writes=[s])
            if not first:
                op("dve", lambda e: e.tensor_tensor(out=s[:], in0=s[:], in1=mprev[:], op=ALU.add), reads=[s, mprev], writes=[s])
            dma(mixT[b * 128:(b + 1) * 128, :], s[:], reads=[s], writes=[mixT])
        return epi
    alloc_gemm(0, norm=False)
    gemm(vT, 16, 0, NOWN, wview(I.a_out, 0), 32, gate_epi(0, True))
    hl.release(m0)
    if stop_after <= 3.5:
        hl.finish(); return nc

    xcT = hl.dram("xcT", [6144, NALL], BF16)
    yT = hl.dram("yT", [D, NOWN], BF16, kind="ExternalOutput" if dbg else "Internal")
    cw = hl.sb([128, 48, 4], F32, "cw"); cb = hl.sb([128, 48], F32, "cb")
    for c in range(48):
        dma(cw[:, c, :], I.ssd_conv_w.ap()[:, c * 128:(c + 1) * 128].rearrange("k p -> p k"), writes=[cw], allow_slow_non_contiguous=True)
    dma(cb[:], I.ssd_conv_b.ap().rearrange("(c p) -> p c", p=128), writes=[cb], allow_slow_non_contiguous=True)
    xin = [hl.sb([128, NALL], BF16, "xin%d" % i) for i in range(2)]
    xe = hl.sb([128, 3 + 2048], F32, "xe"); xes = hl.sb([128, 16, 11], F32, "xes")
    cacc = hl.sb([128, NALL], F32, "cacc"); xco = [hl.sb([128, NALL], BF16, "xco%d" % i) for i in range(2)]
    sscN = hl.sb([48, 6144], F32, "sscN")
    dma(sscN[:], I.ssc.ap().rearrange("s k f -> (s k) f"), writes=[sscN])
    scPS = hl.sb([128, 2, 6144], F32, "scPS")
    pbb = [pst[i][:, :].bitcast(BF16) for i in range(2)]
    op("dve", lambda e: e.memset(xe[:, 0:3], 0.0), writes=[xe])
    for c in range(48):
        xi = xin[c % 2]; xo = xco[c % 2]
        dma(xi[:], projT[C_XBC + c * 128:C_XBC + (c + 1) * 128, :], reads=[projT], writes=[xi])
        op("act", lambda e: e.copy(out=xe[:, 3:3 + 2048], in_=xi[:, 0:2048]), reads=[xi], writes=[xe])
        kk = cnt["t"] % 2; cnt["t"] += 1
        op("pe", lambda e: e.transpose(pst[kk][:, 0:48], sscN[0:48, c * 128:(c + 1) * 128], ident[0:48, 0:48]), reads=[sscN, ident], writes=[pst[kk]])
        op("act", lambda e: e.copy(out=xes[:, :, 0:3], in_=pst[kk][:, 0:48].rearrange("p (s k) -> p s k", k=3)), reads=[pst[kk]], writes=[xes])
        for j, c0 in enumerate((1920, 2048)):
            op("pe", lambda e: e.transpose(pbb[kk][:, 256 + j * 128:384 + j * 128], xi[:, c0:c0 + 128], identb[:]), reads=[xi, identb], writes=[pst[kk]])
        op("dve", lambda e: e.tensor_copy(out=scPS[:, :, c * 128:(c + 1) * 128], in_=pbb[kk][:, 256:512].rearrange("p (a b) -> p a b", b=128)),
           reads=[pst[kk]], writes=[scPS])
        op("act", lambda e: e.copy(out=xes[:, :, 3:11], in_=xi[:, 2048:NALL].rearrange("p (s t) -> p s t", t=8)), reads=[xi], writes=[xes])
        cs = cacc[:, 2048:NALL].rearrange("p (s t) -> p s t", t=8)
        op("dve", lambda e: e.tensor_scalar(out=cacc[:, 0:2048], in0=xe[:, 0:2048], scalar1=cw[:, c, 0:1], scalar2=None, op0=ALU.mult),
           reads=[xe, cw], writes=[cacc])
        op("dve", lambda e: e.tensor_scalar(out=cs, in0=xes[:, :, 0:8], scalar1=cw[:, c, 0:1], scalar2=None, op0=ALU.mult),
           reads=[xes, cw], writes=[cacc])
        for k in (1, 2, 3):
            op("dve", lambda e: e.scalar_tensor_tensor(out=cacc[:, 0:2048], in0=xe[:, k:k + 2048], scalar=cw[:, c, k:k + 1],
                                                       in1=cacc[:, 0:2048], op0=ALU.mult, op1=ALU.add), reads=[xe, cw, cacc], writes=[cacc])
            op("dve", lambda e: e.scalar_tensor_tensor(out=cs, in0=xes[:, :, k:k + 8], scalar=cw[:, c, k:k + 1], in1=cs,
                                                       op0=ALU.mult, op1=ALU.add), reads=[xes, cw, cacc], writes=[cacc])
        op("act", lambda e: e.activation(out=xo[:], in_=cacc[:], func=AF.Silu, bias=cb[:, c:c + 1]), reads=[cacc, cb], writes=[xo])
        dma(xcT[c * 128:(c + 1) * 128, :], xo[:], reads=[xo], writes=[xcT])
    dma(sc_p.ap(), scPS[125:128, 0, :], reads=[scPS], writes=[sc_p])
    for sq in range(16):
        dma(sc_s.ap()[sq], scPS[sq * 8 + 5:sq * 8 + 8, 1, :], reads=[scPS], writes=[sc_s])
    hl.release(m0)
    if stop_after <= 4.1:
        hl.finish(); return nc

    MK = hl.sb([128, 6, 128], F32, "MK")
    dma(MK[:], I.c_masks.ap(), writes=[MK])
    rowm = hl.sb([128, 32], F32, "rowm")
    dma(rowm[:], I.c_rowm.ap(), writes=[rowm])
    hv = hl.sb([64, 4], F32, "hv")
    dma(hv[:, 0:1], I.ssd_dt_bias.ap().rearrange("(p o) -> p o", o=1), writes=[hv])
    dma(hv[:, 1:2], I.ssd_a_log.ap().rearrange("(p o) -> p o", o=1), writes=[hv])
    op("act", lambda e: e.activation(out=hv[:, 2:3], in_=hv[:, 1:2], func=AF.Exp), reads=[hv], writes=[hv])
    op("dve", lambda e: e.tensor_scalar(out=hv[:, 2:3], in0=hv[:, 2:3], scalar1=-1.0, scalar2=None, op0=ALU.mult), reads=[hv], writes=[hv])
    dtA = hl.sb([64, NALL], F32, "dtA"); adtA = hl.sb([64, NALL], F32, "adtA")
    dma(dtA[:], dtT.ap(), reads=[dtT], writes=[dtA])
    op("act", lambda e: e.activation(out=dtA[:], in_=dtA[:], func=AF.Exp, bias=hv[:, 0:1]), reads=[dtA, hv], writes=[dtA])
    op("act", lambda e: e.activation(out=dtA[:], in_=dtA[:], func=AF.Ln, bias=1.0), reads=[dtA], writes=[dtA])
    op("dve", lambda e: e.tensor_scalar(out=adtA[:], in0=dtA[:], scalar1=hv[:, 2:3], scalar2=None, op0=ALU.mult), reads=[dtA, hv], writes=[adtA])
    dbc = hl.sb([128, 64], F32, "dbc")
    dma(dbc[:], I.ssd_d.ap().partition_broadcast(128), writes=[dbc])
    nbc = hl.sb([128, D], F32, "nbc")
    dma(nbc[:], I.ssd_norm.ap().partition_broadcast(128), writes=[nbc])
    flg = hl.sb([128, 1], F32, "flg")
    dma(flg[:], I.flag.ap(), writes=[flg])

    xcc = hl.sb([128, 48, 128], BF16, "xcc")
    zc = hl.sb([128, 32, 128], BF16, "zc")
    x_tok = hl.sb([128, D], BF16, "x_tok"); xs_tok = hl.sb([128, D], BF16, "xs_tok"); xs_dec = hl.sb([128, D], BF16, "xs_dec")
    B_tok = hl.sb([128, 1024], BF16, "B_tok"); Bm = hl.sb([128, 1024], BF16, "Bm")
    z_tok = hl.sb([128, D], BF16, "z_tok")
    ytk = hl.sb([128, D], F32, "ytk"); ybf = hl.sb([128, D], BF16, "ybf")
    dtk = hl.sb([128, 64 * 6], F32, "dtk")
    DT_, ADT_, ACU_, TOT_, EAC_, EDS_ = [dtk[:, i * 64:(i + 1) * 64] for i in range(6)]
    cdb = hl.sb([128, 64], F32, "cdb")
    segrs = [hl.sb([128, 4, 128], F32, "segr%d" % i) for i in range(1)] * 2; LTs = [hl.sb([128, 4, 128], F32, "LT%d" % i) for i in range(1)] * 2; MT = [hl.sb([128, 4, 128], BF16, "MT%d" % i) for i in range(2)]
    CBms = [hl.sb([128, 128], F32, "CBm%d" % i) for i in range(2)]
    hT = hl.sb([128, D], F32, "hT"); hTb = hl.sb([128, D], BF16, "hTb")
    hnat = hl.sb([128, 32, 128], F32, "hnat")
    yoff = hl.sb([128, D], F32, "yoff")
    selm = hl.sb([128, 128], F32, "selm")
    g8 = hl.sb([128, 16], F32, "g8")
    yTs = hl.sb([128, 32, 128], BF16, "yTs")
    pb = [pst[i][:, :].bitcast(BF16) for i in range(2)]

    def tr_bf(src_fn, n, dst, dcol0, w=128):
        for g0 in range(0, n, 8):
            k = cnt["t"] % 2; cnt["t"] += 1
            m = min(8, n - g0)
            for j in range(m):
                ap_, rd = src_fn(g0 + j)
                op("pe", lambda e: e.transpose(pb[k][:, j * 128:(j + 1) * 128], ap_, identb[:]), reads=[rd, identb], writes=[pst[k]])
            eng = "act" if (g0 // 8) % 2 == 0 else "dve"
            o = dst[:, dcol0 + g0 * 128:dcol0 + (g0 + m) * 128]
            if eng == "act":
                op("act", lambda e: e.copy(out=o, in_=pb[k][:, 0:m * 128]), reads=[pst[k]], writes=[dst])
            else:
                op("dve", lambda e: e.tensor_copy(out=o, in_=pb[k][:, 0:m * 128]), reads=[pst[k]], writes=[dst])

    def small_mm(out_ps, lhsT, rhs, rds):
        op("pe", lambda e: e.matmul(out_ps, lhsT=lhsT, rhs=rhs, start=True, stop=True), reads=rds, writes=[pst[0]])

    def chunk(tok0, smp, need_y, own_tile):
        mo = 3 if smp else 0
        TRI, SL, ONES = MK[:, mo + 0, :], MK[:, mo + 1, :], MK[:, mo + 2, :]
        dma(xcc[:], xcT.ap().rearrange("(c p) t -> p c t", p=128)[:, :, tok0:tok0 + 128], reads=[xcT], writes=[xcc])
        for i, srcA in enumerate((dtA, adtA)):
            op("pe", lambda e: e.transpose(pst[0][:, i * 64:(i + 1) * 64], srcA[:, tok0:tok0 + 128], ident[0:64, 0:64]),
               reads=[srcA, ident], writes=[pst[0]])
        op("dve", lambda e: e.tensor_copy(out=dtk[:, 0:128], in_=pst[0][:, 0:128]), reads=[pst[0]], writes=[dtk])
        small_mm(pst[0][:, 0:64], TRI, ADT_, [MK, dtk])
        small_mm(pst[0][:, 64:128], ONES, ADT_, [MK, dtk])
        op("dve", lambda e: e.tensor_copy(out=dtk[:, 128:256], in_=pst[0][:, 0:128]), reads=[pst[0]], writes=[dtk])
        op("act", lambda e: e.activation(out=EAC_, in_=ACU_, func=AF.Exp), reads=[dtk], writes=[dtk])
        op("dve", lambda e: e.tensor_tensor(out=EDS_, in0=TOT_, in1=ACU_, op=ALU.subtract), reads=[dtk], writes=[dtk])
        op("act", lambda e: e.activation(out=EDS_, in_=EDS_, func=AF.Exp), reads=[dtk], writes=[dtk])
        op("act", lambda e: e.activation(out=cdb[:], in_=TOT_, func=AF.Exp), reads=[dtk], writes=[cdb])
        tr_bf(lambda j: (xcc[:, j, :], xcc), 32, x_tok, 0)
        tr_bf(lambda j: (xcc[:, 32 + j, :], xcc), 8, B_tok, 0)
        x3 = x_tok[:, :].rearrange("p (h q) -> p h q", q=64)
        op("dve", lambda e: e.tensor_tensor(out=xs_tok[:, :].rearrange("p (h q) -> p h q", q=64), in0=x3,
                                            in1=DT_.unsqueeze(2).to_broadcast([128, 64, 64]), op=ALU.mult), reads=[x_tok, dtk], writes=[xs_tok])
        op("pool", lambda e: e.tensor_tensor(out=xs_dec[:, :].rearrange("p (h q) -> p h q", q=64),
                                             in0=xs_tok[:, :].rearrange("p (h q) -> p h q", q=64),
                                             in1=EDS_.unsqueeze(2).to_broadcast([128, 64, 64]), op=ALU.mult), reads=[xs_tok, dtk], writes=[xs_dec])
        if need_y:
            if smp:
                op("dve", lambda e: e.memset(yoff[:], 0.0), writes=[yoff])
            for g in range(8):
                BT, CT = xcc[:, 32 + g, :], xcc[:, 40 + g, :]
                CBm = CBms[g % 2]
                op("pe", lambda e: e.matmul(pst[1][:, 0:128], lhsT=BT, rhs=CT, start=True, stop=True), reads=[xcc], writes=[pst[1]])
                op("dve", lambda e: e.tensor_tensor(out=CBm[:], in0=pst[1][:, 0:128], in1=TRI, op=ALU.mult), reads=[pst[1], MK], writes=[CBm])
                for hh in range(2):
                    h0 = g * 8 + hh * 4
                    segr, LT = segrs[hh], LTs[hh]
                    op("dve", lambda e: e.tensor_tensor(out=segr[:], in0=TRI.unsqueeze(1).to_broadcast([128, 4, 128]),
                                                        in1=dtk[:, 64 + h0:64 + h0 + 4].unsqueeze(2).to_broadcast([128, 4, 128]), op=ALU.mult),
                       reads=[MK, dtk], writes=[segr])
                    pseg = psg[0][:, hh * 512:(hh + 1) * 512]
                    op("pe", lambda e: e.matmul(pseg, lhsT=SL, rhs=segr[:, :, :].rearrange("p a b -> p (a b)"), start=True, stop=True),
                       reads=[MK, segr], writes=[PB[0][hh]])
                    op("act", lambda e: e.activation(out=LT[:, :, :].rearrange("p a b -> p (a b)"), in_=pseg, func=AF.Exp), reads=[PB[0][hh]], writes=[LT])
                    mt = MT[hh]
                    op("dve", lambda e: e.tensor_tensor(out=mt[:], in0=LT[:], in1=CBm[:, :].unsqueeze(1).to_broadcast([128, 4, 128]), op=ALU.mult),
                       reads=[LT, CBm], writes=[mt])
                    for r in range(4):
                        hd = h0 + r
                        op("pe", lambda e: e.matmul(psg[1][:, (hh * 4 + r) * 64:(hh * 4 + r + 1) * 64], lhsT=mt[:, r, :],
                                                    rhs=xs_tok[:, hd * 64:(hd + 1) * 64], start=True, stop=True), reads=[mt, xs_tok], writes=[PB[1][0]])
                if not smp:
                    op("pe", lambda e: e.matmul(psg[1][:, 512:1024], lhsT=CT, rhs=hTb[:, g * 512:(g + 1) * 512], start=True, stop=True),
                       reads=[xcc, hTb], writes=[PB[1][1]])
                    op("act", lambda e: e.copy(out=yoff[:, g * 512:(g + 1) * 512], in_=psg[1][:, 512:1024]), reads=[PB[1][1]], writes=[yoff])
                op("act", lambda e: e.copy(out=ytk[:, g * 512:(g + 1) * 512], in_=psg[1][:, 0:512]), reads=[PB[1][0]], writes=[ytk])
        if not smp:
            for g in range(8):
                sl = slice(g * 512, (g + 1) * 512)
                op("pe", lambda e: e.matmul(psg[1][:, 1024:1536], lhsT=B_tok[:, g * 128:(g + 1) * 128], rhs=xs_dec[:, sl], start=True, stop=True),
                   reads=[B_tok, xs_dec], writes=[PB[1][2]])
                h3 = hT[:, sl].rearrange("p (r q) -> p r q", q=64)
                op("dve", lambda e: e.tensor_tensor(out=h3, in0=h3, in1=cdb[:, g * 8:(g + 1) * 8].unsqueeze(2).to_broadcast([128, 8, 64]), op=ALU.mult),
                   reads=[hT, cdb], writes=[hT])
                op("dve", lambda e: e.tensor_tensor(out=hT[:, sl], in0=hT[:, sl], in1=psg[1][:, 1024:1536], op=ALU.add), reads=[hT, PB[1][2]], writes=[hT])
            op("act", lambda e: e.copy(out=hTb[:], in_=hT[:]), reads=[hT], writes=[hTb])
        else:
            for s in range(16):
                load_state_T(I.sst.ap()[s])
                for g in range(8):
                    op("pe", lambda e: e.matmul(psg[1][:, 512:1024], lhsT=xcc[:, 40 + g, :], rhs=hTb[:, g * 512:(g + 1) * 512], start=True, stop=True),
                       reads=[xcc, hTb], writes=[PB[1][1]])
                    ysl = yoff[:, g * 512:(g + 1) * 512]
                    op("dve", lambda e: e.scalar_tensor_tensor(out=ysl, in0=psg[1][:, 512:1024], scalar=rowm[:, s:s + 1], in1=ysl,
                                                               op0=ALU.mult, op1=ALU.add), reads=[PB[1][1], rowm, yoff], writes=[yoff])
                op("dve", lambda e: e.tensor_scalar(out=selm[:], in0=MK[:, 2, :], scalar1=rowm[:, 16 + s:17 + s], scalar2=None, op0=ALU.mult),
                   reads=[MK, rowm], writes=[selm])
                small_mm(pst[0][:, 0:64], selm[:], TOT_, [selm, dtk])
                op("act", lambda e: e.activation(out=cdb[:], in_=pst[0][:, 0:64], func=AF.Exp), reads=[pst[0]], writes=[cdb])
                op("pool", lambda e: e.tensor_scalar(out=Bm[:], in0=B_tok[:], scalar1=rowm[:, s:s + 1], scalar2=None, op0=ALU.mult),
                   reads=[B_tok, rowm], writes=[Bm])
                for g in range(8):
                    sl = slice(g * 512, (g + 1) * 512)
                    op("pe", lambda e: e.matmul(psg[1][:, 1024:1536], lhsT=Bm[:, g * 128:(g + 1) * 128], rhs=xs_dec[:, sl], start=True, stop=True),
                       reads=[Bm, xs_dec], writes=[PB[1][2]])
                    h3 = hT[:, sl].rearrange("p (r q) -> p r q", q=64)
                    op("dve", lambda e: e.tensor_tensor(out=h3, in0=h3, in1=cdb[:, g * 8:(g + 1) * 8].unsqueeze(2).to_broadcast([128, 8, 64]), op=ALU.mult),
                       reads=[hT, cdb], writes=[hT])
                    op("dve", lambda e: e.tensor_tensor(out=hT[:, sl], in0=hT[:, sl], in1=psg[1][:, 1024:1536], op=ALU.add), reads=[hT, PB[1][2]], writes=[hT])
                store_state(st_s.ap()[s], None)
        if need_y:
            finish_y(tok0, own_tile)

    def load_state_T(src):
        dma(hnat[:], src.rearrange("(pr h2) q n -> (h2 q) pr n", h2=2), writes=[hnat])
        for g0 in range(0, 32, 4):
            k = cnt["t"] % 2; cnt["t"] += 1
            for j in range(4):
                op("pe", lambda e: e.transpose(pst[k][:, j * 128:(j + 1) * 128], hnat[:, g0 + j, :], ident[:]), reads=[hnat, ident], writes=[pst[k]])
            op("act", lambda e: e.copy(out=hT[:, g0 * 128:(g0 + 4) * 128], in_=pst[k][:, :]), reads=[pst[k]], writes=[hT])
            op("dve", lambda e: e.tensor_copy(out=hTb[:, g0 * 128:(g0 + 4) * 128], in_=pst[k][:, :]), reads=[pst[k]], writes=[hTb])

    def store_state(dst, scale):
        for g0 in range(0, 32, 4):
            k = cnt["t"] % 2; cnt["t"] += 1
            for j in range(4):
                op("pe", lambda e: e.transpose(pst[k][:, j * 128:(j + 1) * 128], hT[:, (g0 + j) * 128:(g0 + j + 1) * 128], ident[:]),
                   reads=[hT, ident], writes=[pst[k]])
            o = hnat[:, g0:g0 + 4, :].rearrange("p a b -> p (a b)")
            if scale is None:
                op("act", lambda e: e.copy(out=o, in_=pst[k][:, :]), reads=[pst[k]], writes=[hnat])
            else:
                op("dve", lambda e: e.tensor_scalar(out=o, in0=pst[k][:, :], scalar1=scale, scalar2=None, op0=ALU.mult), reads=[pst[k], flg], writes=[hnat])
        dma(dst.rearrange("(pr h2) q n -> (h2 q) pr n", h2=2), hnat[:], reads=[hnat], writes=["st_out"])

    def finish_y(tok0, own_tile):
        y3 = ytk[:, :].rearrange("p (h q) -> p h q", q=64)
        o3 = yoff[:, :].rearrange("p (h q) -> p h q", q=64)
        op("dve", lambda e: e.tensor_tensor(out=o3, in0=o3, in1=EAC_.unsqueeze(2).to_broadcast([128, 64, 64]), op=ALU.mult), reads=[yoff, dtk], writes=[yoff])
        op("dve", lambda e: e.tensor_tensor(out=ytk[:], in0=ytk[:], in1=yoff[:], op=ALU.add), reads=[ytk, yoff], writes=[ytk])
        op("dve", lambda e: e.tensor_tensor(out=o3, in0=x_tok[:, :].rearrange("p (h q) -> p h q", q=64),
                                             in1=dbc[:, :].unsqueeze(2).to_broadcast([128, 64, 64]), op=ALU.mult), reads=[x_tok, dbc], writes=[yoff])
        op("dve", lambda e: e.tensor_tensor(out=ytk[:], in0=ytk[:], in1=yoff[:], op=ALU.add), reads=[ytk, yoff], writes=[ytk])
        dma(zc[:], projT.ap()[C_Z:C_Z + D, :].rearrange("(c p) t -> p c t", p=128)[:, :, tok0:tok0 + 128], reads=[projT], writes=[zc])
        tr_bf(lambda j: (zc[:, j, :], zc), 32, z_tok, 0)
        op("act", lambda e: e.activation(out=z_tok[:], in_=z_tok[:], func=AF.Silu), reads=[z_tok], writes=[z_tok])
        op("dve", lambda e: e.tensor_tensor(out=ytk[:], in0=ytk[:], in1=z_tok[:], op=ALU.mult), reads=[ytk, z_tok], writes=[ytk])
        op("dve", lambda e: e.memset(g8[:], 0.0), writes=[g8])
        for g in range(8):
            op("act", lambda e: e.activation(out=ybf[:, g * 512:(g + 1) * 512], in_=ytk[:, g * 512:(g + 1) * 512], func=AF.Square,
                                             accum_out=g8[:, g:g + 1]), reads=[ytk, g8], writes=[ybf, g8])
        op("dve", lambda e: e.tensor_scalar(out=g8[:, 0:8], in0=g8[:, 0:8], scalar1=1.0 / 512, scalar2=EPS, op0=ALU.mult, op1=ALU.add), reads=[g8], writes=[g8])
        op("act", lambda e: e.activation(out=g8[:, 0:8], in_=g8[:, 0:8], func=AF.Sqrt), reads=[g8], writes=[g8])
        op("dve", lambda e: e.reciprocal(out=g8[:, 8:16], in_=g8[:, 0:8]), reads=[g8], writes=[g8])
        op("dve", lambda e: e.tensor_tensor(out=ytk[:, :].rearrange("p (g q) -> p g q", q=512), in0=ytk[:, :].rearrange("p (g q) -> p g q", q=512),
                                            in1=g8[:, 8:16].unsqueeze(2).to_broadcast([128, 8, 512]), op=ALU.mult), reads=[ytk, g8], writes=[ytk])
        op("dve", lambda e: e.tensor_tensor(out=ybf[:], in0=ytk[:], in1=nbc[:], op=ALU.mult), reads=[ytk, nbc], writes=[ybf])
        tr_bf(lambda j: (ybf[:, j * 128:(j + 1) * 128], ybf), 32, yTs[:, :, :].rearrange("p a b -> p (a b)") if False else yTs_flat, 0)
        dma(yT.ap().rearrange("(c p) t -> p c t", p=128)[:, :, own_tile * 128:(own_tile + 1) * 128], yTs[:], reads=[yTs], writes=[yT])

    class _Flat:
        key = yTs.key

        def __getitem__(self, idx):
            return yTs[:, :, :].rearrange("p a b -> p (a b)")[idx]
    yTs_flat = T.__new__(T); yTs_flat.h = _Flat(); yTs_flat.key = yTs.key

    op("dve", lambda e: e.memset(hT[:], 0.0), writes=[hT])
    op("dve", lambda e: e.memset(hTb[:], 0.0), writes=[hTb])
    for ci in range(8):
        chunk(ci * 128, False, False, None)
    op("dve", lambda e: e.tensor_scalar(out=hT[:], in0=hT[:], scalar1=flg[:, 0:1], scalar2=None, op0=ALU.mult), reads=[hT, flg], writes=[hT])
    op("act", lambda e: e.copy(out=hTb[:], in_=hT[:]), reads=[hT], writes=[hTb])
    for ci in range(8):
        chunk(NPRE + ci * 128, False, True, ci)
    store_state(st_p.ap(), None)
    chunk(NPRE + 1024, True, True, 8)
    hl.release(m0)
    if stop_after <= 4.5:
        hl.finish(); return nc
    gt = hl.sb([128, NOWN], BF16, "gt"); gs = hl.sb([128, NOWN], F32, "gs"); mprev = hl.sb([128, NOWN], F32, "mprev")
    alloc_gemm(NOWN, norm=False)
    dma(B["actT"][:], yT.ap().rearrange("(c p) t -> p c t", p=128), reads=[yT], writes=[B["actT"]])
    gemm(B["actT"], 32, 0, NOWN, wview(I.ssd_out, 0), 32, gate_epi(1, False))
    hl.release(m0)
    if stop_after <= 4.9:
        hl.finish(); return nc
    SC = 512 ** -0.5
    oT_all = hl.sb([128, 16, NOWN], BF16, "oT_all")
    qc = hl.sb([128, 16, 128], BF16, "qc")
    sms = [hl.sb([128, 8], F32, "sm%d" % i) for i in range(4)]
    Pbs = [hl.sb([128, 4, 256], BF16, "Pb%d" % i) for i in range(4)]; PTs = [hl.sb([128, 2, 128], BF16, "PT%d" % i) for i in range(4)]
    o_tok = hl.sb([128, 2048], BF16, "o_tok")
    Knat = hl.sb([128, 2, 2048], F32, "Knat"); Kbf = hl.sb([128, 2, 2048], BF16, "Kbf"); kTs = hl.sb([128, 16, 256], BF16, "kTs")
    Vnat = hl.sb([128, 2, 2048], F32, "Vnat"); Vbf = hl.sb([128, 2, 2048], BF16, "Vbf")
    pb = [pst[i][:, :].bitcast(BF16) for i in range(2)]
    qv = projT.ap()[C_Q:C_Q + 2048, :].rearrange("(c p) t -> p c t", p=128)

    def attend(np_, qsl, kT_, v_, h):
        sm, Pb, PT = sms[h], Pbs[h], PTs[h]
        ps = psg[0]; psk = PB[0][h % 3]; pc0 = (h % 3) * 512
        for dc in range(4):
            op("pe", lambda e: e.matmul(ps[0:np_, pc0:pc0 + 256], lhsT=qc[:, h * 4 + dc, qsl], rhs=kT_[:, h * 4 + dc, :], start=(dc == 0), stop=(dc == 3)),
               reads=[qc, kT_], writes=[psk])
        op("dve", lambda e: e.reduce_max(out=sm[0:np_, 0:1], in_=ps[0:np_, pc0:pc0 + 256], axis=mybir.AxisListType.X), reads=[psk], writes=[sm])
        op("dve", lambda e: e.tensor_scalar(out=sm[0:np_, 1:2], in0=sm[0:np_, 0:1], scalar1=-SC, scalar2=None, op0=ALU.mult), reads=[sm], writes=[sm])
        op("dve", lambda e: e.memset(sm[0:np_, 2:3], 0.0), reads=[sm], writes=[sm])
        op("act", lambda e: e.activation(out=Pb[0:np_, h, :], in_=ps[0:np_, pc0:pc0 + 256], func=AF.Exp, scale=SC, bias=sm[0:np_, 1:2], accum_out=sm[0:np_, 2:3]),
           reads=[psk, sm], writes=[Pb, sm])
        op("dve", lambda e: e.reciprocal(out=sm[0:np_, 3:4], in_=sm[0:np_, 2:3]), reads=[sm], writes=[sm])
        k = cnt["t"] % 2; cnt["t"] += 1
        for mt in range(2):
            op("pe", lambda e: e.transpose(pb[k][:, mt * 128:mt * 128 + np_], Pb[0:np_, h, mt * 128:(mt + 1) * 128], identb[0:np_, 0:np_]),
               reads=[Pb, identb], writes=[pst[k]])
        op("act", lambda e: e.copy(out=PT[:, 0:2, 0:np_], in_=pb[k][:, 0:256].rearrange("p (a b) -> p a b", b=128)[:, :, 0:np_]), reads=[pst[k]], writes=[PT])
        po = psg[1]; pok = PB[1][h % 3]; oc0 = (h % 3) * 512
        for mt in range(2):
            op("pe", lambda e: e.matmul(po[0:np_, oc0:oc0 + 512], lhsT=PT[:, mt, 0:np_], rhs=v_[:, mt, h * 512:(h + 1) * 512], start=(mt == 0), stop=(mt == 1)),
               reads=[PT, v_], writes=[pok])
        op("dve", lambda e: e.tensor_scalar(out=o_tok[0:np_, h * 512:(h + 1) * 512], in0=po[0:np_, oc0:oc0 + 512], scalar1=sm[0:np_, 3:4], scalar2=None, op0=ALU.mult),
           reads=[pok, sm], writes=[o_tok])

    def o_to_fm(np_, col0):
        for g0 in (0, 8):
            k = cnt["t"] % 2; cnt["t"] += 1
            for j in range(8):
                c = g0 + j
                op("pe", lambda e: e.transpose(pb[k][:, j * 128:j * 128 + np_], o_tok[0:np_, c * 128:(c + 1) * 128], identb[0:np_, 0:np_]),
                   reads=[o_tok, identb], writes=[pst[k]])
            op("act", lambda e: e.copy(out=oT_all[:, g0:g0 + 8, col0:col0 + np_], in_=pb[k][:, :].rearrange("p (a b) -> p a b", b=128)[:, :, 0:np_]),
               reads=[pst[k]], writes=[oT_all])

    for ti in range(8):
        dma(qc[:], qv[:, :, NPRE + ti * 128:NPRE + (ti + 1) * 128], reads=[projT], writes=[qc])
        for h in range(4):
            attend(128, slice(0, 128), kT, vtok, h)
        o_to_fm(128, ti * 128)
    dma(qc[:], qv[:, :, NPRE + 1024:NALL], reads=[projT], writes=[qc])
    for s in range(16):
        dma(Knat[:], I.ck.ap()[s].rearrange("(a p) f -> p a f", p=128), writes=[Knat])
        dma(Vnat[:], I.cv.ap()[s].rearrange("(a p) f -> p a f", p=128), writes=[Vnat])
        op("dve", lambda e: e.tensor_copy(out=Kbf[:], in_=Knat[:]), reads=[Knat], writes=[Kbf])
        op("act", lambda e: e.copy(out=Vbf[:], in_=Vnat[:]), reads=[Vnat], writes=[Vbf])
        for mt in range(2):
            for g0 in (0, 8):
                k = cnt["t"] % 2; cnt["t"] += 1
                for j in range(8):
                    op("pe", lambda e: e.transpose(pb[k][:, j * 128:(j + 1) * 128], Kbf[:, mt, (g0 + j) * 128:(g0 + j + 1) * 128], identb[:]),
                       reads=[Kbf, identb], writes=[pst[k]])
                op("dve", lambda e: e.tensor_copy(out=kTs[:, g0:g0 + 8, mt * 128:(mt + 1) * 128], in_=pb[k][:, :].rearrange("p (a b) -> p a b", b=128)),
                   reads=[pst[k]], writes=[kTs])
        for h in range(4):
            attend(8, slice(s * 8, s * 8 + 8), kTs, Vbf, h)
        o_to_fm(8, 1024 + s * 8)
    gt = hl.sb([128, NOWN], BF16, "gt"); gs = hl.sb([128, NOWN], F32, "gs"); mprev = hl.sb([128, NOWN], F32, "mprev")
    alloc_gemm(0, norm=False)
    gemm(oT_all, 16, 0, NOWN, wview(I.xatt_out, 0), 32, gate_epi(2, False))
    hl.release(m0)
    if stop_after <= 5:
        hl.finish(); return nc

    x2 = hl.dram("x2", [NOWN, D], F32, kind="ExternalOutput" if dbg else "Internal")
    alloc_gemm(NOWN, norm=False)
    for c in range(32):
        s = B["stgf"][c % 2]
        dma(s[:], mixT[c * 128:(c + 1) * 128, :], reads=[mixT], writes=[s])
        op("act" if c % 2 else "dve", (lambda e: e.copy(out=B["actT"][:, c, :], in_=s[:])) if c % 2 else (lambda e: e.tensor_copy(out=B["actT"][:, c, :], in_=s[:])),
           reads=[s], writes=[B["actT"]])
    xres = hl.sb([128, 9, 128], F32, "xres"); xo9 = hl.sb([128, 9, 128], F32, "xo9")

    def res_epi(src_tok, dst_tok):
        def epi(b, ps, psv, nch, csz):
            s = B["stgf"][cnt["s"] % 2]; cnt["s"] += 1
            op("act", lambda e: e.copy(out=v3(s, nch, csz), in_=psv), reads=[ps], writes=[s])
            dma(xres[:], src_tok.ap().rearrange("(i p) f -> p i f", p=128)[:, :, b * 128:(b + 1) * 128], reads=[src_tok], writes=[xres])
            for g0 in (0, 4, 8):
                k = cnt["t"] % 2; cnt["t"] += 1
                m = min(4, 9 - g0)
                for j in range(m):
                    op("pe", lambda e: e.transpose(pst[k][:, j * 128:(j + 1) * 128], s[:, (g0 + j) * 128:(g0 + j + 1) * 128], ident[:]),
                       reads=[s, ident], writes=[pst[k]])
                op("dve", lambda e: e.tensor_tensor(out=xo9[:, g0:g0 + m, :], in0=pst[k][:, 0:m * 128].rearrange("p (a b) -> p a b", b=128),
                                                    in1=xres[:, g0:g0 + m, :], op=ALU.add), reads=[pst[k], xres], writes=[xo9])
            dma(dst_tok.ap().rearrange("(i p) f -> p i f", p=128)[:, :, b * 128:(b + 1) * 128], xo9[:], reads=[xo9], writes=[dst_tok])
        return epi
    gemm(B["actT"], 32, 0, NOWN, wview(I.w_o, 0), 32, res_epi(I.xa, x2))
    hl.release(m0)
    if stop_after <= 6:
        hl.finish(); return nc
    hl.release(mA)
    m0 = mA
    sS = hl.dram("sS", [NOWN, 2048], F32)
    HgT = hl.dram("HgT", [16384, NOWN], BF16)
    x3 = hl.dram("x3", [NOWN, D], F32, kind="ExternalOutput" if dbg else "Internal")
    actT = hl.sb([128, 32, NOWN], BF16, "xn2T")
    B["actT"] = actT
    m1 = hl.mark()
    alloc_gemm(0, norm=True, gemm_bufs=False)
    norm_fm(x2.ap(), 9, I.norm_ffn, actT, rd=[x2])
    hl.release(m1)
    alloc_gemm(0, norm=False)
    qT_all = hl.sb([128, 16, NOWN], BF16, "qT_all")

    def q_epi(b, ps, psv, nch, csz):
        op("act", lambda e: e.copy(out=v3(qT_all[:, b, :], nch, csz), in_=psv), reads=[ps], writes=[qT_all])
    gemm(actT, 32, 0, NOWN, wview(I.peer_wq, 0), 16, q_epi)
    skn = hl.sb([128, 16, 128], F32, "skn"); skb = hl.sb([128, 16, 128], BF16, "skb"); skT = hl.sb([128, 16, 128], BF16, "skT")
    pb = [pst[i][:, :].bitcast(BF16) for i in range(2)]
    dma(skn[:], I.peer_subkeys.ap().rearrange("b i d -> i b d"), writes=[skn])
    op("dve", lambda e: e.tensor_copy(out=skb[:], in_=skn[:]), reads=[skn], writes=[skb])
    for g0 in (0, 8):
        k = cnt["t"] % 2; cnt["t"] += 1
        for j in range(8):
            op("pe", lambda e: e.transpose(pb[k][:, j * 128:(j + 1) * 128], skb[:, g0 + j, :], identb[:]), reads=[skb, identb], writes=[pst[k]])
        op("act", lambda e: e.copy(out=skT[:, g0:g0 + 8, :], in_=pb[k][:, :].rearrange("p (a b) -> p a b", b=128)), reads=[pst[k]], writes=[skT])
    for ti in range(9):
        for g0 in range(0, 16, 4):
            k = cnt["t"] % 2; cnt["t"] += 1
            for j in range(4):
                b = g0 + j
                op("pe", lambda e: e.matmul(pst[k][:, j * 128:(j + 1) * 128], lhsT=qT_all[:, b, ti * 128:(ti + 1) * 128], rhs=skT[:, b, :], start=True, stop=True),
                   reads=[qT_all, skT], writes=[pst[k]])
            s = B["stgf"][cnt["s"] % 2]; cnt["s"] += 1
            op("act", lambda e: e.copy(out=s[:, 0:512], in_=pst[k][:, :]), reads=[pst[k]], writes=[s])
            dma(sS[ti * 128:(ti + 1) * 128, g0 * 128:(g0 + 4) * 128], s[:, 0:512], reads=[s], writes=[sS])
    hl.release(m1)
    if stop_after <= 7.1:
        hl.finish(); return nc

    s2_all = hl.sb([128, 9, 8, 128], F32, "s2_all"); r1_all = hl.sb([128, 9, 8, 128], F32, "r1_all")
    Dg = hl.sb([128, 9, 8, 128], BF16, "Dg")
    m2 = hl.mark()
    St = hl.sb([128, 16, 128], F32, "St"); scr = hl.sb([128, 256], F32, "scr")
    m16 = hl.sb([128, 16, 16], F32, "m16"); cand = hl.sb([128, 8, 256], F32, "cand"); c16 = hl.sb([128, 8, 16], F32, "c16")
    e16 = hl.sb([128, 8, 16], F32, "e16"); tk = hl.sb([128, 8, 4], F32, "tk")
    for ti in range(9):
        dma(St[:], sS[ti * 128:(ti + 1) * 128, :].rearrange("p (b i) -> p b i", i=128), reads=[sS], writes=[St])
        for b in range(16):
            op("dve", lambda e: e.max(out=m16[:, b, 0:8], in_=St[:, b, :]), reads=[St], writes=[m16])
            op("dve", lambda e: e.match_replace(out=scr[:, 0:128], in_to_replace=m16[:, b, 0:8], in_values=St[:, b, :], imm_value=-1e30),
               reads=[St, m16], writes=[scr])
            op("dve", lambda e: e.max(out=m16[:, b, 8:16], in_=scr[:, 0:128]), reads=[scr], writes=[m16])
        for h in range(8):
            op("dve", lambda e: e.tensor_tensor(out=cand[:, h, :].rearrange("p (a b) -> p a b", b=16),
                                                in0=m16[:, 2 * h, :].unsqueeze(2).to_broadcast([128, 16, 16]),
                                                in1=m16[:, 2 * h + 1, :].unsqueeze(1).to_broadcast([128, 16, 16]), op=ALU.add), reads=[m16], writes=[cand])
            op("dve", lambda e: e.max(out=c16[:, h, 0:8], in_=cand[:, h, :]), reads=[cand], writes=[c16])
            op("dve", lambda e: e.match_replace(out=scr[:], in_to_replace=c16[:, h, 0:8], in_values=cand[:, h, :], imm_value=-1e30),
               reads=[cand, c16], writes=[scr])
            op("dve", lambda e: e.max(out=c16[:, h, 8:16], in_=scr[:]), reads=[scr], writes=[c16])
        op("dve", lambda e: e.tensor_scalar(out=tk[:, :, 0:1], in0=c16[:, :, 15:16], scalar1=-1e-5, scalar2=None, op0=ALU.add), reads=[c16], writes=[tk])
        op("dve", lambda e: e.tensor_tensor(out=e16[:], in0=c16[:], in1=c16[:, :, 0:1].to_broadcast([128, 8, 16]), op=ALU.subtract), reads=[c16], writes=[e16])
        op("act", lambda e: e.activation(out=e16[:], in_=e16[:], func=AF.Exp), reads=[e16], writes=[e16])
        op("dve", lambda e: e.reduce_sum(out=tk[:, :, 1], in_=e16[:], axis=mybir.AxisListType.X), reads=[e16], writes=[tk])
        op("dve", lambda e: e.tensor_tensor(out=tk[:, :, 3:4], in0=tk[:, :, 0:1], in1=c16[:, :, 0:1], op=ALU.subtract), reads=[tk, c16], writes=[tk])
        op("act", lambda e: e.activation(out=tk[:, :, 3:4], in_=tk[:, :, 3:4], func=AF.Exp), reads=[tk], writes=[tk])
        op("dve", lambda e: e.reciprocal(out=tk[:, :, 1:2], in_=tk[:, :, 1:2]), reads=[tk], writes=[tk])
        op("dve", lambda e: e.tensor_tensor(out=tk[:, :, 2:3], in0=tk[:, :, 3:4], in1=tk[:, :, 1:2], op=ALU.mult), reads=[tk], writes=[tk])
        S4 = St[:, :, :].rearrange("p (h two) i -> p h two i", two=2)
        op("dve", lambda e: e.tensor_tensor(out=r1_all[:, ti], in0=S4[:, :, 0, :], in1=tk[:, :, 0:1].to_broadcast([128, 8, 128]), op=ALU.subtract),
           reads=[St, tk], writes=[r1_all])
        op("act", lambda e: e.copy(out=s2_all[:, ti], in_=S4[:, :, 1, :]), reads=[St], writes=[s2_all])
        for h in range(8):
            op("dve", lambda e: e.tensor_scalar(out=Dg[:, ti, h, :], in0=ident[:], scalar1=tk[:, h, 2:3], scalar2=None, op0=ALU.mult),
               reads=[ident, tk], writes=[Dg])
    hl.release(m2)
    if stop_after <= 7.2:
        hl.finish(); return nc

    Unat = [hl.sb([128, 512], F32, "Unat%d" % i) for i in range(2)]; Ubf = hl.sb([128, D], BF16, "Ubf"); UT = hl.sb([128, 32, 128], BF16, "UT")
    hid = hl.sb([128, NOWN], BF16, "hid"); hgs = [hl.sb([128, NOWN], BF16, "hgs%d" % i) for i in range(1)]
    Ybs = [hl.sb([128, 8, 128], F32, "Yb%d" % i) for i in range(2)]; Ebs = [hl.sb([128, 8, 128], BF16, "Eb%d" % i) for i in range(2)]; Mb = [hl.sb([128, 8, 128], BF16, "Mb%d" % i) for i in range(3)]
    NEB = 128 if stop_after > 7.3 else 2
    def uload(i, qd):
        dma(Unat[qd % 2][:], I.peer_u.ap()[i * 128:(i + 1) * 128, qd * 512:(qd + 1) * 512], writes=[Unat[qd % 2]])

    def ucast(i):
        for qd in range(8):
            op("act", lambda e: e.copy(out=Ubf[:, qd * 512:(qd + 1) * 512], in_=Unat[qd % 2][:]), reads=[Unat[qd % 2]], writes=[Ubf])
            nq = i * 8 + qd + 2
            if nq < NEB * 8:
                uload(nq // 8, nq % 8)
    uload(0, 0); uload(0, 1)
    ucast(0)
    for i in range(NEB):
        for g0 in range(0, 32, 8):
            k = cnt["t"] % 2; cnt["t"] += 1
            for j in range(8):
                op("pe", lambda e: e.transpose(pb[k][:, j * 128:(j + 1) * 128], Ubf[:, (g0 + j) * 128:(g0 + j + 1) * 128], identb[:]),
                   reads=[Ubf, identb], writes=[pst[k]])
            op("dve", lambda e: e.tensor_copy(out=UT[:, g0:g0 + 8, :], in_=pb[k][:, :].rearrange("p (a b) -> p a b", b=128)), reads=[pst[k]], writes=[UT])
        if i + 1 < NEB:
            ucast(i + 1)
        ps = psg[0]; pg = psg[1]
        for c in range(3):
            for ti in range(3 * c, 3 * c + 3):
                Yb, Eb, mb = Ybs[ti % 2], Ebs[ti % 2], Mb[ti % 3]
                op("dve" if ti % 3 == 0 else "pool",
                   lambda e: e.tensor_tensor(out=Yb[:], in0=s2_all[:, ti], in1=r1_all[:, ti, :, i:i + 1].to_broadcast([128, 8, 128]), op=ALU.add),
                   reads=[s2_all, r1_all], writes=[Yb])
                op("act", lambda e: e.activation(out=Eb[:], in_=Yb[:], func=AF.Exp), reads=[Yb], writes=[Eb])
                op("dve", lambda e: e.scalar_tensor_tensor(out=mb[:], in0=Yb[:], scalar=0.0, in1=Eb[:], op0=ALU.is_ge, op1=ALU.mult), reads=[Yb, Eb], writes=[mb])
            for kc in range(32):
                op("pe", lambda e: e.matmul(ps[:, c * 512:c * 512 + 384], lhsT=UT[:, kc, :], rhs=actT[:, kc, c * 384:(c + 1) * 384],
                                            start=(kc == 0), stop=(kc == 31)), reads=[UT, actT], writes=[ps])
            for ti in range(3 * c, 3 * c + 3):
                mb = Mb[ti % 3]
                col = (ti // 3) * 512 + (ti % 3) * 128
                for h in range(8):
                    op("pe", lambda e: e.matmul(pg[:, col:col + 128], lhsT=mb[:, h, :], rhs=Dg[:, ti, h, :], start=(h == 0), stop=(h == 7)),
                       reads=[mb, Dg], writes=[pg])
        op("act", lambda e: e.activation(out=v3(hid, 3, 384), in_=ps[:, :].rearrange("p (c n) -> p c n", n=512)[:, :, 0:384], func=AF.Gelu),
           reads=[ps], writes=[hid])
        hg = hgs[0]
        op("dve", lambda e: e.tensor_tensor(out=v3(hg, 3, 384), in0=pg[:, :].rearrange("p (c n) -> p c n", n=512)[:, :, 0:384], in1=v3(hid, 3, 384), op=ALU.mult),
           reads=[pg, hid], writes=[hg])
        dma(HgT[i * 128:(i + 1) * 128, :], hg[:], reads=[hg], writes=[HgT])
    hl.release(mA)
    if stop_after <= 7.4:
        hl.finish(); return nc

    B["stgf"] = [hl.sb([128, NOWN], F32, "stgf%d" % i) for i in range(2)]
    xres = hl.sb([128, 9, 128], F32, "xres"); xo9 = hl.sb([128, 9, 128], F32, "xo9")
    vst = [hl.sb([128, 2, 256], F32, "vst%d" % i) for i in range(4)]; vbf = [hl.sb([128, 2, 256], BF16, "vbf%d" % i) for i in range(4)]
    hgl = [hl.sb([128, NOWN], BF16, "hgl%d" % i) for i in range(4)]
    NEC = 128 if stop_after > 7.3 else 2
    NRES = min(72, NEC)
    hgR = [hl.sb([128, NOWN], BF16, "hgR%d" % i) for i in range(NRES)]
    for ec in range(NRES):
        dma(hgR[ec][:], HgT[ec * 128:(ec + 1) * 128, :], reads=[HgT], writes=[hgR[ec]])
    vview = I.peer_v.ap().rearrange("(a p) d -> p a d", p=128)
    for dp in range(16):
        for e2 in range(NEC // 2):
            vs, vb = vst[e2 % 4], vbf[e2 % 4]
            dma(vs[:], vview[:, 2 * e2:2 * e2 + 2, dp * 256:(dp + 1) * 256], writes=[vs])
            if e2 % 2 == 0:
                op("dve", lambda e: e.tensor_copy(out=vb[:], in_=vs[:]), reads=[vs], writes=[vb])
            else:
                op("act", lambda e: e.copy(out=vb[:], in_=vs[:]), reads=[vs], writes=[vb])
            for sub in range(2):
                ec = 2 * e2 + sub
                if ec < NRES:
                    hgx = hgR[ec]
                else:
                    hgx = hgl[ec % 4]
                    dma(hgx[:], HgT[ec * 128:(ec + 1) * 128, :], reads=[HgT], writes=[hgx])
                for half in range(2):
                    ps = psg[half]
                    for c in range(3):
                        op("pe", lambda e: e.matmul(ps[:, c * 512:c * 512 + 384], lhsT=vb[:, sub, half * 128:(half + 1) * 128], rhs=hgx[:, c * 384:(c + 1) * 384],
                                                    start=(ec == 0), stop=(ec == NEC - 1)), reads=[vb, hgx], writes=[ps])
        for half in range(2):
            ps = psg[half]
            res_epi(x2, x3)(dp * 2 + half, ps, ps[:, :].rearrange("p (c n) -> p c n", n=512)[:, :, 0:384], 3, 384)
    hl.release(mA)
    if stop_after <= 7.5:
        hl.finish(); return nc

    alloc_gemm(0, norm=True, gemm_bufs=False)
    load_gain(I.norm_final)
    for i in range(9):
        xb = B["xt"][i % 2]
        rms_tile(x3[i * 128:(i + 1) * 128, :], xb, rd=[x3])
        dma(y[i * 128:(i + 1) * 128, :], xb[:], reads=[xb], writes=[y])

    hl.finish()
    return nc


def const_masks():
    k = np.arange(128)[:, None]; i = np.arange(128)[None, :]
    same = (k // 8) == (i // 8)
    mk = np.stack([k <= i, k > i, np.ones((128, 128), bool), (k <= i) & same, (k > i) & same, same], axis=1).astype(np.float32)
    rowm = np.zeros((128, 32), np.float32)
    for s in range(16):
        rowm[s * 8:(s + 1) * 8, s] = 1.0
        rowm[s * 8, 16 + s] = 1.0
    return {"c_masks": np.ascontiguousarray(mk), "c_rowm": rowm}


def make_in_maps(inp):
    f = lambda a: np.ascontiguousarray(np.asarray(a, dtype=np.float32))
    maps = []
    shared = {k: f(inp[k][0]) for k in ["norm_mix", "norm_mem", "norm_ffn", "w_in", "a_conv_w", "a_out", "ssd_conv_w", "ssd_conv_b",
                                        "ssd_dt_bias", "ssd_a_log", "ssd_d", "ssd_norm", "ssd_out", "w_mem_k", "w_mem_v",
                                        "xatt_out", "w_o", "peer_wq", "peer_u", "peer_v"]}
    shared["norm_final"] = f(inp["norm_final"])
    shared["peer_subkeys"] = f(inp["peer_subkeys"][0]).reshape(16, 128, 128)
    shared["c_ident"] = np.eye(128, dtype=np.float32)
    shared.update(const_masks())
    xp = np.asarray(inp["x_prompt"]); xs = np.asarray(inp["x_sample"])
    for c in range(8):
        s, half = c // 2, c % 2
        m = dict(shared)
        m["xa"] = f(np.concatenate([xp[s, half * 1024:(half + 1) * 1024], xs[16 * c:16 * c + 16].reshape(128, D)], axis=0))
        m["xpre"] = f(xp[s, 0:1024]) if half == 1 else np.zeros((NPRE, D), np.float32)
        m["flag"] = np.full((128, 1), float(half), np.float32)
        m["mem"] = f(inp["mem_prompt"][s])
        m["ck"] = f(inp["cache_mem_k"][0, 16 * c:16 * c + 16]).reshape(16, 256, 2048)
        m["cv"] = f(inp["cache_mem_v"][0, 16 * c:16 * c + 16]).reshape(16, 256, 2048)
        m["sca"] = f(inp["state_conv_a"][0, 16 * c:16 * c + 16])
        m["ssc"] = f(inp["state_ssd_conv"][0, 16 * c:16 * c + 16])
        m["sst"] = f(inp["state_ssd"][0, 16 * c:16 * c + 16])
        maps.append(m)
    return maps


def assemble(res):
    r = res
    y_p = np.stack([np.concatenate([r[2 * s]["y"][:1024], r[2 * s + 1]["y"][:1024]], axis=0) for s in range(4)])
    y_s = np.concatenate([r[c]["y"][1024:].reshape(16, 8, D) for c in range(8)], axis=0)
    mk_ = np.stack([r[2 * s]["mk"].reshape(256, 4, 512) for s in range(4)])[None]
    mv_ = np.stack([r[2 * s]["mv"].reshape(256, 4, 512) for s in range(4)])[None]
    ca_p = np.stack([r[2 * s + 1]["ca_p"] for s in range(4)])[None]
    sc_p = np.stack([r[2 * s + 1]["sc_p"] for s in range(4)])[None]
    st_p = np.stack([r[2 * s + 1]["st_p"] for s in range(4)])[None]
    ca_s = np.concatenate([r[c]["ca_s"] for c in range(8)], axis=0)[None]
    sc_s = np.concatenate([r[c]["sc_s"] for c in range(8)], axis=0)[None]
    st_s = np.concatenate([r[c]["st_s"] for c in range(8)], axis=0)[None]
    return tuple(np.ascontiguousarray(a, dtype=np.float32) for a in (y_p, y_s, mk_, mv_, ca_p, sc_p, st_p, ca_s, sc_s, st_s))


def kernel(**inputs):
    nc = build()
    maps = make_in_maps(inputs)
    maps = [{k: v for k, v in m.items() if k in nc._used_inputs} for m in maps]
    res = run_bass_kernel_spmd(nc, maps, core_ids=list(range(8)))
    return assemble(res.results)
```

```python
import numpy as np
import concourse.bass as bass
import concourse.mybir as mybir
from concourse.bass_utils import run_bass_kernel_spmd

F32 = mybir.dt.float32
BF16 = mybir.dt.bfloat16
AF = mybir.ActivationFunctionType
ALU = mybir.AluOpType

D = 4096
NPRE = 1024
NOWN = 1152
NALL = NPRE + NOWN
INW = 30784
C_AIN, C_ABG, C_ACG, C_Z, C_XBC, C_DT, C_Q, C_G = 0, 2048, 4096, 6144, 10240, 16384, 16448, 18496
EPS = 1e-6


class T:
    _n = 0

    def __init__(self, h, key):
        self.h = h
        self.key = key

    def __getitem__(self, idx):
        return self.h[idx]

    def ap(self):
        return self.h.ap()


class HL:
    NDMA = 32

    def __init__(self, nc):
        self.nc = nc
        self.eng = {"pe": nc.tensor, "dve": nc.vector, "act": nc.scalar, "pool": nc.gpsimd, "sp": nc.sync}
        self.sem = {e: nc.alloc_semaphore("s_" + e) for e in ("pe", "dve", "act", "pool")}
        self.cnt = {e: 0 for e in self.sem}
        self.dsem = [nc.alloc_semaphore("d%d" % i) for i in range(self.NDMA)]
        self.dcnt = [0] * self.NDMA
        self.dnext = 0
        self.seen = {e: {} for e in self.eng}
        self.lastw = {}
        self.reads = {}
        self.nid = 0

    def sb(self, shape, dt=F32, name=None):
        self.nid += 1
        name = (name or "sb") + "_%d" % self.nid
        return T(self.nc.alloc_sbuf_tensor(name, list(shape), dt), name)

    def ps(self, shape, dt=F32, name=None):
        self.nid += 1
        name = name or ("ps%d" % self.nid)
        return T(self.nc.alloc_psum_tensor(name, list(shape), dt), name)

    def dram(self, name, shape, dt=F32, kind="Internal"):
        return T(self.nc.dram_tensor(name, list(shape), dt, kind=kind), name)

    def _wait(self, e, ticket):
        kind, ident, val = ticket
        k = (kind, ident)
        if self.seen[e].get(k, 0) >= val:
            return
        self.seen[e][k] = val
        if kind == "c":
            if ident == e and e == "pe":
                return
            self.eng[e].wait_ge(self.sem[ident], val)
        else:
            self.eng[e].wait_ge(self.dsem[ident], 16 * val)

    @staticmethod
    def _k(b):
        return b.key if isinstance(b, T) else b

    def _deps(self, e, reads, writes):
        for b in list(reads) + list(writes):
            t = self.lastw.get(self._k(b))
            if t is not None:
                self._wait(e, t)
        for b in writes:
            for t in self.reads.get(self._k(b), ()):
                self._wait(e, t)

    def _record(self, ticket, reads, writes):
        for b in reads:
            k = self._k(b)
            lst = self.reads.setdefault(k, [])
            lst.append(ticket)
            if len(lst) > 16:
                d = {}
                for t in lst:
                    kk = (t[0], t[1])
                    if kk not in d or d[kk][2] < t[2]:
                        d[kk] = t
                self.reads[k] = list(d.values())
        for b in writes:
            k = self._k(b)
            self.lastw[k] = ticket
            self.reads[k] = []

    def op(self, e, fn, reads=(), writes=()):
        px = [b for b in reads if isinstance(b, T) and b.key.startswith("ps") and e != "pe"]
        if px:
            writes = list(writes) + px
        self._deps(e, reads, writes)
        inst = fn(self.eng[e])
        inst.then_inc(self.sem[e], 1)
        self.cnt[e] += 1
        ticket = ("c", e, self.cnt[e])
        self._record(ticket, reads, writes)
        return ticket

    def dma(self, out, in_, reads=(), writes=(), q="sp", **kw):
        s = self.dnext
        self.dnext = (self.dnext + 1) % self.NDMA
        if self.dcnt[s] > 0:
            self._wait(q, ("d", s, self.dcnt[s]))
        self._deps(q, reads, writes)
        inst = self.eng[q].dma_start(out=out, in_=in_, **kw)
        inst.then_inc(self.dsem[s], 16)
        self.dcnt[s] += 1
        ticket = ("d", s, self.dcnt[s])
        self._record(ticket, reads, writes)
        return ticket

    def barrier(self):
        for e in self.eng:
            for s in range(self.NDMA):
                if self.dcnt[s]:
                    self._wait(e, ("d", s, self.dcnt[s]))
            for e2 in self.sem:
                if self.cnt[e2] and e2 != e:
                    self._wait(e, ("c", e2, self.cnt[e2]))

    def mark(self):
        return (self.nc.sbuf_base, self.nc.sbuf_top)

    def release(self, m):
        self.barrier()
        self.nc.sbuf_base, self.nc.sbuf_top = m

    def finish(self):
        for s in range(self.NDMA):
            if self.dcnt[s]:
                self._wait("sp", ("d", s, self.dcnt[s]))
        for e in self.sem:
            if self.cnt[e]:
                self._wait("sp", ("c", e, self.cnt[e]))


def build(stop_after=99, dbg=False):
    nc = bass.Bass("TRN2", target_bir_lowering=False)
    hl = HL(nc)
    op, dma = hl.op, hl.dma

    def din(name, shape, dt=F32):
        return hl.dram(name, shape, dt, kind="ExternalInput")

    def dout(name, shape, dt=F32):
        return hl.dram(name, shape, dt, kind="ExternalOutput")

    SHP = dict(xa=[NOWN, D], xpre=[NPRE, D], flag=[128, 1], mem=[256, D], ck=[16, 256, 2048], cv=[16, 256, 2048],
               sca=[16, 2, 2048], ssc=[16, 3, 6144], sst=[16, 64, 64, 128], norm_mix=[D], norm_mem=[D], norm_ffn=[D],
               norm_final=[D], w_in=[D, INW], a_conv_w=[3, 2048], a_out=[2048, D], ssd_conv_w=[4, 6144], ssd_conv_b=[6144],
               ssd_dt_bias=[64], ssd_a_log=[64], ssd_d=[64], ssd_norm=[D], ssd_out=[D, D], w_mem_k=[D, 2048],
               w_mem_v=[D, 2048], xatt_out=[2048, D], w_o=[D, D], peer_wq=[D, 2048], peer_subkeys=[16, 128, 128],
               peer_u=[16384, D], peer_v=[16384, D], c_ident=[128, 128], c_masks=[128, 6, 128], c_rowm=[128, 32])
    used = {}

    class _I:
        def __getattr__(self, name):
            if name not in used:
                used[name] = din(name, SHP[name])
            return used[name]
    I = _I()
    nc._used_inputs = used
    nc._hl = hl

    y = dout("y", [NOWN, D])
    mk = dout("mk", [256, 2048]); mv = dout("mv", [256, 2048])
    ca_p = dout("ca_p", [2, 2048]); sc_p = dout("sc_p", [3, 6144]); st_p = dout("st_p", [64, 64, 128])
    ca_s = dout("ca_s", [16, 2, 2048]); sc_s = dout("sc_s", [16, 3, 6144]); st_s = dout("st_s", [16, 64, 64, 128])

    projT = hl.dram("projT", [INW, NALL], BF16)
    dtT = hl.dram("dtT", [64, NALL], F32)

    ident = hl.sb([128, 128], F32, "ident")
    identb = hl.sb([128, 128], BF16, "identb")
    ss = hl.sb([128, 4], F32, "ss")
    mA = hl.mark()
    kT = hl.sb([128, 16, 256], BF16, "kT")
    vtok = hl.sb([128, 2, 2048], BF16, "vtok")
    psg = [hl.ps([128, 1536], F32, "psg%d" % i) for i in range(2)]
    pst = [hl.ps([128, 512], F32, "pst%d" % i) for i in range(2)]
    cnt = {"g": 0, "t": 0, "x": 0, "s": 0}
    B = {}

    def bank(i, j):
        t = T(psg[i].h, "psg%d_b%d" % (i, j))
        return t
    PB = [[bank(i, j) for j in range(3)] for i in range(2)]

    def alloc_gemm(ntok_act, kc_act=32, norm=True, gemm_bufs=True):
        if norm:
            B["gbc"] = hl.sb([128, D], F32, "gbc")
            B["xt"] = [hl.sb([128, D], F32, "xt%d" % i) for i in range(2)]
            B["junk"] = hl.sb([128, D], BF16, "junk")
        if ntok_act:
            B["actT"] = hl.sb([128, kc_act, ntok_act], BF16, "actT")
        if not gemm_bufs:
            return
        B["wst"] = [hl.sb([128, 32, 128], F32, "wst%d" % i) for i in range(2)]
        B["wbf"] = [hl.sb([128, 32, 128], BF16, "wbf%d" % i) for i in range(2)]
        B["stg"] = [hl.sb([128, NOWN], BF16, "stg%d" % i) for i in range(2)]
        B["stgf"] = [hl.sb([128, NOWN], F32, "stgf%d" % i) for i in range(2)]

    dma(ident[:], I.c_ident.ap(), writes=[ident])
    op("dve", lambda e: e.tensor_copy(out=identb[:], in_=ident[:]), reads=[ident], writes=[identb])

    def load_gain(g):
        dma(B["gbc"][:], g.ap().partition_broadcast(128), writes=[B["gbc"]])

    def rms_tile(src_ap, xb, rd=()):
        dma(xb[:], src_ap, reads=list(rd), writes=[xb])
        op("dve", lambda e: e.memset(ss[:, 0:1], 0.0), writes=[ss])
        op("act", lambda e: e.activation(out=B["junk"][:], in_=xb[:], func=AF.Square, accum_out=ss[:, 0:1]),
           reads=[xb, ss], writes=[B["junk"], ss])
        op("dve", lambda e: e.tensor_scalar(out=ss[:, 1:2], in0=ss[:, 0:1], scalar1=1.0 / D, scalar2=EPS,
                                            op0=ALU.mult, op1=ALU.add), reads=[ss], writes=[ss])
        op("act", lambda e: e.activation(out=ss[:, 3:4], in_=ss[:, 1:2], func=AF.Sqrt), reads=[ss], writes=[ss])
        op("dve", lambda e: e.reciprocal(out=ss[:, 2:3], in_=ss[:, 3:4]), reads=[ss], writes=[ss])
        op("dve", lambda e: e.scalar_tensor_tensor(out=xb[:], in0=xb[:], scalar=ss[:, 2:3], in1=B["gbc"][:],
                                                   op0=ALU.mult, op1=ALU.mult), reads=[xb, ss, B["gbc"]], writes=[xb])

    def to_fm(xb, dst, col0):
        for g4 in range(8):
            p = pst[cnt["t"] % 2]; cnt["t"] += 1
            for j in range(4):
                c = g4 * 4 + j
                op("pe", lambda e: e.transpose(p[:, j * 128:(j + 1) * 128], xb[:, c * 128:(c + 1) * 128], ident[:]),
                   reads=[xb, ident], writes=[p])
            eng = "act" if g4 % 2 == 0 else "dve"
            src = p[:, :].rearrange("p (j n) -> p j n", n=128)
            dsta = dst[:, g4 * 4:(g4 + 1) * 4, col0:col0 + 128]
            if eng == "act":
                op("act", lambda e: e.copy(out=dsta, in_=src), reads=[p], writes=[dst])
            else:
                op("dve", lambda e: e.tensor_copy(out=dsta, in_=src), reads=[p], writes=[dst])

    def norm_fm(src, nt, gain, dst, rd=()):
        load_gain(gain)
        for i in range(nt):
            xb = B["xt"][cnt["x"] % 2]; cnt["x"] += 1
            rms_tile(src[i * 128:(i + 1) * 128, :], xb, rd)
            to_fm(xb, dst, i * 128)

    def gemm(act, KC, t0, ntok, wfn, nblk, epi):
        nch = (ntok + 511) // 512
        csz = ntok // nch
        assert csz * nch == ntok

        def load(b):
            i = b % 2
            dma(B["wst"][i][:, 0:KC, :], wfn(b), writes=[B["wst"][i]])
            h2 = KC // 2
            op("dve", lambda e: e.tensor_copy(out=B["wbf"][i][:, 0:h2, :], in_=B["wst"][i][:, 0:h2, :]), reads=[B["wst"][i]], writes=[B["wbf"][i]])
            op("act", lambda e: e.copy(out=B["wbf"][i][:, h2:KC, :], in_=B["wst"][i][:, h2:KC, :]), reads=[B["wst"][i]], writes=[B["wbf"][i]])
        load(0)
        for b in range(nblk):
            if b + 1 < nblk:
                load(b + 1)
            ps = psg[cnt["g"] % 2]; cnt["g"] += 1
            w = B["wbf"][b % 2]
            for c in range(nch):
                for kc in range(KC):
                    op("pe", lambda e: e.matmul(ps[:, c * 512:c * 512 + csz], lhsT=w[:, kc, :],
                                                rhs=act[:, kc, t0 + c * csz:t0 + (c + 1) * csz],
                                                start=(kc == 0), stop=(kc == KC - 1)),
                       reads=[w, act], writes=[ps])
            psv = ps[:, 0:nch * 512].rearrange("p (c n) -> p c n", n=512)[:, :, 0:csz]
            epi(b, ps, psv, nch, csz)

    def wview(W, c0):
        v = W.ap().rearrange("(kc p) n -> p kc n", p=128)
        return lambda b: v[:, :, c0 + b * 128:c0 + (b + 1) * 128]

    def v3(t, nch, csz, n0=0):
        return t[:, n0:n0 + nch * csz].rearrange("p (c n) -> p c n", n=csz)

    m0 = hl.mark()
    alloc_gemm(256)
    if stop_after == 0.05:
        load_gain(I.norm_mem)
        dma(mk.ap()[0:128, :], B["gbc"][:, 0:2048], reads=[B["gbc"]], writes=[mk])
        hl.finish(); return nc
    if stop_after == 0.07:
        load_gain(I.norm_mem)
        rms_tile(I.mem.ap()[0:128, :], B["xt"][0])
        dma(mk.ap()[0:128, :], B["xt"][0][:, 0:2048], reads=[B["xt"][0]], writes=[mk])
        hl.finish(); return nc
    norm_fm(I.mem.ap(), 2, I.norm_mem, B["actT"])
    if stop_after == 0.1:
        o = B["stgf"][0]
        op("dve", lambda e: e.tensor_copy(out=o[:, 0:256], in_=B["actT"][:, 0, 0:256]), reads=[B["actT"]], writes=[o])
        dma(mk.ap()[0:128, 0:256], o[:, 0:256], reads=[o], writes=[mk])
        hl.finish(); return nc

    def kv_epi(dst_dram, keepT, keepTok):
        def epi(b, ps, psv, nch, csz):
            s = B["stgf"][cnt["s"] % 2]; cnt["s"] += 1
            op("act", lambda e: e.copy(out=s[:, 0:256], in_=ps[:, 0:256]), reads=[ps], writes=[s])
            if keepT is not None:
                op("dve", lambda e: e.tensor_copy(out=keepT[:, b, :], in_=ps[:, 0:256]), reads=[ps], writes=[keepT])
            p = pst[cnt["t"] % 2]; cnt["t"] += 1
            for i in range(2):
                op("pe", lambda e: e.transpose(p[:, i * 128:(i + 1) * 128], s[:, i * 128:(i + 1) * 128], ident[:]),
                   reads=[s, ident], writes=[p])
            o = B["stgf"][cnt["s"] % 2]; cnt["s"] += 1
            op("dve", lambda e: e.tensor_copy(out=o[:, 0:256], in_=p[:, 0:256]), reads=[p], writes=[o])
            if keepTok is not None:
                op("act", lambda e: e.copy(out=keepTok[:, :, b * 128:(b + 1) * 128],
                                           in_=p[:, 0:256].rearrange("p (i n) -> p i n", n=128)), reads=[p], writes=[keepTok])
            if stop_after == 0.3:
                dma(dst_dram.ap()[0:128, 0:256], o[:, 0:256], reads=[o], writes=[dst_dram])
            else:
                for i in range(2):
                    dma(dst_dram.ap()[i * 128:(i + 1) * 128, b * 128:(b + 1) * 128], o[:, i * 128:(i + 1) * 128], reads=[o], writes=[dst_dram])
        return epi
    if stop_after in (0.2, 0.25):
        def epi0(b, ps, psv, nch, csz):
            s = B["stgf"][cnt["s"] % 2]; cnt["s"] += 1
            op("act", lambda e: e.copy(out=s[:, 0:256], in_=ps[:, 0:256]), reads=[ps], writes=[s])
            dma(mk.ap()[0:128, (b % 8) * 256:(b % 8 + 1) * 256], s[:, 0:256], reads=[s], writes=[mk])
        gemm(B["actT"], 32, 0, 256, wview(I.w_mem_k, 0), 2 if stop_after == 0.2 else 16, epi0)
        hl.finish(); return nc
    gemm(B["actT"], 32, 0, 256, wview(I.w_mem_k, 0), 16, kv_epi(mk, kT, None))
    gemm(B["actT"], 32, 0, 256, wview(I.w_mem_v, 0), 16, kv_epi(mv, None, vtok))
    hl.release(m0)
    if stop_after <= 1:
        hl.finish(); return nc
    alloc_gemm(NOWN)

    def proj_epi(row0, tokbase):
        def epi(b, ps, psv, nch, csz):
            s = B["stg"][cnt["s"] % 2]; cnt["s"] += 1
            op("act", lambda e: e.copy(out=v3(s, nch, csz), in_=psv), reads=[ps], writes=[s])
            r = row0 + b * 128
            dma(projT[r:r + 128, tokbase:tokbase + nch * csz], s[:, 0:nch * csz], reads=[s], writes=[projT])
        return epi

    def dt_epi(tokbase):
        def epi(b, ps, psv, nch, csz):
            s = B["stgf"][cnt["s"] % 2]; cnt["s"] += 1
            op("act", lambda e: e.copy(out=v3(s, nch, csz)[0:64], in_=psv[0:64]), reads=[ps], writes=[s])
            dma(dtT[:, tokbase:tokbase + nch * csz], s[0:64, 0:nch * csz], reads=[s], writes=[dtT])
        return epi

    def wview_dt(b):
        v = I.w_in.ap().rearrange("(kc p) n -> p kc n", p=128)
        return v[:, :, C_DT:C_DT + 128]

    norm_fm(I.xpre.ap(), 8, I.norm_mix, B["actT"])
    gemm(B["actT"], 32, 0, NPRE, wview(I.w_in, C_XBC), 48, proj_epi(C_XBC, 0))
    gemm(B["actT"], 32, 0, NPRE, wview_dt, 1, dt_epi(0))
    gemm(B["actT"], 32, NPRE - 128, 128, wview(I.w_in, C_AIN), 16, proj_epi(C_AIN, NPRE - 128))
    gemm(B["actT"], 32, NPRE - 128, 128, wview(I.w_in, C_ACG), 16, proj_epi(C_ACG, NPRE - 128))
    norm_fm(I.xa.ap(), 9, I.norm_mix, B["actT"])
    gemm(B["actT"], 32, 0, NOWN, wview(I.w_in, 0), 128, proj_epi(0, NPRE))
    gemm(B["actT"], 32, 0, NOWN, wview_dt, 1, dt_epi(NPRE))
    gemm(B["actT"], 32, 0, NOWN, wview(I.w_in, C_Q), 112, proj_epi(C_Q, NPRE))

    hl.release(m0)
    acw = hl.sb([128, 16, 3], F32, "acw")
    for c in range(16):
        dma(acw[:, c, :], I.a_conv_w.ap()[:, c * 128:(c + 1) * 128].rearrange("k p -> p k"), writes=[acw], allow_slow_non_contiguous=True)
    scaN = hl.sb([32, 2048], F32, "scaN")
    dma(scaN[:], I.sca.ap().rearrange("s k f -> (s k) f"), writes=[scaN])
    caPS = hl.sb([128, 2, 2048], F32, "caPS")
    ain = hl.sb([128, 1280], BF16, "ain"); acg = hl.sb([128, 1280], BF16, "acg"); abg = hl.sb([128, NOWN], BF16, "abg")
    u = hl.sb([128, 1280], F32, "u"); uext = hl.sb([128, 16, 10], F32, "uext"); cvo = hl.sb([128, NOWN], F32, "cvo")
    vT = hl.sb([128, 16, NOWN], BF16, "vT")
    for c in range(16):
        dma(ain[:], projT[C_AIN + c * 128:C_AIN + (c + 1) * 128, 896:2176], reads=[projT], writes=[ain])
        dma(acg[:], projT[C_ACG + c * 128:C_ACG + (c + 1) * 128, 896:2176], reads=[projT], writes=[acg])
        dma(abg[:], projT[C_ABG + c * 128:C_ABG + (c + 1) * 128, 1024:2176], reads=[projT], writes=[abg])
        op("dve", lambda e: e.tensor_tensor(out=u[:], in0=ain[:], in1=acg[:], op=ALU.mult), reads=[ain, acg], writes=[u])
        op("dve", lambda e: e.tensor_scalar(out=cvo[:, 0:1024], in0=u[:, 126:1150], scalar1=acw[:, c, 0:1], scalar2=None, op0=ALU.mult),
           reads=[u, acw], writes=[cvo])
        for k in (1, 2):
            op("dve", lambda e: e.scalar_tensor_tensor(out=cvo[:, 0:1024], in0=u[:, 126 + k:1150 + k], scalar=acw[:, c, k:k + 1],
                                                       in1=cvo[:, 0:1024], op0=ALU.mult, op1=ALU.add), reads=[u, acw, cvo], writes=[cvo])
        kk = cnt["t"] % 2; cnt["t"] += 1
        op("pe", lambda e: e.transpose(pst[kk][:, 0:32], scaN[0:32, c * 128:(c + 1) * 128], ident[0:32, 0:32]), reads=[scaN, ident], writes=[pst[kk]])
        op("act", lambda e: e.copy(out=uext[:, :, 0:2], in_=pst[kk][:, 0:32].rearrange("p (s k) -> p s k", k=2)), reads=[pst[kk]], writes=[uext])
        for j, c0 in enumerate((1024, 1152)):
            op("pe", lambda e: e.transpose(pst[kk][:, 128 + j * 128:256 + j * 128], u[:, c0:c0 + 128], ident[:]), reads=[u, ident], writes=[pst[kk]])
        op("dve", lambda e: e.tensor_copy(out=caPS[:, :, c * 128:(c + 1) * 128], in_=pst[kk][:, 128:384].rearrange("p (a b) -> p a b", b=128)),
           reads=[pst[kk]], writes=[caPS])
        op("act", lambda e: e.copy(out=uext[:, :, 2:10], in_=u[:, 1152:1280].rearrange("p (s t) -> p s t", t=8)), reads=[u], writes=[uext])
        cs = cvo[:, 1024:1152].rearrange("p (s t) -> p s t", t=8)
        op("dve", lambda e: e.tensor_scalar(out=cs, in0=uext[:, :, 0:8], scalar1=acw[:, c, 0:1], scalar2=None, op0=ALU.mult),
           reads=[uext, acw], writes=[cvo])
        for k in (1, 2):
            op("dve", lambda e: e.scalar_tensor_tensor(out=cs, in0=uext[:, :, k:k + 8], scalar=acw[:, c, k:k + 1], in1=cs,
                                                       op0=ALU.mult, op1=ALU.add), reads=[uext, acw, cvo], writes=[cvo])
        op("dve", lambda e: e.tensor_tensor(out=vT[:, c, :], in0=cvo[:], in1=abg[:], op=ALU.mult), reads=[cvo, abg], writes=[vT])
    dma(ca_p.ap(), caPS[126:128, 0, :], reads=[caPS], writes=[ca_p])
    for sq in range(16):
        dma(ca_s.ap()[sq], caPS[sq * 8 + 6:sq * 8 + 8, 1, :], reads=[caPS], writes=[ca_s])
    if stop_after <= 3:
        hl.finish(); return nc
    mixT = hl.dram("mixT", [D, NOWN], F32, kind="ExternalOutput" if dbg else "Internal")
    gt = hl.sb([128, NOWN], BF16, "gt"); gs = hl.sb([128, NOWN], F32, "gs"); mprev = hl.sb([128, NOWN], F32, "mprev")

    def gate_epi(gi, first):
        def epi(b, ps, psv, nch, csz):
            r0 = C_G + gi * D + b * 128
            dma(gt[:], projT[r0:r0 + 128, NPRE:NALL], reads=[projT], writes=[gt])
            op("act", lambda e: e.activation(out=gs[:], in_=gt[:], func=AF.Sigmoid), reads=[gt], writes=[gs])
            if not first:
                dma(mprev[:], mixT[b * 128:(b + 1) * 128, :], reads=[mixT], writes=[mprev])
            s = B["stgf"][cnt["s"] % 2]; cnt["s"] += 1
            op("dve", lambda e: e.tensor_tensor(out=v3(s, nch, csz), in0=psv, in1=v3(gs, nch, csz), op=ALU.mult), reads=[ps, gs], writes=[s])
            if not first:
                op("dve", lambda e: e.tensor_tensor(out=s[:], in0=s[:], in1=mprev[:], op=ALU.add), reads=[s, mprev], writes=[s])
            dma(mixT[b * 128:(b + 1) * 128, :], s[:], reads=[s], writes=[mixT])
        return epi
    alloc_gemm(0, norm=False)
    gemm(vT, 16, 0, NOWN, wview(I.a_out, 0), 32, gate_epi(0, True))
    hl.release(m0)
    if stop_after <= 3.5:
        hl.finish(); return nc

    xcT = hl.dram("xcT", [6144, NALL], BF16)
    yT = hl.dram("yT", [D, NOWN], BF16, kind="ExternalOutput" if dbg else "Internal")
    cw = hl.sb([128, 48, 4], F32, "cw"); cb = hl.sb([128, 48], F32, "cb")
    for c in range(48):
        dma(cw[:, c, :], I.ssd_conv_w.ap()[:, c * 128:(c + 1) * 128].rearrange("k p -> p k"), writes=[cw], allow_slow_non_contiguous=True)
    dma(cb[:], I.ssd_conv_b.ap().rearrange("(c p) -> p c", p=128), writes=[cb], allow_slow_non_contiguous=True)
    xin = [hl.sb([128, NALL], BF16, "xin%d" % i) for i in range(2)]
    xe = hl.sb([128, 3 + 2048], F32, "xe"); xes = hl.sb([128, 16, 11], F32, "xes")
    cacc = hl.sb([128, NALL], F32, "cacc"); xco = [hl.sb([128, NALL], BF16, "xco%d" % i) for i in range(2)]
    sscN = hl.sb([48, 6144], F32, "sscN")
    dma(sscN[:], I.ssc.ap().rearrange("s k f -> (s k) f"), writes=[sscN])
    scPS = hl.sb([128, 2, 6144], F32, "scPS")
    pbb = [pst[i][:, :].bitcast(BF16) for i in range(2)]
    op("dve", lambda e: e.memset(xe[:, 0:3], 0.0), writes=[xe])
    for c in range(48):
        xi = xin[c % 2]; xo = xco[c % 2]
        dma(xi[:], projT[C_XBC + c * 128:C_XBC + (c + 1) * 128, :], reads=[projT], writes=[xi])
        op("act", lambda e: e.copy(out=xe[:, 3:3 + 2048], in_=xi[:, 0:2048]), reads=[xi], writes=[xe])
        kk = cnt["t"] % 2; cnt["t"] += 1
        op("pe", lambda e: e.transpose(pst[kk][:, 0:48], sscN[0:48, c * 128:(c + 1) * 128], ident[0:48, 0:48]), reads=[sscN, ident], writes=[pst[kk]])
        op("act", lambda e: e.copy(out=xes[:, :, 0:3], in_=pst[kk][:, 0:48].rearrange("p (s k) -> p s k", k=3)), reads=[pst[kk]], writes=[xes])
        for j, c0 in enumerate((1920, 2048)):
            op("pe", lambda e: e.transpose(pbb[kk][:, 256 + j * 128:384 + j * 128], xi[:, c0:c0 + 128], identb[:]), reads=[xi, identb], writes=[pst[kk]])
        op("dve", lambda e: e.tensor_copy(out=scPS[:, :, c * 128:(c + 1) * 128], in_=pbb[kk][:, 256:512].rearrange("p (a b) -> p a b", b=128)),
           reads=[pst[kk]], writes=[scPS])
        op("act", lambda e: e.copy(out=xes[:, :, 3:11], in_=xi[:, 2048:NALL].rearrange("p (s t) -> p s t", t=8)), reads=[xi], writes=[xes])
        cs = cacc[:, 2048:NALL].rearrange("p (s t) -> p s t", t=8)
        op("dve", lambda e: e.tensor_scalar(out=cacc[:, 0:2048], in0=xe[:, 0:2048], scalar1=cw[:, c, 0:1], scalar2=None, op0=ALU.mult),
           reads=[xe, cw], writes=[cacc])
        op("dve", lambda e: e.tensor_scalar(out=cs, in0=xes[:, :, 0:8], scalar1=cw[:, c, 0:1], scalar2=None, op0=ALU.mult),
           reads=[xes, cw], writes=[cacc])
        for k in (1, 2, 3):
            op("dve", lambda e: e.scalar_tensor_tensor(out=cacc[:, 0:2048], in0=xe[:, k:k + 2048], scalar=cw[:, c, k:k + 1],
                                                       in1=cacc[:, 0:2048], op0=ALU.mult, op1=ALU.add), reads=[xe, cw, cacc], writes=[cacc])
            op("dve", lambda e: e.scalar_tensor_tensor(out=cs, in0=xes[:, :, k:k + 8], scalar=cw[:, c, k:k + 1], in1=cs,
                                                       op0=ALU.mult, op1=ALU.add), reads=[xes, cw, cacc], writes=[cacc])
        op("act", lambda e: e.activation(out=xo[:], in_=cacc[:], func=AF.Silu, bias=cb[:, c:c + 1]), reads=[cacc, cb], writes=[xo])
        dma(xcT[c * 128:(c + 1) * 128, :], xo[:], reads=[xo], writes=[xcT])
    dma(sc_p.ap(), scPS[125:128, 0, :], reads=[scPS], writes=[sc_p])
    for sq in range(16):
        dma(sc_s.ap()[sq], scPS[sq * 8 + 5:sq * 8 + 8, 1, :], reads=[scPS], writes=[sc_s])
    hl.release(m0)
    if stop_after <= 4.1:
        hl.finish(); return nc

    MK = hl.sb([128, 6, 128], F32, "MK")
    dma(MK[:], I.c_masks.ap(), writes=[MK])
    rowm = hl.sb([128, 32], F32, "rowm")
    dma(rowm[:], I.c_rowm.ap(), writes=[rowm])
    hv = hl.sb([64, 4], F32, "hv")
    dma(hv[:, 0:1], I.ssd_dt_bias.ap().rearrange("(p o) -> p o", o=1), writes=[hv])
    dma(hv[:, 1:2], I.ssd_a_log.ap().rearrange("(p o) -> p o", o=1), writes=[hv])
    op("act", lambda e: e.activation(out=hv[:, 2:3], in_=hv[:, 1:2], func=AF.Exp), reads=[hv], writes=[hv])
    op("dve", lambda e: e.tensor_scalar(out=hv[:, 2:3], in0=hv[:, 2:3], scalar1=-1.0, scalar2=None, op0=ALU.mult), reads=[hv], writes=[hv])
    dtA = hl.sb([64, NALL], F32, "dtA"); adtA = hl.sb([64, NALL], F32, "adtA")
    dma(dtA[:], dtT.ap(), reads=[dtT], writes=[dtA])
    op("act", lambda e: e.activation(out=dtA[:], in_=dtA[:], func=AF.Exp, bias=hv[:, 0:1]), reads=[dtA, hv], writes=[dtA])
    op("act", lambda e: e.activation(out=dtA[:], in_=dtA[:], func=AF.Ln, bias=1.0), reads=[dtA], writes=[dtA])
    op("dve", lambda e: e.tensor_scalar(out=adtA[:], in0=dtA[:], scalar1=hv[:, 2:3], scalar2=None, op0=ALU.mult), reads=[dtA, hv], writes=[adtA])
    dbc = hl.sb([128, 64], F32, "dbc")
    dma(dbc[:], I.ssd_d.ap().partition_broadcast(128), writes=[dbc])
    nbc = hl.sb([128, D], F32, "nbc")
    dma(nbc[:], I.ssd_norm.ap().partition_broadcast(128), writes=[nbc])
    flg = hl.sb([128, 1], F32, "flg")
    dma(flg[:], I.flag.ap(), writes=[flg])

    xcc = hl.sb([128, 48, 128], BF16, "xcc")
    zc = hl.sb([128, 32, 128], BF16, "zc")
    x_tok = hl.sb([128, D], BF16, "x_tok"); xs_tok = hl.sb([128, D], BF16, "xs_tok"); xs_dec = hl.sb([128, D], BF16, "xs_dec")
    B_tok = hl.sb([128, 1024], BF16, "B_tok"); Bm = hl.sb([128, 1024], BF16, "Bm")
    z_tok = hl.sb([128, D], BF16, "z_tok")
    ytk = hl.sb([128, D], F32, "ytk"); ybf = hl.sb([128, D], BF16, "ybf")
    dtk = hl.sb([128, 64 * 6], F32, "dtk")
    DT_, ADT_, ACU_, TOT_, EAC_, EDS_ = [dtk[:, i * 64:(i + 1) * 64] for i in range(6)]
    cdb = hl.sb([128, 64], F32, "cdb")
    segrs = [hl.sb([128, 4, 128], F32, "segr%d" % i) for i in range(1)] * 2; LTs = [hl.sb([128, 4, 128], F32, "LT%d" % i) for i in range(1)] * 2; MT = [hl.sb([128, 4, 128], BF16, "MT%d" % i) for i in range(2)]
    CBms = [hl.sb([128, 128], F32, "CBm%d" % i) for i in range(2)]
    hT = hl.sb([128, D], F32, "hT"); hTb = hl.sb([128, D], BF16, "hTb")
    hnat = hl.sb([128, 32, 128], F32, "hnat")
    yoff = hl.sb([128, D], F32, "yoff")
    selm = hl.sb([128, 128], F32, "selm")
    g8 = hl.sb([128, 16], F32, "g8")
    yTs = hl.sb([128, 32, 128], BF16, "yTs")
    pb = [pst[i][:, :].bitcast(BF16) for i in range(2)]

    def tr_bf(src_fn, n, dst, dcol0, w=128):
        for g0 in range(0, n, 8):
            k = cnt["t"] % 2; cnt["t"] += 1
            m = min(8, n - g0)
            for j in range(m):
                ap_, rd = src_fn(g0 + j)
                op("pe", lambda e: e.transpose(pb[k][:, j * 128:(j + 1) * 128], ap_, identb[:]), reads=[rd, identb], writes=[pst[k]])
            eng = "act" if (g0 // 8) % 2 == 0 else "dve"
            o = dst[:, dcol0 + g0 * 128:dcol0 + (g0 + m) * 128]
            if eng == "act":
                op("act", lambda e: e.copy(out=o, in_=pb[k][:, 0:m * 128]), reads=[pst[k]], writes=[dst])
            else:
                op("dve", lambda e: e.tensor_copy(out=o, in_=pb[k][:, 0:m * 128]), reads=[pst[k]], writes=[dst])

    def small_mm(out_ps, lhsT, rhs, rds):
        op("pe", lambda e: e.matmul(out_ps, lhsT=lhsT, rhs=rhs, start=True, stop=True), reads=rds, writes=[pst[0]])

    def chunk(tok0, smp, need_y, own_tile):
        mo = 3 if smp else 0
        TRI, SL, ONES = MK[:, mo + 0, :], MK[:, mo + 1, :], MK[:, mo + 2, :]
        dma(xcc[:], xcT.ap().rearrange("(c p) t -> p c t", p=128)[:, :, tok0:tok0 + 128], reads=[xcT], writes=[xcc])
        for i, srcA in enumerate((dtA, adtA)):
            op("pe", lambda e: e.transpose(pst[0][:, i * 64:(i + 1) * 64], srcA[:, tok0:tok0 + 128], ident[0:64, 0:64]),
               reads=[srcA, ident], writes=[pst[0]])
        op("dve", lambda e: e.tensor_copy(out=dtk[:, 0:128], in_=pst[0][:, 0:128]), reads=[pst[0]], writes=[dtk])
        small_mm(pst[0][:, 0:64], TRI, ADT_, [MK, dtk])
        small_mm(pst[0][:, 64:128], ONES, ADT_, [MK, dtk])
        op("dve", lambda e: e.tensor_copy(out=dtk[:, 128:256], in_=pst[0][:, 0:128]), reads=[pst[0]], writes=[dtk])
        op("act", lambda e: e.activation(out=EAC_, in_=ACU_, func=AF.Exp), reads=[dtk], writes=[dtk])
        op("dve", lambda e: e.tensor_tensor(out=EDS_, in0=TOT_, in1=ACU_, op=ALU.subtract), reads=[dtk], writes=[dtk])
        op("act", lambda e: e.activation(out=EDS_, in_=EDS_, func=AF.Exp), reads=[dtk], writes=[dtk])
        op("act", lambda e: e.activation(out=cdb[:], in_=TOT_, func=AF.Exp), reads=[dtk], writes=[cdb])
        tr_bf(lambda j: (xcc[:, j, :], xcc), 32, x_tok, 0)
        tr_bf(lambda j: (xcc[:, 32 + j, :], xcc), 8, B_tok, 0)
        x3 = x_tok[:, :].rearrange("p (h q) -> p h q", q=64)
        op("dve", lambda e: e.tensor_tensor(out=xs_tok[:, :].rearrange("p (h q) -> p h q", q=64), in0=x3,
                                            in1=DT_.unsqueeze(2).to_broadcast([128, 64, 64]), op=ALU.mult), reads=[x_tok, dtk], writes=[xs_tok])
        op("pool", lambda e: e.tensor_tensor(out=xs_dec[:, :].rearrange("p (h q) -> p h q", q=64),
                                             in0=xs_tok[:, :].rearrange("p (h q) -> p h q", q=64),
                                             in1=EDS_.unsqueeze(2).to_broadcast([128, 64, 64]), op=ALU.mult), reads=[xs_tok, dtk], writes=[xs_dec])
        if need_y:
            if smp:
                op("dve", lambda e: e.memset(yoff[:], 0.0), writes=[yoff])
            for g in range(8):
                BT, CT = xcc[:, 32 + g, :], xcc[:, 40 + g, :]
                CBm = CBms[g % 2]
                op("pe", lambda e: e.matmul(pst[1][:, 0:128], lhsT=BT, rhs=CT, start=True, stop=True), reads=[xcc], writes=[pst[1]])
                op("dve", lambda e: e.tensor_tensor(out=CBm[:], in0=pst[1][:, 0:128], in1=TRI, op=ALU.mult), reads=[pst[1], MK], writes=[CBm])
                for hh in range(2):
                    h0 = g * 8 + hh * 4
                    segr, LT = segrs[hh], LTs[hh]
                    op("dve", lambda e: e.tensor_tensor(out=segr[:], in0=TRI.unsqueeze(1).to_broadcast([128, 4, 128]),
                                                        in1=dtk[:, 64 + h0:64 + h0 + 4].unsqueeze(2).to_broadcast([128, 4, 128]), op=ALU.mult),
                       reads=[MK, dtk], writes=[segr])
                    pseg = psg[0][:, hh * 512:(hh + 1) * 512]
                    op("pe", lambda e: e.matmul(pseg, lhsT=SL, rhs=segr[:, :, :].rearrange("p a b -> p (a b)"), start=True, stop=True),
                       reads=[MK, segr], writes=[PB[0][hh]])
                    op("act", lambda e: e.activation(out=LT[:, :, :].rearrange("p a b -> p (a b)"), in_=pseg, func=AF.Exp), reads=[PB[0][hh]], writes=[LT])
                    mt = MT[hh]
                    op("dve", lambda e: e.tensor_tensor(out=mt[:], in0=LT[:], in1=CBm[:, :].unsqueeze(1).to_broadcast([128, 4, 128]), op=ALU.mult),
                       reads=[LT, CBm], writes=[mt])
                    for r in range(4):
                        hd = h0 + r
                        op("pe", lambda e: e.matmul(psg[1][:, (hh * 4 + r) * 64:(hh * 4 + r + 1) * 64], lhsT=mt[:, r, :],
                                                    rhs=xs_tok[:, hd * 64:(hd + 1) * 64], start=True, stop=True), reads=[mt, xs_tok], writes=[PB[1][0]])
                if not smp:
                    op("pe", lambda e: e.matmul(psg[1][:, 512:1024], lhsT=CT, rhs=hTb[:, g * 512:(g + 1) * 512], start=True, stop=True),
                       reads=[xcc, hTb], writes=[PB[1][1]])
                    op("act", lambda e: e.copy(out=yoff[:, g * 512:(g + 1) * 512], in_=psg[1][:, 512:1024]), reads=[PB[1][1]], writes=[yoff])
                op("act", lambda e: e.copy(out=ytk[:, g * 512:(g + 1) * 512], in_=psg[1][:, 0:512]), reads=[PB[1][0]], writes=[ytk])
        if not smp:
            for g in range(8):
                sl = slice(g * 512, (g + 1) * 512)
                op("pe", lambda e: e.matmul(psg[1][:, 1024:1536], lhsT=B_tok[:, g * 128:(g + 1) * 128], rhs=xs_dec[:, sl], start=True, stop=True),
                   reads=[B_tok, xs_dec], writes=[PB[1][2]])
                h3 = hT[:, sl].rearrange("p (r q) -> p r q", q=64)
                op("dve", lambda e: e.tensor_tensor(out=h3, in0=h3, in1=cdb[:, g * 8:(g + 1) * 8].unsqueeze(2).to_broadcast([128, 8, 64]), op=ALU.mult),
                   reads=[hT, cdb], writes=[hT])
                op("dve", lambda e: e.tensor_tensor(out=hT[:, sl], in0=hT[:, sl], in1=psg[1][:, 1024:1536], op=ALU.add), reads=[hT, PB[1][2]], writes=[hT])
            op("act", lambda e: e.copy(out=hTb[:], in_=hT[:]), reads=[hT], writes=[hTb])
        else:
            for s in range(16):
                load_state_T(I.sst.ap()[s])
                for g in range(8):
                    op("pe", lambda e: e.matmul(psg[1][:, 512:1024], lhsT=xcc[:, 40 + g, :], rhs=hTb[:, g * 512:(g + 1) * 512], start=True, stop=True),
                       reads=[xcc, hTb], writes=[PB[1][1]])
                    ysl = yoff[:, g * 512:(g + 1) * 512]
                    op("dve", lambda e: e.scalar_tensor_tensor(out=ysl, in0=psg[1][:, 512:1024], scalar=rowm[:, s:s + 1], in1=ysl,
                                                               op0=ALU.mult, op1=ALU.add), reads=[PB[1][1], rowm, yoff], writes=[yoff])
                op("dve", lambda e: e.tensor_scalar(out=selm[:], in0=MK[:, 2, :], scalar1=rowm[:, 16 + s:17 + s], scalar2=None, op0=ALU.mult),
                   reads=[MK, rowm], writes=[selm])
                small_mm(pst[0][:, 0:64], selm[:], TOT_, [selm, dtk])
                op("act", lambda e: e.activation(out=cdb[:], in_=pst[0][:, 0:64], func=AF.Exp), reads=[pst[0]], writes=[cdb])
                op("pool", lambda e: e.tensor_scalar(out=Bm[:], in0=B_tok[:], scalar1=rowm[:, s:s + 1], scalar2=None, op0=ALU.mult),
                   reads=[B_tok, rowm], writes=[Bm])
                for g in range(8):
                    sl = slice(g * 512, (g + 1) * 512)
                    op("pe", lambda e: e.matmul(psg[1][:, 1024:1536], lhsT=Bm[:, g * 128:(g + 1) * 128], rhs=xs_dec[:, sl], start=True, stop=True),
                       reads=[Bm, xs_dec], writes=[PB[1][2]])
                    h3 = hT[:, sl].rearrange("p (r q) -> p r q", q=64)
                    op("dve", lambda e: e.tensor_tensor(out=h3, in0=h3, in1=cdb[:, g * 8:(g + 1) * 8].unsqueeze(2).to_broadcast([128, 8, 64]), op=ALU.mult),
                       reads=[hT, cdb], writes=[hT])
                    op("dve", lambda e: e.tensor_tensor(out=hT[:, sl], in0=hT[:, sl], in1=psg[1][:, 1024:1536], op=ALU.add), reads=[hT, PB[1][2]], writes=[hT])
                store_state(st_s.ap()[s], None)
        if need_y:
            finish_y(tok0, own_tile)

    def load_state_T(src):
        dma(hnat[:], src.rearrange("(pr h2) q n -> (h2 q) pr n", h2=2), writes=[hnat])
        for g0 in range(0, 32, 4):
            k = cnt["t"] % 2; cnt["t"] += 1
            for j in range(4):
                op("pe", lambda e: e.transpose(pst[k][:, j * 128:(j + 1) * 128], hnat[:, g0 + j, :], ident[:]), reads=[hnat, ident], writes=[pst[k]])
            op("act", lambda e: e.copy(out=hT[:, g0 * 128:(g0 + 4) * 128], in_=pst[k][:, :]), reads=[pst[k]], writes=[hT])
            op("dve", lambda e: e.tensor_copy(out=hTb[:, g0 * 128:(g0 + 4) * 128], in_=pst[k][:, :]), reads=[pst[k]], writes=[hTb])

    def store_state(dst, scale):
        for g0 in range(0, 32, 4):
            k = cnt["t"] % 2; cnt["t"] += 1
            for j in range(4):
                op("pe", lambda e: e.transpose(pst[k][:, j * 128:(j + 1) * 128], hT[:, (g0 + j) * 128:(g0 + j + 1) * 128], ident[:]),
                   reads=[hT, ident], writes=[pst[k]])
            o = hnat[:, g0:g0 + 4, :].rearrange("p a b -> p (a b)")
            if scale is None:
                op("act", lambda e: e.copy(out=o, in_=pst[k][:, :]), reads=[pst[k]], writes=[hnat])
            else:
                op("dve", lambda e: e.tensor_scalar(out=o, in0=pst[k][:, :], scalar1=scale, scalar2=None, op0=ALU.mult), reads=[pst[k], flg], writes=[hnat])
        dma(dst.rearrange("(pr h2) q n -> (h2 q) pr n", h2=2), hnat[:], reads=[hnat], writes=["st_out"])

    def finish_y(tok0, own_tile):
        y3 = ytk[:, :].rearrange("p (h q) -> p h q", q=64)
        o3 = yoff[:, :].rearrange("p (h q) -> p h q", q=64)
        op("dve", lambda e: e.tensor_tensor(out=o3, in0=o3, in1=EAC_.unsqueeze(2).to_broadcast([128, 64, 64]), op=ALU.mult), reads=[yoff, dtk], writes=[yoff])
        op("dve", lambda e: e.tensor_tensor(out=ytk[:], in0=ytk[:], in1=yoff[:], op=ALU.add), reads=[ytk, yoff], writes=[ytk])
        op("dve", lambda e: e.tensor_tensor(out=o3, in0=x_tok[:, :].rearrange("p (h q) -> p h q", q=64),
                                             in1=dbc[:, :].unsqueeze(2).to_broadcast([128, 64, 64]), op=ALU.mult), reads=[x_tok, dbc], writes=[yoff])
        op("dve", lambda e: e.tensor_tensor(out=ytk[:], in0=ytk[:], in1=yoff[:], op=ALU.add), reads=[ytk, yoff], writes=[ytk])
        dma(zc[:], projT.ap()[C_Z:C_Z + D, :].rearrange("(c p) t -> p c t", p=128)[:, :, tok0:tok0 + 128], reads=[projT], writes=[zc])
        tr_bf(lambda j: (zc[:, j, :], zc), 32, z_tok, 0)
        op("act", lambda e: e.activation(out=z_tok[:], in_=z_tok[:], func=AF.Silu), reads=[z_tok], writes=[z_tok])
        op("dve", lambda e: e.tensor_tensor(out=ytk[:], in0=ytk[:], in1=z_tok[:], op=ALU.mult), reads=[ytk, z_tok], writes=[ytk])
        op("dve", lambda e: e.memset(g8[:], 0.0), writes=[g8])
        for g in range(8):
            op("act", lambda e: e.activation(out=ybf[:, g * 512:(g + 1) * 512], in_=ytk[:, g * 512:(g + 1) * 512], func=AF.Square,
                                             accum_out=g8[:, g:g + 1]), reads=[ytk, g8], writes=[ybf, g8])
        op("dve", lambda e: e.tensor_scalar(out=g8[:, 0:8], in0=g8[:, 0:8], scalar1=1.0 / 512, scalar2=EPS, op0=ALU.mult, op1=ALU.add), reads=[g8], writes=[g8])
        op("act", lambda e: e.activation(out=g8[:, 0:8], in_=g8[:, 0:8], func=AF.Sqrt), reads=[g8], writes=[g8])
        op("dve", lambda e: e.reciprocal(out=g8[:, 8:16], in_=g8[:, 0:8]), reads=[g8], writes=[g8])
        op("dve", lambda e: e.tensor_tensor(out=ytk[:, :].rearrange("p (g q) -> p g q", q=512), in0=ytk[:, :].rearrange("p (g q) -> p g q", q=512),
                                            in1=g8[:, 8:16].unsqueeze(2).to_broadcast([128, 8, 512]), op=ALU.mult), reads=[ytk, g8], writes=[ytk])
        op("dve", lambda e: e.tensor_tensor(out=ybf[:], in0=ytk[:], in1=nbc[:], op=ALU.mult), reads=[ytk, nbc], writes=[ybf])
        tr_bf(lambda j: (ybf[:, j * 128:(j + 1) * 128], ybf), 32, yTs[:, :, :].rearrange("p a b -> p (a b)") if False else yTs_flat, 0)
        dma(yT.ap().rearrange("(c p) t -> p c t", p=128)[:, :, own_tile * 128:(own_tile + 1) * 128], yTs[:], reads=[yTs], writes=[yT])

    class _Flat:
        key = yTs.key

        def __getitem__(self, idx):
            return yTs[:, :, :].rearrange("p a b -> p (a b)")[idx]
    yTs_flat = T.__new__(T); yTs_flat.h = _Flat(); yTs_flat.key = yTs.key

    op("dve", lambda e: e.memset(hT[:], 0.0), writes=[hT])
    op("dve", lambda e: e.memset(hTb[:], 0.0), writes=[hTb])
    for ci in range(8):
        chunk(ci * 128, False, False, None)
    op("dve", lambda e: e.tensor_scalar(out=hT[:], in0=hT[:], scalar1=flg[:, 0:1], scalar2=None, op0=ALU.mult), reads=[hT, flg], writes=[hT])
    op("act", lambda e: e.copy(out=hTb[:], in_=hT[:]), reads=[hT], writes=[hTb])
    for ci in range(8):
        chunk(NPRE + ci * 128, False, True, ci)
    store_state(st_p.ap(), None)
    chunk(NPRE + 1024, True, True, 8)
    hl.release(m0)
    if stop_after <= 4.5:
        hl.finish(); return nc
    gt = hl.sb([128, NOWN], BF16, "gt"); gs = hl.sb([128, NOWN], F32, "gs"); mprev = hl.sb([128, NOWN], F32, "mprev")
    alloc_gemm(NOWN, norm=False)
    dma(B["actT"][:], yT.ap().rearrange("(c p) t -> p c t", p=128), reads=[yT], writes=[B["actT"]])
    gemm(B["actT"], 32, 0, NOWN, wview(I.ssd_out, 0), 32, gate_epi(1, False))
    hl.release(m0)
    if stop_after <= 4.9:
        hl.finish(); return nc
    SC = 512 ** -0.5
    oT_all = hl.sb([128, 16, NOWN], BF16, "oT_all")
    qc = hl.sb([128, 16, 128], BF16, "qc")
    sms = [hl.sb([128, 8], F32, "sm%d" % i) for i in range(4)]
    Pbs = [hl.sb([128, 4, 256], BF16, "Pb%d" % i) for i in range(4)]; PTs = [hl.sb([128, 2, 128], BF16, "PT%d" % i) for i in range(4)]
    o_tok = hl.sb([128, 2048], BF16, "o_tok")
    Knat = hl.sb([128, 2, 2048], F32, "Knat"); Kbf = hl.sb([128, 2, 2048], BF16, "Kbf"); kTs = hl.sb([128, 16, 256], BF16, "kTs")
    Vnat = hl.sb([128, 2, 2048], F32, "Vnat"); Vbf = hl.sb([128, 2, 2048], BF16, "Vbf")
    pb = [pst[i][:, :].bitcast(BF16) for i in range(2)]
    qv = projT.ap()[C_Q:C_Q + 2048, :].rearrange("(c p) t -> p c t", p=128)

    def attend(np_, qsl, kT_, v_, h):
        sm, Pb, PT = sms[h], Pbs[h], PTs[h]
        ps = psg[0]; psk = PB[0][h % 3]; pc0 = (h % 3) * 512
        for dc in range(4):
            op("pe", lambda e: e.matmul(ps[0:np_, pc0:pc0 + 256], lhsT=qc[:, h * 4 + dc, qsl], rhs=kT_[:, h * 4 + dc, :], start=(dc == 0), stop=(dc == 3)),
               reads=[qc, kT_], writes=[psk])
        op("dve", lambda e: e.reduce_max(out=sm[0:np_, 0:1], in_=ps[0:np_, pc0:pc0 + 256], axis=mybir.AxisListType.X), reads=[psk], writes=[sm])
        op("dve", lambda e: e.tensor_scalar(out=sm[0:np_, 1:2], in0=sm[0:np_, 0:1], scalar1=-SC, scalar2=None, op0=ALU.mult), reads=[sm], writes=[sm])
        op("dve", lambda e: e.memset(sm[0:np_, 2:3], 0.0), reads=[sm], writes=[sm])
        op("act", lambda e: e.activation(out=Pb[0:np_, h, :], in_=ps[0:np_, pc0:pc0 + 256], func=AF.Exp, scale=SC, bias=sm[0:np_, 1:2], accum_out=sm[0:np_, 2:3]),
           reads=[psk, sm], writes=[Pb, sm])
        op("dve", lambda e: e.reciprocal(out=sm[0:np_, 3:4], in_=sm[0:np_, 2:3]), reads=[sm], writes=[sm])
        k = cnt["t"] % 2; cnt["t"] += 1
        for mt in range(2):
            op("pe", lambda e: e.transpose(pb[k][:, mt * 128:mt * 128 + np_], Pb[0:np_, h, mt * 128:(mt + 1) * 128], identb[0:np_, 0:np_]),
               reads=[Pb, identb], writes=[pst[k]])
        op("act", lambda e: e.copy(out=PT[:, 0:2, 0:np_], in_=pb[k][:, 0:256].rearrange("p (a b) -> p a b", b=128)[:, :, 0:np_]), reads=[pst[k]], writes=[PT])
        po = psg[1]; pok = PB[1][h % 3]; oc0 = (h % 3) * 512
        for mt in range(2):
            op("pe", lambda e: e.matmul(po[0:np_, oc0:oc0 + 512], lhsT=PT[:, mt, 0:np_], rhs=v_[:, mt, h * 512:(h + 1) * 512], start=(mt == 0), stop=(mt == 1)),
               reads=[PT, v_], writes=[pok])
        op("dve", lambda e: e.tensor_scalar(out=o_tok[0:np_, h * 512:(h + 1) * 512], in0=po[0:np_, oc0:oc0 + 512], scalar1=sm[0:np_, 3:4], scalar2=None, op0=ALU.mult),
           reads=[pok, sm], writes=[o_tok])

    def o_to_fm(np_, col0):
        for g0 in (0, 8):
            k = cnt["t"] % 2; cnt["t"] += 1
            for j in range(8):
                c = g0 + j
                op("pe", lambda e: e.transpose(pb[k][:, j * 128:j * 128 + np_], o_tok[0:np_, c * 128:(c + 1) * 128], identb[0:np_, 0:np_]),
                   reads=[o_tok, identb], writes=[pst[k]])
            op("act", lambda e: e.copy(out=oT_all[:, g0:g0 + 8, col0:col0 + np_], in_=pb[k][:, :].rearrange("p (a b) -> p a b", b=128)[:, :, 0:np_]),
               reads=[pst[k]], writes=[oT_all])

    for ti in range(8):
        dma(qc[:], qv[:, :, NPRE + ti * 128:NPRE + (ti + 1) * 128], reads=[projT], writes=[qc])
        for h in range(4):
            attend(128, slice(0, 128), kT, vtok, h)
        o_to_fm(128, ti * 128)
    dma(qc[:], qv[:, :, NPRE + 1024:NALL], reads=[projT], writes=[qc])
    for s in range(16):
        dma(Knat[:], I.ck.ap()[s].rearrange("(a p) f -> p a f", p=128), writes=[Knat])
        dma(Vnat[:], I.cv.ap()[s].rearrange("(a p) f -> p a f", p=128), writes=[Vnat])
        op("dve", lambda e: e.tensor_copy(out=Kbf[:], in_=Knat[:]), reads=[Knat], writes=[Kbf])
        op("act", lambda e: e.copy(out=Vbf[:], in_=Vnat[:]), reads=[Vnat], writes=[Vbf])
        for mt in range(2):
            for g0 in (0, 8):
                k = cnt["t"] % 2; cnt["t"] += 1
                for j in range(8):
                    op("pe", lambda e: e.transpose(pb[k][:, j * 128:(j + 1) * 128], Kbf[:, mt, (g0 + j) * 128:(g0 + j + 1) * 128], identb[:]),
                       reads=[Kbf, identb], writes=[pst[k]])
                op("dve", lambda e: e.tensor_copy(out=kTs[:, g0:g0 + 8, mt * 128:(mt + 1) * 128], in_=pb[k][:, :].rearrange("p (a b) -> p a b", b=128)),
                   reads=[pst[k]], writes=[kTs])
        for h in range(4):
            attend(8, slice(s * 8, s * 8 + 8), kTs, Vbf, h)
        o_to_fm(8, 1024 + s * 8)
    gt = hl.sb([128, NOWN], BF16, "gt"); gs = hl.sb([128, NOWN], F32, "gs"); mprev = hl.sb([128, NOWN], F32, "mprev")
    alloc_gemm(0, norm=False)
    gemm(oT_all, 16, 0, NOWN, wview(I.xatt_out, 0), 32, gate_epi(2, False))
    hl.release(m0)
    if stop_after <= 5:
        hl.finish(); return nc

    x2 = hl.dram("x2", [NOWN, D], F32, kind="ExternalOutput" if dbg else "Internal")
    alloc_gemm(NOWN, norm=False)
    for c in range(32):
        s = B["stgf"][c % 2]
        dma(s[:], mixT[c * 128:(c + 1) * 128, :], reads=[mixT], writes=[s])
        op("act" if c % 2 else "dve", (lambda e: e.copy(out=B["actT"][:, c, :], in_=s[:])) if c % 2 else (lambda e: e.tensor_copy(out=B["actT"][:, c, :], in_=s[:])),
           reads=[s], writes=[B["actT"]])
    xres = hl.sb([128, 9, 128], F32, "xres"); xo9 = hl.sb([128, 9, 128], F32, "xo9")

    def res_epi(src_tok, dst_tok):
        def epi(b, ps, psv, nch, csz):
            s = B["stgf"][cnt["s"] % 2]; cnt["s"] += 1
            op("act", lambda e: e.copy(out=v3(s, nch, csz), in_=psv), reads=[ps], writes=[s])
            dma(xres[:], src_tok.ap().rearrange("(i p) f -> p i f", p=128)[:, :, b * 128:(b + 1) * 128], reads=[src_tok], writes=[xres])
            for g0 in (0, 4, 8):
                k = cnt["t"] % 2; cnt["t"] += 1
                m = min(4, 9 - g0)
                for j in range(m):
                    op("pe", lambda e: e.transpose(pst[k][:, j * 128:(j + 1) * 128], s[:, (g0 + j) * 128:(g0 + j + 1) * 128], ident[:]),
                       reads=[s, ident], writes=[pst[k]])
                op("dve", lambda e: e.tensor_tensor(out=xo9[:, g0:g0 + m, :], in0=pst[k][:, 0:m * 128].rearrange("p (a b) -> p a b", b=128),
                                                    in1=xres[:, g0:g0 + m, :], op=ALU.add), reads=[pst[k], xres], writes=[xo9])
            dma(dst_tok.ap().rearrange("(i p) f -> p i f", p=128)[:, :, b * 128:(b + 1) * 128], xo9[:], reads=[xo9], writes=[dst_tok])
        return epi
    gemm(B["actT"], 32, 0, NOWN, wview(I.w_o, 0), 32, res_epi(I.xa, x2))
    hl.release(m0)
    if stop_after <= 6:
        hl.finish(); return nc
    hl.release(mA)
    m0 = mA
    sS = hl.dram("sS", [NOWN, 2048], F32)
    HgT = hl.dram("HgT", [16384, NOWN], BF16)
    x3 = hl.dram("x3", [NOWN, D], F32, kind="ExternalOutput" if dbg else "Internal")
    actT = hl.sb([128, 32, NOWN], BF16, "xn2T")
    B["actT"] = actT
    m1 = hl.mark()
    alloc_gemm(0, norm=True, gemm_bufs=False)
    norm_fm(x2.ap(), 9, I.norm_ffn, actT, rd=[x2])
    hl.release(m1)
    alloc_gemm(0, norm=False)
    qT_all = hl.sb([128, 16, NOWN], BF16, "qT_all")

    def q_epi(b, ps, psv, nch, csz):
        op("act", lambda e: e.copy(out=v3(qT_all[:, b, :], nch, csz), in_=psv), reads=[ps], writes=[qT_all])
    gemm(actT, 32, 0, NOWN, wview(I.peer_wq, 0), 16, q_epi)
    skn = hl.sb([128, 16, 128], F32, "skn"); skb = hl.sb([128, 16, 128], BF16, "skb"); skT = hl.sb([128, 16, 128], BF16, "skT")
    pb = [pst[i][:, :].bitcast(BF16) for i in range(2)]
    dma(skn[:], I.peer_subkeys.ap().rearrange("b i d -> i b d"), writes=[skn])
    op("dve", lambda e: e.tensor_copy(out=skb[:], in_=skn[:]), reads=[skn], writes=[skb])
    for g0 in (0, 8):
        k = cnt["t"] % 2; cnt["t"] += 1
        for j in range(8):
            op("pe", lambda e: e.transpose(pb[k][:, j * 128:(j + 1) * 128], skb[:, g0 + j, :], identb[:]), reads=[skb, identb], writes=[pst[k]])
        op("act", lambda e: e.copy(out=skT[:, g0:g0 + 8, :], in_=pb[k][:, :].rearrange("p (a b) -> p a b", b=128)), reads=[pst[k]], writes=[skT])
    for ti in range(9):
        for g0 in range(0, 16, 4):
            k = cnt["t"] % 2; cnt["t"] += 1
            for j in range(4):
                b = g0 + j
                op("pe", lambda e: e.matmul(pst[k][:, j * 128:(j + 1) * 128], lhsT=qT_all[:, b, ti * 128:(ti + 1) * 128], rhs=skT[:, b, :], start=True, stop=True),
                   reads=[qT_all, skT], writes=[pst[k]])
            s = B["stgf"][cnt["s"] % 2]; cnt["s"] += 1
            op("act", lambda e: e.copy(out=s[:, 0:512], in_=pst[k][:, :]), reads=[pst[k]], writes=[s])
            dma(sS[ti * 128:(ti + 1) * 128, g0 * 128:(g0 + 4) * 128], s[:, 0:512], reads=[s], writes=[sS])
    hl.release(m1)
    if stop_after <= 7.1:
        hl.finish(); return nc

    s2_all = hl.sb([128, 9, 8, 128], F32, "s2_all"); r1_all = hl.sb([128, 9, 8, 128], F32, "r1_all")
    Dg = hl.sb([128, 9, 8, 128], BF16, "Dg")
    m2 = hl.mark()
    St = hl.sb([128, 16, 128], F32, "St"); scr = hl.sb([128, 256], F32, "scr")
    m16 = hl.sb([128, 16, 16], F32, "m16"); cand = hl.sb([128, 8, 256], F32, "cand"); c16 = hl.sb([128, 8, 16], F32, "c16")
    e16 = hl.sb([128, 8, 16], F32, "e16"); tk = hl.sb([128, 8, 4], F32, "tk")
    for ti in range(9):
        dma(St[:], sS[ti * 128:(ti + 1) * 128, :].rearrange("p (b i) -> p b i", i=128), reads=[sS], writes=[St])
        for b in range(16):
            op("dve", lambda e: e.max(out=m16[:, b, 0:8], in_=St[:, b, :]), reads=[St], writes=[m16])
            op("dve", lambda e: e.match_replace(out=scr[:, 0:128], in_to_replace=m16[:, b, 0:8], in_values=St[:, b, :], imm_value=-1e30),
               reads=[St, m16], writes=[scr])
            op("dve", lambda e: e.max(out=m16[:, b, 8:16], in_=scr[:, 0:128]), reads=[scr], writes=[m16])
        for h in range(8):
            op("dve", lambda e: e.tensor_tensor(out=cand[:, h, :].rearrange("p (a b) -> p a b", b=16),
                                                in0=m16[:, 2 * h, :].unsqueeze(2).to_broadcast([128, 16, 16]),
                                                in1=m16[:, 2 * h + 1, :].unsqueeze(1).to_broadcast([128, 16, 16]), op=ALU.add), reads=[m16], writes=[cand])
            op("dve", lambda e: e.max(out=c16[:, h, 0:8], in_=cand[:, h, :]), reads=[cand], writes=[c16])
            op("dve", lambda e: e.match_replace(out=scr[:], in_to_replace=c16[:, h, 0:8], in_values=cand[:, h, :], imm_value=-1e30),
               reads=[cand, c16], writes=[scr])
            op("dve", lambda e: e.max(out=c16[:, h, 8:16], in_=scr[:]), reads=[scr], writes=[c16])
        op("dve", lambda e: e.tensor_scalar(out=tk[:, :, 0:1], in0=c16[:, :, 15:16], scalar1=-1e-5, scalar2=None, op0=ALU.add), reads=[c16], writes=[tk])
        op("dve", lambda e: e.tensor_tensor(out=e16[:], in0=c16[:], in1=c16[:, :, 0:1].to_broadcast([128, 8, 16]), op=ALU.subtract), reads=[c16], writes=[e16])
        op("act", lambda e: e.activation(out=e16[:], in_=e16[:], func=AF.Exp), reads=[e16], writes=[e16])
        op("dve", lambda e: e.reduce_sum(out=tk[:, :, 1], in_=e16[:], axis=mybir.AxisListType.X), reads=[e16], writes=[tk])
        op("dve", lambda e: e.tensor_tensor(out=tk[:, :, 3:4], in0=tk[:, :, 0:1], in1=c16[:, :, 0:1], op=ALU.subtract), reads=[tk, c16], writes=[tk])
        op("act", lambda e: e.activation(out=tk[:, :, 3:4], in_=tk[:, :, 3:4], func=AF.Exp), reads=[tk], writes=[tk])
        op("dve", lambda e: e.reciprocal(out=tk[:, :, 1:2], in_=tk[:, :, 1:2]), reads=[tk], writes=[tk])
        op("dve", lambda e: e.tensor_tensor(out=tk[:, :, 2:3], in0=tk[:, :, 3:4], in1=tk[:, :, 1:2], op=ALU.mult), reads=[tk], writes=[tk])
        S4 = St[:, :, :].rearrange("p (h two) i -> p h two i", two=2)
        op("dve", lambda e: e.tensor_tensor(out=r1_all[:, ti], in0=S4[:, :, 0, :], in1=tk[:, :, 0:1].to_broadcast([128, 8, 128]), op=ALU.subtract),
           reads=[St, tk], writes=[r1_all])
        op("act", lambda e: e.copy(out=s2_all[:, ti], in_=S4[:, :, 1, :]), reads=[St], writes=[s2_all])
        for h in range(8):
            op("dve", lambda e: e.tensor_scalar(out=Dg[:, ti, h, :], in0=ident[:], scalar1=tk[:, h, 2:3], scalar2=None, op0=ALU.mult),
               reads=[ident, tk], writes=[Dg])
    hl.release(m2)
    if stop_after <= 7.2:
        hl.finish(); return nc

    Unat = [hl.sb([128, 512], F32, "Unat%d" % i) for i in range(2)]; Ubf = hl.sb([128, D], BF16, "Ubf")
    hid = hl.sb([128, NOWN], BF16, "hid"); hgs = [hl.sb([128, NOWN], BF16, "hgs%d" % i) for i in range(1)]
    Ybs = [hl.sb([128, 8, 128], F32, "Yb%d" % i) for i in range(2)]; Ebs = [hl.sb([128, 8, 128], BF16, "Eb%d" % i) for i in range(2)]; Mb = [hl.sb([128, 8, 128], BF16, "Mb%d" % i) for i in range(3)]
    NEB = 128 if stop_after > 7.3 else 2
    def uload(i, qd):
        dma(Unat[qd % 2][:], I.peer_u.ap()[i * 128:(i + 1) * 128, qd * 512:(qd + 1) * 512], writes=[Unat[qd % 2]])

    def ucast(i):
        for qd in range(8):
            op("act", lambda e: e.copy(out=Ubf[:, qd * 512:(qd + 1) * 512], in_=Unat[qd % 2][:]), reads=[Unat[qd % 2]], writes=[Ubf])
            nq = i * 8 + qd + 2
            if nq < NEB * 8:
                uload(nq // 8, nq % 8)
    UTg = [hl.sb([128, 8, 128], BF16, "UTg%d" % g) for g in range(4)]

    def tgroup(g):
        k = cnt["t"] % 2; cnt["t"] += 1
        for j in range(8):
            op("pe", lambda e: e.transpose(pb[k][:, j * 128:(j + 1) * 128], Ubf[:, (g * 8 + j) * 128:(g * 8 + j + 1) * 128], identb[:]),
               reads=[Ubf, identb], writes=[pst[k]])
        op("dve", lambda e: e.tensor_copy(out=UTg[g][:], in_=pb[k][:, :].rearrange("p (a b) -> p a b", b=128)), reads=[pst[k]], writes=[UTg[g]])
    uload(0, 0); uload(0, 1)
    ucast(0)
    for g in range(4):
        tgroup(g)
    if NEB > 1:
        ucast(1)
    for i in range(NEB):
        ps = psg[0]; pg = psg[1]
        for c in range(3):
            for ti in range(3 * c, 3 * c + 3):
                Yb, Eb, mb = Ybs[ti % 2], Ebs[ti % 2], Mb[ti % 3]
                op("dve" if ti % 3 == 0 else "pool",
                   lambda e: e.tensor_tensor(out=Yb[:], in0=s2_all[:, ti], in1=r1_all[:, ti, :, i:i + 1].to_broadcast([128, 8, 128]), op=ALU.add),
                   reads=[s2_all, r1_all], writes=[Yb])
                op("act", lambda e: e.activation(out=Eb[:], in_=Yb[:], func=AF.Exp), reads=[Yb], writes=[Eb])
                op("dve", lambda e: e.scalar_tensor_tensor(out=mb[:], in0=Yb[:], scalar=0.0, in1=Eb[:], op0=ALU.is_ge, op1=ALU.mult), reads=[Yb, Eb], writes=[mb])
            for kc in range(32):
                ug = UTg[kc // 8]
                op("pe", lambda e: e.matmul(ps[:, c * 512:c * 512 + 384], lhsT=ug[:, kc % 8, :], rhs=actT[:, kc, c * 384:(c + 1) * 384],
                                            start=(kc == 0), stop=(kc == 31)), reads=[ug, actT], writes=[ps])
                if c == 2 and i + 1 < NEB and kc % 8 == 7:
                    tgroup(kc // 8)
            for ti in range(3 * c, 3 * c + 3):
                mb = Mb[ti % 3]
                col = (ti // 3) * 512 + (ti % 3) * 128
                for h in range(8):
                    op("pe", lambda e: e.matmul(pg[:, col:col + 128], lhsT=mb[:, h, :], rhs=Dg[:, ti, h, :], start=(h == 0), stop=(h == 7)),
                       reads=[mb, Dg], writes=[pg])
        op("act", lambda e: e.activation(out=v3(hid, 3, 384), in_=ps[:, :].rearrange("p (c n) -> p c n", n=512)[:, :, 0:384], func=AF.Gelu),
           reads=[ps], writes=[hid])
        if i + 2 < NEB:
            ucast(i + 2)
        hg = hgs[0]
        op("dve", lambda e: e.tensor_tensor(out=v3(hg, 3, 384), in0=pg[:, :].rearrange("p (c n) -> p c n", n=512)[:, :, 0:384], in1=v3(hid, 3, 384), op=ALU.mult),
           reads=[pg, hid], writes=[hg])
        dma(HgT[i * 128:(i + 1) * 128, :], hg[:], reads=[hg], writes=[HgT])
    hl.release(mA)
    if stop_after <= 7.4:
        hl.finish(); return nc

    B["stgf"] = [hl.sb([128, NOWN], F32, "stgf%d" % i) for i in range(2)]
    xres = hl.sb([128, 9, 128], F32, "xres"); xo9 = hl.sb([128, 9, 128], F32, "xo9")
    vst = [hl.sb([128, 2, 256], F32, "vst%d" % i) for i in range(4)]; vbf = [hl.sb([128, 2, 256], BF16, "vbf%d" % i) for i in range(4)]
    hgl = [hl.sb([128, NOWN], BF16, "hgl%d" % i) for i in range(4)]
    NEC = 128 if stop_after > 7.3 else 2
    NRES = min(72, NEC)
    hgR = [hl.sb([128, NOWN], BF16, "hgR%d" % i) for i in range(NRES)]
    for ec in range(NRES):
        dma(hgR[ec][:], HgT[ec * 128:(ec + 1) * 128, :], reads=[HgT], writes=[hgR[ec]])
    vview = I.peer_v.ap().rearrange("(a p) d -> p a d", p=128)
    for dp in range(16):
        for e2 in range(NEC // 2):
            vs, vb = vst[e2 % 4], vbf[e2 % 4]
            dma(vs[:], vview[:, 2 * e2:2 * e2 + 2, dp * 256:(dp + 1) * 256], writes=[vs])
            if e2 % 2 == 0:
                op("dve", lambda e: e.tensor_copy(out=vb[:], in_=vs[:]), reads=[vs], writes=[vb])
            else:
                op("act", lambda e: e.copy(out=vb[:], in_=vs[:]), reads=[vs], writes=[vb])
            for sub in range(2):
                ec = 2 * e2 + sub
                if ec < NRES:
                    hgx = hgR[ec]
                else:
                    hgx = hgl[ec % 4]
                    dma(hgx[:], HgT[ec * 128:(ec + 1) * 128, :], reads=[HgT], writes=[hgx])
                for half in range(2):
                    ps = psg[half]
                    for c in range(3):
                        op("pe", lambda e: e.matmul(ps[:, c * 512:c * 512 + 384], lhsT=vb[:, sub, half * 128:(half + 1) * 128], rhs=hgx[:, c * 384:(c + 1) * 384],
                                                    start=(ec == 0), stop=(ec == NEC - 1)), reads=[vb, hgx], writes=[ps])
        for half in range(2):
            ps = psg[half]
            res_epi(x2, x3)(dp * 2 + half, ps, ps[:, :].rearrange("p (c n) -> p c n", n=512)[:, :, 0:384], 3, 384)
    hl.release(mA)
    if stop_after <= 7.5:
        hl.finish(); return nc

    alloc_gemm(0, norm=True, gemm_bufs=False)
    load_gain(I.norm_final)
    for i in range(9):
        xb = B["xt"][i % 2]
        rms_tile(x3[i * 128:(i + 1) * 128, :], xb, rd=[x3])
        dma(y[i * 128:(i + 1) * 128, :], xb[:], reads=[xb], writes=[y])

    hl.finish()
    return nc


def const_masks():
    k = np.arange(128)[:, None]; i = np.arange(128)[None, :]
    same = (k // 8) == (i // 8)
    mk = np.stack([k <= i, k > i, np.ones((128, 128), bool), (k <= i) & same, (k > i) & same, same], axis=1).astype(np.float32)
    rowm = np.zeros((128, 32), np.float32)
    for s in range(16):
        rowm[s * 8:(s + 1) * 8, s] = 1.0
        rowm[s * 8, 16 + s] = 1.0
    return {"c_masks": np.ascontiguousarray(mk), "c_rowm": rowm}


def make_in_maps(inp):
    f = lambda a: np.ascontiguousarray(np.asarray(a, dtype=np.float32))
    maps = []
    shared = {k: f(inp[k][0]) for k in ["norm_mix", "norm_mem", "norm_ffn", "w_in", "a_conv_w", "a_out", "ssd_conv_w", "ssd_conv_b",
                                        "ssd_dt_bias", "ssd_a_log", "ssd_d", "ssd_norm", "ssd_out", "w_mem_k", "w_mem_v",
                                        "xatt_out", "w_o", "peer_wq", "peer_u", "peer_v"]}
    shared["norm_final"] = f(inp["norm_final"])
    shared["peer_subkeys"] = f(inp["peer_subkeys"][0]).reshape(16, 128, 128)
    shared["c_ident"] = np.eye(128, dtype=np.float32)
    shared.update(const_masks())
    xp = np.asarray(inp["x_prompt"]); xs = np.asarray(inp["x_sample"])
    for c in range(8):
        s, half = c // 2, c % 2
        m = dict(shared)
        m["xa"] = f(np.concatenate([xp[s, half * 1024:(half + 1) * 1024], xs[16 * c:16 * c + 16].reshape(128, D)], axis=0))
        m["xpre"] = f(xp[s, 0:1024]) if half == 1 else np.zeros((NPRE, D), np.float32)
        m["flag"] = np.full((128, 1), float(half), np.float32)
        m["mem"] = f(inp["mem_prompt"][s])
        m["ck"] = f(inp["cache_mem_k"][0, 16 * c:16 * c + 16]).reshape(16, 256, 2048)
        m["cv"] = f(inp["cache_mem_v"][0, 16 * c:16 * c + 16]).reshape(16, 256, 2048)
        m["sca"] = f(inp["state_conv_a"][0, 16 * c:16 * c + 16])
        m["ssc"] = f(inp["state_ssd_conv"][0, 16 * c:16 * c + 16])
        m["sst"] = f(inp["state_ssd"][0, 16 * c:16 * c + 16])
        maps.append(m)
    return maps


def assemble(res):
    r = res
    y_p = np.stack([np.concatenate([r[2 * s]["y"][:1024], r[2 * s + 1]["y"][:1024]], axis=0) for s in range(4)])
    y_s = np.concatenate([r[c]["y"][1024:].reshape(16, 8, D) for c in range(8)], axis=0)
    mk_ = np.stack([r[2 * s]["mk"].reshape(256, 4, 512) for s in range(4)])[None]
    mv_ = np.stack([r[2 * s]["mv"].reshape(256, 4, 512) for s in range(4)])[None]
    ca_p = np.stack([r[2 * s + 1]["ca_p"] for s in range(4)])[None]
    sc_p = np.stack([r[2 * s + 1]["sc_p"] for s in range(4)])[None]
    st_p = np.stack([r[2 * s + 1]["st_p"] for s in range(4)])[None]
    ca_s = np.concatenate([r[c]["ca_s"] for c in range(8)], axis=0)[None]
    sc_s = np.concatenate([r[c]["sc_s"] for c in range(8)], axis=0)[None]
    st_s = np.concatenate([r[c]["st_s"] for c in range(8)], axis=0)[None]
    return tuple(np.ascontiguousarray(a, dtype=np.float32) for a in (y_p, y_s, mk_, mv_, ca_p, sc_p, st_p, ca_s, sc_s, st_s))


def kernel(**inputs):
    nc = build()
    maps = make_in_maps(inputs)
    maps = [{k: v for k, v in m.items() if k in nc._used_inputs} for m in maps]
    res = run_bass_kernel_spmd(nc, maps, core_ids=list(range(8)))
    return assemble(res.results)
```
